# Optimizing a Trainium2 kernel written in Bass

```python
import math
import jax, jax.numpy as jnp
from jax import lax
import numpy as np

D_MODEL = 4096
BATCH = 32
SEQ = 256
DEPTH = 1
DEC_BATCH = 4
DEC_SEQ = 4096
PAST_LEN = 512

GRID_W = 64
ATT_WIDTH = D_MODEL // 2
V_HD = 256
QK_HD = V_HD // 2
N_ATT_HEADS = ATT_WIDTH // V_HD
AXIS_DIM = QK_HD // 2
ROPE_BASE = 10000.0
Q_BLOCK = 128
D_INNER = D_MODEL - ATT_WIDTH
SSM_HD = 64
N_SSM_HEADS = D_INNER // SSM_HD
N_GROUPS = 4
HEADS_PER_GROUP = N_SSM_HEADS // N_GROUPS
D_STATE = 128
CONV_K = 5
CHUNK = 128
D_FF = 4 * D_MODEL
RMS_EPS = 1e-6

Q_COLS = N_ATT_HEADS * 2 * QK_HD
K_COLS = N_ATT_HEADS * 2 * QK_HD
V_COLS = N_ATT_HEADS * V_HD
Z_COLS = D_INNER
XBC_COLS = D_INNER + 2 * N_GROUPS * D_STATE
DT_COLS = 2 * N_SSM_HEADS
IN_COLS = Q_COLS + K_COLS + V_COLS + Z_COLS + XBC_COLS + DT_COLS

kernel_name = "hymba_diffattn_bissd_sandwich_adaln"

F32 = jnp.float32


def rmsnorm(x, g, eps=RMS_EPS):
    xf = x.astype(F32)
    y = xf * lax.rsqrt(jnp.mean(xf * xf, axis=-1, keepdims=True) + eps)
    return y.astype(x.dtype) * g


def axial_rope(x):
    L = x.shape[1]
    rows = L // GRID_W
    row = jnp.repeat(jnp.arange(rows, dtype=F32), GRID_W)
    col = jnp.tile(jnp.arange(GRID_W, dtype=F32), rows)
    inv = 1.0 / (ROPE_BASE ** (jnp.arange(0, AXIS_DIM, 2, dtype=F32) / AXIS_DIM))
    half = AXIS_DIM // 2

    def rot(xa, pos):
        ang = pos[:, None] * inv[None, :]
        cos = jnp.cos(ang)[None, :, None, None, :].astype(x.dtype)
        sin = jnp.sin(ang)[None, :, None, None, :].astype(x.dtype)
        x1, x2 = xa[..., :half], xa[..., half:]
        return jnp.concatenate([x1 * cos - x2 * sin, x1 * sin + x2 * cos], axis=-1)

    return jnp.concatenate([rot(x[..., :AXIS_DIM], row), rot(x[..., AXIS_DIM:], col)], axis=-1)


def diff_attention(q, k, v, lam):
    b, Lq = q.shape[:2]
    nb = Lq // Q_BLOCK
    qb = jnp.swapaxes(q.reshape(b, nb, Q_BLOCK, N_ATT_HEADS, 2, QK_HD), 0, 1)
    scale = QK_HD ** -0.5

    def block(qi):
        s = jnp.einsum('bqhjd,bkhjd->bhjqk', qi, k).astype(F32) * scale
        pr = jax.nn.softmax(s, axis=-1)
        a = pr[:, :, 0] - lam * pr[:, :, 1]
        return jnp.einsum('bhqk,bkhv->bqhv', a.astype(v.dtype), v)

    o = lax.map(block, qb)
    return jnp.swapaxes(o, 0, 1).reshape(b, Lq, N_ATT_HEADS, V_HD)


def centred_conv(x, w, bias):
    L = x.shape[1]
    pad = CONV_K // 2
    xp = jnp.pad(x, ((0, 0), (pad, pad), (0, 0)))
    out = bias
    for j in range(CONV_K):
        out = out + w[j] * xp[:, j:j + L]
    return out


def ssd_scan(x, dt, A, Bm, Cm, init_state):
    b, L, G, HG, P = x.shape
    N = Bm.shape[-1]
    nc = L // CHUNK
    xf = x.astype(F32).reshape(b, nc, CHUNK, G, HG, P)
    dtc = dt.reshape(b, nc, CHUNK, G, HG)
    Bc = Bm.astype(F32).reshape(b, nc, CHUNK, G, N)
    Cc = Cm.astype(F32).reshape(b, nc, CHUNK, G, N)
    cs = jnp.cumsum(dtc * A, axis=2)
    seg = cs[:, :, :, None] - cs[:, :, None, :]
    causal = jnp.tril(jnp.ones((CHUNK, CHUNK), dtype=bool))[:, :, None, None]
    Lm = jnp.where(causal, jnp.exp(jnp.where(causal, seg, 0.0)), 0.0)
    cb = jnp.einsum('bclgn,bcsgn->bclsg', Cc, Bc)
    wts = cb[..., None] * Lm * dtc[:, :, None]
    y_diag = jnp.einsum('bclsgh,bcsghp->bclghp', wts, xf)
    decay_s = jnp.exp(cs[:, :, -1:] - cs)
    states = jnp.einsum('bcsgn,bcsghp->bcghpn', Bc, xf * (decay_s * dtc)[..., None])
    chunk_decay = jnp.exp(cs[:, :, -1])

    def step(carry, inp):
        st, dec = inp
        return carry * dec[..., None, None] + st, carry

    final, prev = lax.scan(step, init_state.astype(F32),
                           (jnp.moveaxis(states, 1, 0), jnp.moveaxis(chunk_decay, 1, 0)))
    prev = jnp.moveaxis(prev, 0, 1)
    y_off = jnp.einsum('bclgn,bcghpn->bclghp', Cc, prev) * jnp.exp(cs)[..., None]
    return (y_diag + y_off).reshape(b, L, G, HG, P), final


def token_mixers(h, p, layer_idx, ctx_k, ctx_v, init_f, init_b):
    latent = ctx_k is not None
    b, L, _ = h.shape
    proj = h @ p['w_in']
    o1 = Q_COLS
    o2 = o1 + K_COLS
    o3 = o2 + V_COLS
    o4 = o3 + Z_COLS
    o5 = o4 + XBC_COLS
    q = proj[..., :o1].reshape(b, L, N_ATT_HEADS, 2, QK_HD)
    k = proj[..., o1:o2].reshape(b, L, N_ATT_HEADS, 2, QK_HD)
    v = proj[..., o2:o3].reshape(b, L, N_ATT_HEADS, V_HD)
    z = proj[..., o3:o4]
    xbc = proj[..., o4:o5]
    dt_raw = proj[..., o5:]

    if latent:
        q_use = axial_rope(q)
        k_all = jnp.concatenate([ctx_k, axial_rope(k)], axis=1)
        v_all = jnp.concatenate([ctx_v, v], axis=1)
    else:
        q_use, k_all, v_all = q, k, v
    lam_init = 0.8 - 0.6 * math.exp(-0.3 * layer_idx)
    lam = (jnp.exp(jnp.sum(p['lq1'].astype(F32) * p['lk1'].astype(F32)))
           - jnp.exp(jnp.sum(p['lq2'].astype(F32) * p['lk2'].astype(F32))) + lam_init)
    o_att = diff_attention(q_use, k_all, v_all, lam)
    o_att = (rmsnorm(o_att, p['g_subln']) * (1.0 - lam_init)).reshape(b, L, ATT_WIDTH)

    xbc = jax.nn.silu(centred_conv(xbc, p['conv_w'], p['conv_b']))
    xs = xbc[..., :D_INNER].reshape(b, L, N_GROUPS, HEADS_PER_GROUP, SSM_HD)
    Bm = xbc[..., D_INNER:D_INNER + N_GROUPS * D_STATE].reshape(b, L, N_GROUPS, D_STATE)
    Cm = xbc[..., D_INNER + N_GROUPS * D_STATE:].reshape(b, L, N_GROUPS, D_STATE)
    dt = jax.nn.softplus(dt_raw.astype(F32).reshape(b, L, 2, N_GROUPS, HEADS_PER_GROUP)
                         + p['dt_bias'].astype(F32).reshape(2, N_GROUPS, HEADS_PER_GROUP))
    A = -jnp.exp(p['a_log'].astype(F32)).reshape(2, N_GROUPS, HEADS_PER_GROUP)
    y_f, fin_f = ssd_scan(xs, dt[:, :, 0], A[0], Bm, Cm, init_f)
    y_b, fin_b = ssd_scan(jnp.flip(xs, 1), jnp.flip(dt[:, :, 1], 1), A[1],
                          jnp.flip(Bm, 1), jnp.flip(Cm, 1), init_b)
    y = (y_f + jnp.flip(y_b, 1)).astype(h.dtype) \
        + p['d_skip'].reshape(N_GROUPS, HEADS_PER_GROUP)[..., None] * xs
    y = y.reshape(b, L, D_INNER) * jax.nn.silu(z)
    y = rmsnorm(y.reshape(b, L, N_GROUPS, D_INNER // N_GROUPS),
                p['g_ssm_norm'].reshape(N_GROUPS, D_INNER // N_GROUPS)).reshape(b, L, D_INNER)

    out = jnp.concatenate([o_att, y], axis=-1) @ p['w_out']
    return out, (k, v, fin_f, fin_b)


def trunk_layer(x, c_vec, p, layer_idx, ctx_k, ctx_v, init_f, init_b):
    mod = (jax.nn.silu(c_vec) @ p['w_ada'] + p['b_ada'])[:, None, :]
    sh_a, sc_a, g_a, sh_m, sc_m, g_m = jnp.split(mod, 6, axis=-1)
    h = rmsnorm(x, p['g_mix_pre']) * (1.0 + sc_a) + sh_a
    mix, ctx_t = token_mixers(h, p, layer_idx, ctx_k, ctx_v, init_f, init_b)
    x = x + g_a * rmsnorm(mix, p['g_mix_post'])
    h = rmsnorm(x, p['g_mlp_pre']) * (1.0 + sc_m) + sh_m
    m = jnp.square(jax.nn.relu(h @ p['w_up'])) @ p['w_down']
    x = x + g_m * rmsnorm(m, p['g_mlp_post'])
    return x, ctx_t


def setup_inputs(seed: int = 0) -> dict:
    key = jax.random.key(seed)
    ks = jax.random.split(key, 32)
    nrm = lambda k, shape, s=1.0: jax.random.normal(k, shape, F32) * s
    dt0 = jnp.exp(jax.random.uniform(ks[20], (DEPTH, 2, N_SSM_HEADS), F32)
                  * (math.log(0.1) - math.log(0.001)) + math.log(0.001))
    return {
        'x_prompt': nrm(ks[0], (BATCH, SEQ, D_MODEL)),
        'x_sample': nrm(ks[1], (DEC_BATCH, DEC_SEQ, D_MODEL)),
        'c': nrm(ks[2], (DEC_BATCH, D_MODEL)),
        'cache_k': nrm(ks[3], (DEC_BATCH, DEPTH, PAST_LEN, N_ATT_HEADS, 2, QK_HD)),
        'cache_v': nrm(ks[4], (DEC_BATCH, DEPTH, PAST_LEN, N_ATT_HEADS, V_HD)),
        'state_ssm_fwd': nrm(ks[5], (DEC_BATCH, DEPTH, N_SSM_HEADS, SSM_HD, D_STATE), 0.5),
        'state_ssm_bwd': nrm(ks[6], (DEC_BATCH, DEPTH, N_SSM_HEADS, SSM_HD, D_STATE), 0.5),
        'c_ctx': nrm(ks[7], (D_MODEL,)),
        'w_ada': nrm(ks[8], (DEPTH, D_MODEL, 6 * D_MODEL), 0.5 * D_MODEL ** -0.5),
        'b_ada': nrm(ks[9], (DEPTH, 6 * D_MODEL), 0.02),
        'g_mix_pre': 1.0 + nrm(ks[10], (DEPTH, D_MODEL), 0.02),
        'g_mix_post': 1.0 + nrm(ks[11], (DEPTH, D_MODEL), 0.02),
        'g_mlp_pre': 1.0 + nrm(ks[12], (DEPTH, D_MODEL), 0.02),
        'g_mlp_post': 1.0 + nrm(ks[13], (DEPTH, D_MODEL), 0.02),
        'w_in': nrm(ks[14], (DEPTH, D_MODEL, IN_COLS), D_MODEL ** -0.5),
        'lambda_q1': nrm(ks[15], (DEPTH, QK_HD), 0.1),
        'lambda_k1': nrm(ks[16], (DEPTH, QK_HD), 0.1),
        'lambda_q2': nrm(ks[17], (DEPTH, QK_HD), 0.1),
        'lambda_k2': nrm(ks[18], (DEPTH, QK_HD), 0.1),
        'g_subln': 1.0 + nrm(ks[19], (DEPTH, V_HD), 0.02),
        'conv_w': nrm(ks[21], (DEPTH, CONV_K, XBC_COLS), CONV_K ** -0.5),
        'conv_b': nrm(ks[22], (DEPTH, XBC_COLS), 0.02),
        'a_log': jnp.log(jax.random.uniform(ks[23], (DEPTH, 2, N_SSM_HEADS), F32, 1.0, 16.0)),
        'dt_bias': dt0 + jnp.log(-jnp.expm1(-dt0)),
        'd_skip': 1.0 + nrm(ks[24], (DEPTH, N_SSM_HEADS), 0.1),
        'g_ssm_norm': 1.0 + nrm(ks[25], (DEPTH, D_INNER), 0.02),
        'w_out': nrm(ks[26], (DEPTH, D_MODEL, D_MODEL), D_MODEL ** -0.5),
        'w_up': nrm(ks[27], (DEPTH, D_MODEL, D_FF), D_MODEL ** -0.5),
        'w_down': nrm(ks[28], (DEPTH, D_FF, D_MODEL), D_FF ** -0.5),
    }


def reference(x_prompt, x_sample, c, cache_k, cache_v, state_ssm_fwd, state_ssm_bwd, c_ctx,
              w_ada, b_ada, g_mix_pre, g_mix_post, g_mlp_pre, g_mlp_post, w_in,
              lambda_q1, lambda_k1, lambda_q2, lambda_k2, g_subln, conv_w, conv_b,
              a_log, dt_bias, d_skip, g_ssm_norm, w_out, w_up, w_down):
    xp = x_prompt
    xs = x_sample
    bp = xp.shape[0]
    bd = xs.shape[0]
    sshape = (N_GROUPS, HEADS_PER_GROUP, SSM_HD, D_STATE)
    new_k, new_v, new_sf, new_sb = [], [], [], []
    for l in range(DEPTH):
        p = dict(w_ada=w_ada[l], b_ada=b_ada[l], g_mix_pre=g_mix_pre[l], g_mix_post=g_mix_post[l],
                 g_mlp_pre=g_mlp_pre[l], g_mlp_post=g_mlp_post[l], w_in=w_in[l],
                 lq1=lambda_q1[l], lk1=lambda_k1[l], lq2=lambda_q2[l], lk2=lambda_k2[l],
                 g_subln=g_subln[l], conv_w=conv_w[l], conv_b=conv_b[l], a_log=a_log[l],
                 dt_bias=dt_bias[l], d_skip=d_skip[l], g_ssm_norm=g_ssm_norm[l],
                 w_out=w_out[l], w_up=w_up[l], w_down=w_down[l])
        zero = jnp.zeros((bp,) + sshape, F32)
        xp, (k_c, v_c, sf, sb) = trunk_layer(xp, c_ctx[None, :], p, l, None, None, zero, zero)
        new_k.append(k_c)
        new_v.append(v_c)
        new_sf.append(sf.reshape(bp, N_SSM_HEADS, SSM_HD, D_STATE).astype(xp.dtype))
        new_sb.append(sb.reshape(bp, N_SSM_HEADS, SSM_HD, D_STATE).astype(xp.dtype))
        xs, _ = trunk_layer(xs, c, p, l, cache_k[:, l], cache_v[:, l],
                            state_ssm_fwd[:, l].reshape((bd,) + sshape),
                            state_ssm_bwd[:, l].reshape((bd,) + sshape))
    return (xp, xs, jnp.stack(new_k, axis=1), jnp.stack(new_v, axis=1),
            jnp.stack(new_sf, axis=1), jnp.stack(new_sb, axis=1))
```

```python
import math
from contextlib import ExitStack
import numpy as np
import concourse.bass as bass
import concourse.mybir as mybir
from concourse.bass_utils import run_bass_kernel_spmd

F32 = mybir.dt.float32
BF16 = mybir.dt.bfloat16
AF = mybir.ActivationFunctionType
ALU = mybir.AluOpType
AX = mybir.AxisListType

D = 4096
NKC = 32
IN_COLS = 11328
NP_TOK = 1024
NS_OWN = 2048
NS_ALL = 4096
PAST = 512
NT = NP_TOK + NS_ALL
NOWN = NP_TOK + NS_OWN
DFF = 16384
EPS = 1e-6
LAM_INIT = 0.8 - 0.6 * math.exp(-0.3 * 0)


class Buf:
    __slots__ = ("name", "acc", "w", "r", "dsem", "dkey", "dcnt")

    def __init__(self, name, acc=False):
        self.name = name
        self.acc = acc
        self.w = {}
        self.r = {}
        self.dsem = None
        self.dkey = None
        self.dcnt = 0


class Sched:
    def __init__(self, nc, stack):
        self.nc = nc
        self.stack = stack
        self.engs = dict(pe=nc.tensor, act=nc.scalar, dve=nc.vector, pool=nc.gpsimd, sp=nc.sync)
        self.esem = {}
        self.ecnt = {}
        self.waited = {}
        for k in self.engs:
            self.esem[k] = stack.enter_context(nc.semaphore("es_" + k))
            self.ecnt[k] = 0
            self.waited[k] = {}
        self.dsems = []
        self.free_sems = []
        self.nsem = 0
        self.nops = 0

    def _deps(self, ek, reads, writes, is_dma):
        deps = {}
        own = None if is_dma else "es_" + ek
        for b in reads:
            for k, sv in b.w.items():
                if k not in deps or deps[k][1] < sv[1]:
                    deps[k] = sv
        for b in writes:
            if b.acc:
                continue
            for dd in (b.w, b.r):
                for k, sv in dd.items():
                    if k == own:
                        continue
                    if k not in deps or deps[k][1] < sv[1]:
                        deps[k] = sv
        return deps

    def _wait(self, ek, deps):
        e = self.engs[ek]
        wd = self.waited[ek]
        for k, (sem, val) in deps.items():
            if wd.get(k, 0) < val:
                e.wait_ge(sem, val)
                wd[k] = val

    def _record(self, k, sv, reads, writes):
        for b in writes:
            if b.acc:
                if k not in b.w or b.w[k][1] < sv[1]:
                    b.w[k] = sv
            else:
                b.w = {k: sv}
                b.r = {}
        for b in reads:
            if k not in b.r or b.r[k][1] < sv[1]:
                b.r[k] = sv

    def op(self, ek, fn, reads=(), writes=()):
        deps = self._deps(ek, reads, writes, False)
        self._wait(ek, deps)
        ins = fn(self.engs[ek])
        self.ecnt[ek] += 1
        ins.then_inc(self.esem[ek], 1)
        self._record("es_" + ek, (self.esem[ek], self.ecnt[ek]), reads, writes)
        self.nops += 1

    def dma(self, qk, out, in_, sb, reads=(), writes=()):
        deps = self._deps(qk, reads, writes, True)
        if sb.dsem is None:
            if self.free_sems:
                sb.dkey, sb.dsem, sb.dcnt = self.free_sems.pop()
            else:
                sb.dkey = "ds%d" % self.nsem
                self.nsem += 1
                sb.dsem = self.stack.enter_context(self.nc.semaphore(sb.dkey))
                sb.dcnt = 0
            self.dsems.append(sb)
        if sb.dcnt > 0:
            k = sb.dkey
            if k not in deps or deps[k][1] < sb.dcnt:
                deps[k] = (sb.dsem, sb.dcnt)
        self._wait(qk, deps)
        self.engs[qk].dma_start(out=out, in_=in_).then_inc(sb.dsem, 16)
        sb.dcnt += 16
        self._record(sb.dkey, (sb.dsem, sb.dcnt), reads, writes)
        self.nops += 1

    def barrier(self):
        deps = {}
        for k in self.engs:
            if self.ecnt[k] > 0:
                deps["es_" + k] = (self.esem[k], self.ecnt[k])
        for sb in self.dsems:
            if sb.dcnt > 0:
                deps[sb.dkey] = (sb.dsem, sb.dcnt)
        for k in self.engs:
            d2 = {kk: v for kk, v in deps.items() if kk != "es_" + k}
            self._wait(k, d2)
        for sb in self.dsems:
            self.free_sems.append((sb.dkey, sb.dsem, sb.dcnt))
            sb.dsem = None
        self.dsems = []

    def finish(self):
        deps = {}
        for k in self.engs:
            if self.ecnt[k] > 0 and k != "sp":
                deps["es_" + k] = (self.esem[k], self.ecnt[k])
        for sb in self.dsems:
            if sb.dcnt > 0:
                deps[sb.dkey] = (sb.dsem, sb.dcnt)
        self._wait("sp", deps)


class Ctx:
    pass


def build(debug=None, cfg=None):
    cfg = cfg or {}
    nc = bass.Bass("TRN2", target_bir_lowering=False)
    g = Ctx()
    g.nc = nc
    g.cfg = cfg
    g.debug = debug or ()

    def din(name, shape, dt=F32):
        return nc.dram_tensor(name, list(shape), dt, kind="ExternalInput").ap()

    def dout(name, shape, dt=F32):
        return nc.dram_tensor(name, list(shape), dt, kind="ExternalOutput").ap()

    def dscr(name, shape, dt=F32):
        kind = "ExternalOutput" if name in g.debug else "Internal"
        return nc.dram_tensor(name, list(shape), dt, kind=kind).ap()

    IN_SHAPES = dict(xp=[NP_TOK, D], xs=[NS_ALL, D], cT=[128, NKC * 2], ck=[PAST, 2048], cv=[PAST, 2048],
                     sf=[32 * 64, 128], sb=[32 * 64, 128], w_ada=[D, 6 * D], b_ada=[1, 6 * D], gcols=[128, 4 * NKC],
                     grows=[4, D], w_in=[D, IN_COLS], lam=[1, 4 * 128], g_subln=[1, 256], conv_w=[5, 3072],
                     conv_b=[1, 3072], a_log=[1, 64], dt_bias=[1, 64], d_skip=[1, 32], g_ssm=[1, 2048],
                     w_out=[D, D], w_up=[D, DFF], w_down=[DFF, D], ropec=[NS_ALL, 128], ropes=[NS_ALL, 128],
                     ident=[128, 128])

    class LazyIn:
        def __getattr__(self, name):
            ap = din(name, IN_SHAPES[name])
            object.__setattr__(self, name, ap)
            g.used_inputs.append(name)
            return ap
    g.used_inputs = []
    I = LazyIn()
    g.I = I
    if not cfg.get("lazy_inputs", False):
        for n_ in IN_SHAPES:
            getattr(I, n_)

    O = Ctx()
    g.O = O
    O.yp = dout("o_yp", [NP_TOK, D])
    O.ys = dout("o_ys", [NS_OWN, D])
    O.nk = dout("o_nk", [NP_TOK, 2048])
    O.nv = dout("o_nv", [NP_TOK, 2048])
    O.nsf = dout("o_nsf", [4 * 32 * 64, 128])
    O.nsb = dout("o_nsb", [4 * 32 * 64, 128])

    S = Ctx()
    g.S = S
    S.modraw = dscr("modraw", [2, 6 * D])
    S.QT = dscr("QT", [16, 128, NOWN], BF16)
    S.KT = dscr("KT", [16, 128, NT + PAST], BF16)
    S.V = dscr("V", [NT + PAST, 2048], BF16)
    S.Z = dscr("Zs", [NOWN, 2048 + g.cfg.get("zpad", 0)])
    S.XBC = dscr("XBC", [NT, 4 * 768])
    S.DT = dscr("DT", [NT, 64])
    S.MIXT = dscr("MIXT", [NKC, 128, NOWN], BF16)
    S.X1 = dscr("X1", [NOWN, D])

    with ExitStack() as stack:
        sc = Sched(nc, stack)
        g.sc = sc
        g.stack = stack
        g.ps = []
        g.psb = []
        for i in range(8):
            t = stack.enter_context(nc.psum_tensor("ps%d" % i, [128, 512], F32))
            g.ps.append(t)
            g.psb.append(Buf("ps%d" % i))
        g.ident_f = stack.enter_context(nc.sbuf_tensor("ident_f", [128, 128], F32))
        g.ident_b = stack.enter_context(nc.sbuf_tensor("ident_b", [128, 128], BF16))
        g.b_ident = Buf("ident")
        sc.dma("sp", g.ident_f[:], I.ident[:, :], g.b_ident, writes=[g.b_ident])
        sc.op("dve", lambda e: e.tensor_copy(g.ident_b[:], g.ident_f[:]), reads=[g.b_ident], writes=[g.b_ident])
        g.dbuf = {n: Buf("d_" + n, acc=True) for n in
                  ("modraw", "QT", "KT", "V", "Z", "XBC", "DT", "MIXT", "X1", "out")}

        stages = cfg.get("stages", "0123456")
        if "0" in stages:
            stage_adaln(g)
            sc.barrier()
        if "1" in stages:
            stage_inproj(g)
            sc.barrier()
        if "2" in stages:
            stage_ctxkv(g)
            sc.barrier()
        if "3" in stages:
            stage_ssd(g)
            sc.barrier()
        if "4" in stages:
            stage_attn(g)
            sc.barrier()
        if "5" in stages:
            stage_mlp(g)
        sc.finish()
    nc.used_inputs = g.used_inputs
    return nc


def stage_adaln(g):
    nc, sc, I, S = g.nc, g.sc, g.I, g.S
    ncol = g.cfg.get("ada_tiles", 48)
    with ExitStack() as st:
        cT = st.enter_context(nc.sbuf_tensor("a_cT", [128, NKC * 2], F32))
        sg = st.enter_context(nc.sbuf_tensor("a_sg", [128, NKC * 2], F32))
        L = st.enter_context(nc.sbuf_tensor("a_L", [128, NKC * 2, 128], BF16))
        brow = st.enter_context(nc.sbuf_tensor("a_brow", [1, 6 * D], F32))
        W = [st.enter_context(nc.sbuf_tensor("a_W%d" % i, [128, NKC, 512], BF16)) for i in range(2)]
        ot = [st.enter_context(nc.sbuf_tensor("a_ot%d" % i, [1, 512], F32)) for i in range(4)]
        b_c, b_L, b_brow = Buf("a_c"), Buf("a_L"), Buf("a_brow")
        b_W = [Buf("a_W0"), Buf("a_W1")]
        b_ot = [Buf("a_ot%d" % i) for i in range(4)]
        sc.dma("sp", cT[:], I.cT[:, :], b_c, writes=[b_c])
        sc.dma("sp", brow[:], I.b_ada[:, :], b_brow, writes=[b_brow])
        sc.op("act", lambda e: e.activation(out=sg[:], in_=cT[:], func=AF.Sigmoid), reads=[b_c], writes=[b_L])
        sc.op("dve", lambda e: e.tensor_tensor(out=sg[:], in0=sg[:], in1=cT[:], op=ALU.mult), reads=[b_c, b_L], writes=[b_L])
        sc.op("dve", lambda e: e.tensor_copy(out=L[:], in_=sg[:].unsqueeze(2).to_broadcast([128, NKC * 2, 128])),
              reads=[b_L], writes=[b_L])
        wv = I.w_ada.rearrange("(kc p) n -> p kc n", p=128)
        k = 0
        for n in range(ncol):
            wi = n % 2
            sc.dma("pool", W[wi][:], wv[:, :, n * 512:(n + 1) * 512], b_W[wi], writes=[b_W[wi]])
            for m in range(2):
                pb = 4 + (k % 2)
                oi = k % 4
                k += 1

                def mm(e, m=m, wi=wi, pb=pb):
                    ins = None
                    for kc in range(NKC):
                        ins = e.matmul(g.ps[pb][:, :], lhsT=L[:, kc * 2 + m, :], rhs=W[wi][:, kc, :],
                                       start=(kc == 0), stop=(kc == NKC - 1))
                    return ins
                sc.op("pe", mm, reads=[b_L, b_W[wi]], writes=[g.psb[pb]])
                sc.op("dve", lambda e, pb=pb, oi=oi, n=n: e.tensor_tensor(
                    out=ot[oi][:], in0=g.ps[pb][0:1, :], in1=brow[0:1, n * 512:(n + 1) * 512], op=ALU.add),
                    reads=[g.psb[pb], b_brow], writes=[b_ot[oi]])
                sc.dma("sp", S.modraw[m:m + 1, n * 512:(n + 1) * 512], ot[oi][:], b_ot[oi],
                       reads=[b_ot[oi]], writes=[g.dbuf["modraw"]])


def load_modcols(g, st, mode, which_g, i_sc, i_sh, name):
    nc, sc, I, S = g.nc, g.sc, g.I, g.S
    t = st.enter_context(nc.sbuf_tensor(name, [128, 3, NKC], F32))
    rr = st.enter_context(nc.sbuf_tensor(name + "_r", [32, 2, 128], F32))
    b = Buf(name)
    b_rr = Buf(name + "_r")
    sc.dma("sp", rr[:, 0, :], S.modraw[mode, i_sc * D:(i_sc + 1) * D].rearrange("(kc p) -> kc p", p=128), b_rr,
           reads=[g.dbuf["modraw"]], writes=[b_rr])
    sc.dma("sp", rr[:, 1, :], S.modraw[mode, i_sh * D:(i_sh + 1) * D].rearrange("(kc p) -> kc p", p=128), b_rr,
           reads=[g.dbuf["modraw"]], writes=[b_rr])

    def tr(e):
        e.transpose(g.ps[7][:, 0:32], rr[:, 0, :], g.ident_f[0:32, 0:32])
        return e.transpose(g.ps[7][:, 32:64], rr[:, 1, :], g.ident_f[0:32, 0:32])
    sc.op("pe", tr, reads=[b_rr, g.b_ident], writes=[g.psb[7]])
    sc.op("dve", lambda e: e.tensor_copy(out=t[:, 0:2, :].rearrange("p a k -> p (a k)"), in_=g.ps[7][:, 0:64]),
          reads=[g.psb[7]], writes=[b])
    sc.dma("sp", t[:, 2, :], I.gcols[:, which_g * NKC:(which_g + 1) * NKC], b, writes=[b])
    sc.op("dve", lambda e: e.scalar_tensor_tensor(out=t[:, 0, :], in0=t[:, 0, :], scalar=1.0, in1=t[:, 2, :],
                                                  op0=ALU.add, op1=ALU.mult), reads=[b], writes=[b])
    return t, b


def norm_transpose(g, xt, b_x, modc, b_modc, hT, b_hT, tt, scratch, b_scr, junk, b_junk):
    sc = g.sc
    ss, rstd = scratch
    sc.op("act", lambda e: e.activation(out=junk[:], in_=xt[:], func=AF.Square, accum_out=ss[:]),
          reads=[b_x], writes=[b_scr, b_junk])
    sc.op("act", lambda e: e.activation(out=rstd[:], in_=ss[:], func=AF.Sqrt, scale=1.0 / D, bias=EPS),
          reads=[b_scr], writes=[b_scr])
    sc.op("dve", lambda e: e.reciprocal(out=rstd[:], in_=rstd[:]), reads=[b_scr], writes=[b_scr])
    sc.op("dve", lambda e: e.tensor_scalar(out=xt[:], in0=xt[:], scalar1=rstd[:, 0:1], scalar2=None, op0=ALU.mult),
          reads=[b_x, b_scr], writes=[b_x])
    for q in range(8):
        pb = 4 + (q % 2)

        def tr(e, q=q, pb=pb):
            ins = None
            for i in range(4):
                kc = q * 4 + i
                ins = e.transpose(g.ps[pb][:, i * 128:(i + 1) * 128], xt[:, kc * 128:(kc + 1) * 128], g.ident_f[:])
            return ins
        sc.op("pe", tr, reads=[b_x, g.b_ident], writes=[g.psb[pb]])
        for i in range(4):
            kc = q * 4 + i
            ek = "act" if (i % 2 == 0) else "dve"
            if ek == "act":
                sc.op("act", lambda e, kc=kc, i=i, pb=pb: e.activation(
                    out=hT[:, kc, tt * 128:(tt + 1) * 128], in_=g.ps[pb][:, i * 128:(i + 1) * 128],
                    func=AF.Identity, scale=modc[:, 0, kc:kc + 1], bias=modc[:, 1, kc:kc + 1]),
                    reads=[g.psb[pb], b_modc], writes=[b_hT])
            else:
                sc.op("dve", lambda e, kc=kc, i=i, pb=pb: e.tensor_scalar(
                    out=hT[:, kc, tt * 128:(tt + 1) * 128], in0=g.ps[pb][:, i * 128:(i + 1) * 128],
                    scalar1=modc[:, 0, kc:kc + 1], scalar2=modc[:, 1, kc:kc + 1], op0=ALU.mult, op1=ALU.add),
                    reads=[g.psb[pb], b_modc], writes=[b_hT])


def tok_src(g, t0):
    if t0 < NP_TOK:
        return g.I.xp[t0:t0 + 128, :]
    return g.I.xs[t0 - NP_TOK:t0 - NP_TOK + 128, :]


def stage_inproj(g):
    nc, sc, I, S, O = g.nc, g.sc, g.I, g.S, g.O
    ngroups = g.cfg.get("in_groups", list(range(NT // 512)))
    with ExitStack() as st:
        xt = st.enter_context(nc.sbuf_tensor("i_x", [128, D], F32))
        junk = st.enter_context(nc.sbuf_tensor("i_junk", [128, D], BF16))
        hT = st.enter_context(nc.sbuf_tensor("i_hT", [128, NKC, 512], BF16))
        W = [st.enter_context(nc.sbuf_tensor("i_W%d" % i, [128, NKC, 512], BF16)) for i in range(2)]
        ss = st.enter_context(nc.sbuf_tensor("i_ss", [128, 1], F32))
        rstd = st.enter_context(nc.sbuf_tensor("i_rstd", [128, 1], F32))
        NF = 6
        sf32 = [st.enter_context(nc.sbuf_tensor("i_sf%d" % i, [128, 512], F32)) for i in range(NF)]
        sb16 = [st.enter_context(nc.sbuf_tensor("i_sb%d" % i, [128, 512], BF16)) for i in range(NF)]
        rt1 = st.enter_context(nc.sbuf_tensor("i_rt1", [128, 512], F32))
        rt2 = st.enter_context(nc.sbuf_tensor("i_rt2", [128, 512], F32))
        qts = [st.enter_context(nc.sbuf_tensor("i_qts%d" % i, [128, 4, 512], BF16)) for i in range(2)]
        rc = st.enter_context(nc.sbuf_tensor("i_rc", [128, 4, 128], F32))
        rs = st.enter_context(nc.sbuf_tensor("i_rs", [128, 4, 128], F32))
        b_x, b_junk, b_hT, b_scr = Buf("i_x"), Buf("i_junk"), Buf("i_hT"), Buf("i_scr")
        b_W = [Buf("i_W0"), Buf("i_W1")]
        b_sf = [Buf("i_sf%d" % i) for i in range(NF)]
        b_sb = [Buf("i_sb%d" % i) for i in range(NF)]
        b_rt, b_rope = Buf("i_rt"), Buf("i_rope")
        b_qts = [Buf("i_qts0"), Buf("i_qts1")]
        modc = {}
        modc[0] = load_modcols(g, st, 0, 0, 1, 0, "i_modp")
        modc[1] = load_modcols(g, st, 1, 0, 1, 0, "i_mods")
        wv = I.w_in.rearrange("(kc p) n -> p kc n", p=128)
        cnt = dict(w=0, ps=0, sf=0, sb=0, q=0)
        for grp in ngroups:
            t0g = grp * 512
            prompt = t0g < NP_TOK
            remote = t0g >= NOWN
            mode = 0 if prompt else 1
            for tt in range(4):
                sc.dma("sp", xt[:], tok_src(g, t0g + tt * 128), b_x, writes=[b_x])
                norm_transpose(g, xt, b_x, modc[mode][0], modc[mode][1], hT, b_hT, tt, (ss, rstd), b_scr, junk, b_junk)
            if not prompt:
                ls = t0g - NP_TOK
                sc.dma("sp", rc[:], I.ropec[ls:ls + 512, :].rearrange("(t p) d -> p t d", p=128), b_rope, writes=[b_rope])
                sc.dma("sp", rs[:], I.ropes[ls:ls + 512, :].rearrange("(t p) d -> p t d", p=128), b_rope, writes=[b_rope])
            tiles = []
            if not remote:
                tiles += [("q", c * 512, 512, c) for c in range(4)]
            tiles += [("k", 2048 + c * 512, 512, c) for c in range(4)]
            tiles += [("v", 4096 + c * 512, 512, c) for c in range(4)]
            if not remote:
                tiles += [("z", 6144 + c * 512, 512, c) for c in range(4)]
            tiles += [("x", 8192 + c * 512, 512, c) for c in range(4)]
            tiles += [("B", 8192 + 2048, 512, 0)]
            tiles += [("C", 8192 + 2560, 512, 0)]
            tiles += [("dt", 11264, 64, 0)]
            kinds_ok = g.cfg.get('in_kinds')
            for (kind, c0, ncols, ci) in tiles:
                if kinds_ok is not None and kind not in kinds_ok:
                    continue
                wi = cnt["w"] % 2
                cnt["w"] += 1
                sc.dma("pool", W[wi][:, :, 0:ncols], wv[:, :, c0:c0 + ncols], b_W[wi], writes=[b_W[wi]])
                qi = None
                if kind in ("q", "k"):
                    qi = cnt["q"] % 2
                    cnt["q"] += 1
                for tt in range(4):
                    t0 = t0g + tt * 128
                    pb = cnt["ps"] % 4
                    cnt["ps"] += 1

                    def mm(e, tt=tt, wi=wi, pb=pb, ncols=ncols):
                        ins = None
                        for kc in range(NKC):
                            ins = e.matmul(g.ps[pb][:, 0:ncols], lhsT=hT[:, kc, tt * 128:(tt + 1) * 128],
                                           rhs=W[wi][:, kc, 0:ncols], start=(kc == 0), stop=(kc == NKC - 1))
                        return ins
                    sc.op("pe", mm, reads=[b_hT, b_W[wi]], writes=[g.psb[pb]])
                    P = g.ps[pb]
                    bP = g.psb[pb]
                    if kind in ("q", "k"):
                        bi = cnt["sb"] % NF
                        cnt["sb"] += 1
                        xb = sb16[bi]
                        if prompt:
                            if kind == "k":
                                fi = cnt["sf"] % NF
                                cnt["sf"] += 1
                                sc.op("act", lambda e, fi=fi, P=P: e.copy(out=sf32[fi][:], in_=P[:, :]),
                                      reads=[bP], writes=[b_sf[fi]])
                                sc.dma("sp", O.nk[t0:t0 + 128, ci * 512:(ci + 1) * 512], sf32[fi][:], b_sf[fi],
                                       reads=[b_sf[fi]], writes=[g.dbuf["out"]])
                                sc.op("dve", lambda e, xb=xb, fi=fi: e.tensor_copy(out=xb[:], in_=sf32[fi][:]),
                                      reads=[b_sf[fi]], writes=[b_sb[bi]])
                            else:
                                sc.op("dve", lambda e, xb=xb, P=P: e.tensor_copy(out=xb[:], in_=P[:, :]),
                                      reads=[bP], writes=[b_sb[bi]])
                        else:
                            Cb = rc[:, tt, :].unsqueeze(1).to_broadcast([128, 4, 128])
                            sc.op("dve", lambda e, P=P, Cb=Cb: e.tensor_tensor(
                                out=rt1[:].rearrange("p (b d) -> p b d", b=4), in0=P[:, :].rearrange("p (b d) -> p b d", b=4),
                                in1=Cb, op=ALU.mult), reads=[bP, b_rope], writes=[b_rt])
                            Pv = P[:, :].rearrange("p (b a h d) -> p b a h d", b=4, a=2, h=2)
                            t2v = rt2[:].rearrange("p (b a h d) -> p b a h d", b=4, a=2, h=2)
                            Sv = rs[:, tt, :].rearrange("p (a h d) -> p a h d", a=2, h=2)
                            for hh in range(2):
                                sc.op("dve", lambda e, hh=hh, Pv=Pv, t2v=t2v, Sv=Sv: e.tensor_tensor(
                                    out=t2v[:, :, :, hh, :], in0=Pv[:, :, :, 1 - hh, :],
                                    in1=Sv[:, :, hh, :].unsqueeze(1).to_broadcast([128, 4, 2, 32]), op=ALU.mult),
                                    reads=[bP, b_rope, b_rt], writes=[b_rt])
                            sc.op("pool", lambda e, xb=xb: e.tensor_tensor(out=xb[:], in0=rt1[:], in1=rt2[:], op=ALU.add),
                                  reads=[b_rt], writes=[b_sb[bi]])
                        tb = 6 + (cnt["ps"] % 2)
                        TP = g.ps[tb][:, :].bitcast(BF16)

                        def tr(e, xb=xb, TP=TP):
                            ins = None
                            for bk in range(4):
                                ins = e.transpose(TP[:, bk * 128:(bk + 1) * 128], xb[:, bk * 128:(bk + 1) * 128], g.ident_b[:])
                            return ins
                        sc.op("pe", tr, reads=[b_sb[bi], g.b_ident], writes=[g.psb[tb]])
                        sc.op("act", lambda e, TP=TP, qi=qi, tt=tt: e.copy(
                            out=qts[qi][:, :, tt * 128:(tt + 1) * 128],
                            in_=TP[:, 0:512].rearrange("p (b t) -> p b t", b=4)),
                            reads=[g.psb[tb]], writes=[b_qts[qi]])
                        if tt == 3:
                            dst = S.QT if kind == "q" else S.KT
                            sc.dma("sp", dst[ci * 4:(ci + 1) * 4, :, t0g:t0g + 512].rearrange("b d t -> d b t"),
                                   qts[qi][:], b_qts[qi], reads=[b_qts[qi]], writes=[g.dbuf["QT" if kind == "q" else "KT"]])
                    elif kind == "v":
                        if prompt and not g.cfg.get("skip_nv"):
                            fi = cnt["sf"] % NF
                            cnt["sf"] += 1
                            sc.op("act", lambda e, fi=fi, P=P: e.copy(out=sf32[fi][:], in_=P[:, :]),
                                  reads=[bP], writes=[b_sf[fi]])
                            sc.dma("sp", O.nv[t0:t0 + 128, ci * 512:(ci + 1) * 512], sf32[fi][:], b_sf[fi],
                                   reads=[b_sf[fi]], writes=[g.dbuf["out"]])
                        bi = cnt["sb"] % NF
                        cnt["sb"] += 1
                        if prompt and not g.cfg.get("skip_nv"):
                            sc.op("dve", lambda e, bi=bi, fi=fi: e.tensor_copy(out=sb16[bi][:], in_=sf32[fi][:]),
                                  reads=[b_sf[fi]], writes=[b_sb[bi]])
                        else:
                            sc.op("dve", lambda e, bi=bi, P=P: e.tensor_copy(out=sb16[bi][:], in_=P[:, :]),
                                  reads=[bP], writes=[b_sb[bi]])
                        if not g.cfg.get("skip_V"):
                            sc.dma("sp", S.V[t0:t0 + 128, ci * 512:(ci + 1) * 512], sb16[bi][:], b_sb[bi],
                                   reads=[b_sb[bi]], writes=[g.dbuf["V"]])
                    else:
                        fi = cnt["sf"] % NF
                        cnt["sf"] += 1
                        ek = "act" if (cnt["sf"] % 2 == 0) else "dve"
                        if ek == "act":
                            sc.op("act", lambda e, fi=fi, P=P, ncols=ncols: e.copy(out=sf32[fi][:, 0:ncols], in_=P[:, 0:ncols]),
                                  reads=[bP], writes=[b_sf[fi]])
                        else:
                            sc.op("dve", lambda e, fi=fi, P=P, ncols=ncols: e.tensor_copy(out=sf32[fi][:, 0:ncols], in_=P[:, 0:ncols]),
                                  reads=[bP], writes=[b_sf[fi]])
                        if kind == "z":
                            dst = S.Z[t0:t0 + 128, ci * 512:(ci + 1) * 512]
                            src = sf32[fi][:]
                            dn = "Z"
                        elif kind == "x":
                            dst = S.XBC[t0:t0 + 128, ci * 768:ci * 768 + 512]
                            src = sf32[fi][:]
                            dn = "XBC"
                        elif kind in ("B", "C"):
                            off = 512 if kind == "B" else 640
                            dst = S.XBC[t0:t0 + 128, :].rearrange("t (g c) -> t g c", g=4)[:, :, off:off + 128]
                            src = sf32[fi][:].rearrange("p (g c) -> p g c", g=4)
                            dn = "XBC"
                        else:
                            dst = S.DT[t0:t0 + 128, :]
                            src = sf32[fi][:, 0:64]
                            dn = "DT"
                        sc.dma(g.cfg.get("zq", "sp"), dst, src, b_sf[fi], reads=[b_sf[fi]], writes=[g.dbuf[dn]])


def _rope_tables(pos):
    pos = np.asarray(pos)
    row = (pos // 64).astype(np.float32)
    col = (pos % 64).astype(np.float32)
    inv = (1.0 / (np.float32(10000.0) ** (np.arange(0, 64, 2, dtype=np.float32) / np.float32(64)))).astype(np.float32)
    ar = row[:, None] * inv[None, :]
    ac = col[:, None] * inv[None, :]
    cr, sr, cc, s_c = np.cos(ar), np.sin(ar), np.cos(ac), np.sin(ac)
    C = np.concatenate([cr, cr, cc, cc], axis=1).astype(np.float32)
    Ssg = np.concatenate([-sr, sr, -s_c, s_c], axis=1).astype(np.float32)
    return np.ascontiguousarray(C), np.ascontiguousarray(Ssg)


def prep_inputs(inp):
    f = lambda a: np.ascontiguousarray(np.asarray(a, dtype=np.float32))
    x_prompt, x_sample = f(inp["x_prompt"]), f(inp["x_sample"])
    w_in = f(inp["w_in"])[0]
    w_in_odd = w_in.copy()
    w_in_odd[:, 11264:11296] = w_in[:, 11296:11328]
    w_in_odd[:, 11296:11328] = w_in[:, 11264:11296]
    shared = dict(
        w_ada=f(inp["w_ada"])[0], b_ada=f(inp["b_ada"]).reshape(1, -1),
        w_out=f(inp["w_out"])[0], w_up=f(inp["w_up"])[0], w_down=f(inp["w_down"])[0],
        lam=np.concatenate([f(inp[k]).reshape(-1) for k in ("lambda_q1", "lambda_k1", "lambda_q2", "lambda_k2")]).reshape(1, -1),
        g_subln=f(inp["g_subln"]).reshape(1, -1), conv_b=f(inp["conv_b"]).reshape(1, -1),
        d_skip=f(inp["d_skip"]).reshape(1, -1), g_ssm=f(inp["g_ssm_norm"]).reshape(1, -1),
        ident=np.eye(128, dtype=np.float32),
    )
    grows = np.stack([f(inp[k])[0] for k in ("g_mix_pre", "g_mix_post", "g_mlp_pre", "g_mlp_post")])
    shared["grows"] = np.ascontiguousarray(grows)
    shared["gcols"] = np.ascontiguousarray(grows.reshape(4, NKC, 128).transpose(2, 0, 1).reshape(128, 4 * NKC))
    conv_w = f(inp["conv_w"])[0]
    a_log, dt_bias = f(inp["a_log"])[0], f(inp["dt_bias"])[0]
    maps = []
    for i in range(8):
        b, hf = i // 2, i % 2
        rev = hf == 1
        m = dict(shared)
        xp = x_prompt[4 * i:4 * i + 4]
        xs = x_sample[b]
        pos = np.arange(NS_ALL)
        if rev:
            xp = xp[:, ::-1]
            xs = xs[::-1]
            pos = pos[::-1]
        m["xp"] = np.ascontiguousarray(xp.reshape(NP_TOK, D))
        m["xs"] = np.ascontiguousarray(xs)
        cc = np.stack([f(inp["c_ctx"]), f(inp["c"])[b]])
        m["cT"] = np.ascontiguousarray(cc.reshape(2, NKC, 128).transpose(2, 1, 0).reshape(128, NKC * 2))
        m["ck"] = np.ascontiguousarray(f(inp["cache_k"])[b, 0].reshape(PAST, 2048))
        m["cv"] = np.ascontiguousarray(f(inp["cache_v"])[b, 0].reshape(PAST, 2048))
        sfw = f(inp["state_ssm_fwd"])[b, 0].reshape(32 * 64, 128)
        sbw = f(inp["state_ssm_bwd"])[b, 0].reshape(32 * 64, 128)
        m["sf"], m["sb"] = (sbw, sfw) if rev else (sfw, sbw)
        m["w_in"] = w_in_odd if rev else w_in
        m["conv_w"] = np.ascontiguousarray(conv_w[::-1]) if rev else conv_w
        m["a_log"] = np.ascontiguousarray((a_log[::-1] if rev else a_log).reshape(1, 64))
        m["dt_bias"] = np.ascontiguousarray((dt_bias[::-1] if rev else dt_bias).reshape(1, 64))
        m["ropec"], m["ropes"] = _rope_tables(pos)
        maps.append(m)
    return maps


_NC_CACHE = {}


def kernel(**inputs):
    maps = prep_inputs(inputs)
    if "nc" not in _NC_CACHE:
        _NC_CACHE["nc"] = build()
    nc = _NC_CACHE["nc"]
    res = run_bass_kernel_spmd(nc, maps, core_ids=list(range(8)))
    R = res.results
    yp = np.zeros((32, 256, D), np.float32)
    ys = np.zeros((4, NS_ALL, D), np.float32)
    nk = np.zeros((32, 1, 256, 8, 2, 128), np.float32)
    nv = np.zeros((32, 1, 256, 8, 256), np.float32)
    nsf = np.zeros((32, 1, 32, 64, 128), np.float32)
    nsb = np.zeros((32, 1, 32, 64, 128), np.float32)
    for i in range(8):
        b, hf = i // 2, i % 2
        r = R[i]
        ypc = np.asarray(r["o_yp"]).reshape(4, 256, D)
        ysc = np.asarray(r["o_ys"])
        nkc = np.asarray(r["o_nk"]).reshape(4, 256, 8, 2, 128)
        nvc = np.asarray(r["o_nv"]).reshape(4, 256, 8, 256)
        f_ = np.asarray(r["o_nsf"]).reshape(4, 32, 64, 128)
        b_ = np.asarray(r["o_nsb"]).reshape(4, 32, 64, 128)
        if hf == 1:
            ypc, nkc, nvc, ysc = ypc[:, ::-1], nkc[:, ::-1], nvc[:, ::-1], ysc[::-1]
            f_, b_ = b_, f_
            ys[b, 2048:] = ysc
        else:
            ys[b, :2048] = ysc
        yp[4 * i:4 * i + 4] = ypc
        nk[4 * i:4 * i + 4, 0] = nkc
        nv[4 * i:4 * i + 4, 0] = nvc
        nsf[4 * i:4 * i + 4, 0] = f_
        nsb[4 * i:4 * i + 4, 0] = b_
    return yp, ys, nk, nv, nsf, nsb


def stage_ctxkv(g):
    nc, sc, I, S = g.nc, g.sc, g.I, g.S
    with ExitStack() as st:
        xf = st.enter_context(nc.sbuf_tensor("c_xf", [128, 2048], F32))
        xb = st.enter_context(nc.sbuf_tensor("c_xb", [128, 2048], BF16))
        kts = st.enter_context(nc.sbuf_tensor("c_kts", [128, 16, 128], BF16))
        b_xf, b_xb, b_kts = Buf("c_xf"), Buf("c_xb"), Buf("c_kts")
        for kt in range(PAST // 128):
            r0 = kt * 128
            sc.dma("sp", xf[:], I.ck[r0:r0 + 128, :], b_xf, writes=[b_xf])
            sc.op("dve", lambda e: e.tensor_copy(out=xb[:], in_=xf[:]), reads=[b_xf], writes=[b_xb])
            for q in range(2):
                tb = 6 + q
                TP = g.ps[tb][:, :].bitcast(BF16)

                def tr(e, q=q, TP=TP):
                    ins = None
                    for i in range(8):
                        bk = q * 8 + i
                        ins = e.transpose(TP[:, i * 128:(i + 1) * 128], xb[:, bk * 128:(bk + 1) * 128], g.ident_b[:])
                    return ins
                sc.op("pe", tr, reads=[b_xb, g.b_ident], writes=[g.psb[tb]])
                sc.op("act", lambda e, q=q, TP=TP: e.copy(out=kts[:, q * 8:(q + 1) * 8, :],
                                                          in_=TP[:, :].rearrange("p (b t) -> p b t", b=8)),
                      reads=[g.psb[tb]], writes=[b_kts])
            sc.dma("sp", S.KT[:, :, NT + r0:NT + r0 + 128].rearrange("b d t -> d b t"), kts[:], b_kts,
                   reads=[b_kts], writes=[g.dbuf["KT"]])
            sc.dma("sp", xf[:], I.cv[r0:r0 + 128, :], b_xf, writes=[b_xf])
            sc.op("dve", lambda e: e.tensor_copy(out=xb[:], in_=xf[:]), reads=[b_xf], writes=[b_xb])
            sc.dma("sp", S.V[NT + r0:NT + r0 + 128, :], xb[:], b_xb, reads=[b_xb], writes=[g.dbuf["V"]])


def stage_ssd(g):
    nc, sc, I, S, O = g.nc, g.sc, g.I, g.S, g.O
    seqs = g.cfg.get("ssd_seqs", [0, 1, 2, 3, 4])
    groups = g.cfg.get("ssd_groups", [0, 1, 2, 3])
    NCH = 16
    with ExitStack() as st:
        def sb(name, shape, dt=F32):
            return st.enter_context(nc.sbuf_tensor("s_" + name, shape, dt))
        Uf, Ub = sb("Uf", [128, 128]), sb("Ub", [128, 128])
        Mf, Mb = sb("Mf", [128, 8, 128]), sb("Mb", [128, 8, 128])
        ones = sb("ones", [128, 128])
        b_const = Buf("s_const")

        def mk_consts(e):
            e.memset(ones[:], 1.0)
            e.memset(Uf[:], 1.0)
            e.memset(Ub[:], 1.0)
            e.memset(Mf[:], 0.0)
            e.memset(Mb[:], 0.0)
            e.affine_select(out=Uf[:], in_=Uf[:], pattern=[[1, 128]], compare_op=ALU.is_ge, fill=0.0, base=0, channel_multiplier=-1)
            e.affine_select(out=Ub[:], in_=Ub[:], pattern=[[-1, 128]], compare_op=ALU.is_ge, fill=0.0, base=0, channel_multiplier=1)
            e.affine_select(out=Mf[:], in_=Mf[:], pattern=[[0, 8], [1, 128]], compare_op=ALU.is_ge, fill=-30000.0, base=0, channel_multiplier=-1)
            return e.affine_select(out=Mb[:], in_=Mb[:], pattern=[[0, 8], [-1, 128]], compare_op=ALU.is_ge, fill=-30000.0, base=0, channel_multiplier=1)
        sc.op("pool", mk_consts, writes=[b_const])
        dtb, Ab, dsk = sb("dtb", [128, 64]), sb("Ab", [128, 64]), sb("dsk", [128, 32])
        sc.dma("sp", dtb[:], I.dt_bias[0, :].partition_broadcast(128), b_const, writes=[b_const])
        sc.dma("sp", Ab[:], I.a_log[0, :].partition_broadcast(128), b_const, writes=[b_const])
        sc.dma("sp", dsk[:], I.d_skip[0, :].partition_broadcast(128), b_const, writes=[b_const])
        sc.op("act", lambda e: e.activation(out=Ab[:], in_=Ab[:], func=AF.Exp), reads=[b_const], writes=[b_const])
        sc.op("dve", lambda e: e.tensor_scalar(out=Ab[:], in0=Ab[:], scalar1=-1.0, scalar2=None, op0=ALU.mult),
              reads=[b_const], writes=[b_const])
        W5, cbias, gssm = sb("W5", [128, 5, 768]), sb("cbias", [128, 768]), sb("gssm", [128, 512])
        b_gc = Buf("s_gc")
        xs = [[sb("xs%d_%d" % (i, j), [128, 768]) for j in range(5)] for i in range(2)]
        b_xs = [[Buf("s_xs%d_%d" % (i, j)) for j in range(5)] for i in range(2)]
        acc, xc = sb("acc", [128, 768]), sb("xc", [128, 768])
        xcb = sb("xcb", [128, 768], BF16)
        b_acc, b_xc = Buf("s_acc"), Buf("s_xc")
        dtr = [sb("dtr%d" % i, [128, 64]) for i in range(2)]
        b_dtr = [Buf("s_dtr0"), Buf("s_dtr1")]
        sm = {n: sb(n, [128, 64]) for n in ("ab", "ex", "dt", "a", "tot", "cs", "ncs", "ecs", "dcs", "cdec", "w2")}
        b_sm = Buf("s_sm")
        BT, CT = sb("BT", [128, 128], BF16), sb("CT", [128, NCH, 128], BF16)
        b_BT, b_CT = Buf("s_BT"), [Buf("s_CT%d" % i) for i in range(NCH)]
        AU = [sb("AU%d" % d, [128, 8, 128]) for d in range(2)]
        b_AU = [Buf("s_AU0"), Buf("s_AU1")]
        Lm = [sb("Lm%d" % d, [128, 8, 128]) for d in range(2)]
        b_Lm = [Buf("s_Lm0"), Buf("s_Lm1")]
        wts = [sb("wts%d" % d, [128, 8, 128], BF16) for d in range(2)]
        b_wts = [Buf("s_wts0"), Buf("s_wts1")]
        cbT = sb("cbT", [128, 128])
        b_cbT = Buf("s_cbT")
        xdt = [sb("xdt%d" % d, [128, 512], BF16) for d in range(2)]
        xdd = [sb("xdd%d" % d, [128, 512], BF16) for d in range(2)]
        b_xdt, b_xdd = [Buf("s_xdt0"), Buf("s_xdt1")], [Buf("s_xdd0"), Buf("s_xdd1")]
        state = [sb("state%d" % d, [128, 512]) for d in range(2)]
        prevb = [sb("prevb%d" % d, [128, 512], BF16) for d in range(2)]
        b_state = [Buf("s_state0"), Buf("s_state1")]
        ysum = sb("ysum", [128, NCH, 512])
        Sb_st = sb("Sbst", [128, NCH, 512])
        ecs_st = sb("ecsst", [128, NCH, 8])
        cdec_st = sb("cdecst", [128, NCH, 8])
        b_ch = [Buf("s_ch%d" % i) for i in range(NCH)]
        tmp = sb("tmp", [128, 512])
        b_tmp = Buf("s_tmp")
        zt = sb("zt", [128, 512])
        b_zt = Buf("s_zt")
        yb = sb("yb", [128, 512], BF16)
        b_yb = Buf("s_yb")
        mxs = sb("mxs", [128, 4, 128], BF16)
        b_mxs = Buf("s_mxs")
        stT = sb("stT", [128, 4, 128])
        b_stT = Buf("s_stT")
        stin = sb("stin", [128, 4, 128])
        b_stin = Buf("s_stin")
        nrm = sb("nrm", [128, 2])
        b_nrm = Buf("s_nrm")
        junk = sb("junk", [128, 512])
        b_junk = Buf("s_junk")
        xsi = [0]

        def load_shift(t0, seq_lo, seq_hi, gi, ncols=768):
            i = xsi[0] % 2
            xsi[0] += 1
            for j in range(5):
                lo = t0 + j - 2
                hi = lo + 128
                p0 = max(0, seq_lo - lo)
                p1 = 128 - max(0, hi - seq_hi)
                if p0 > 0 or p1 < 128:
                    sc.op("pool", lambda e, i=i, j=j: e.memset(xs[i][j][:], 0.0), writes=[b_xs[i][j]])
                sc.dma("sp", xs[i][j][p0:p1, 0:ncols], S.XBC[lo + p0:lo + p1, gi * 768:gi * 768 + ncols], b_xs[i][j],
                       reads=[g.dbuf["XBC"]], writes=[b_xs[i][j]])
            return i

        def conv_silu(i, ncols=768):
            for j in range(5):
                ek = "pool" if j % 2 == 0 else "dve"
                sc.op(ek, lambda e, j=j: e.tensor_tensor(out=xs[i][j][:, 0:ncols], in0=xs[i][j][:, 0:ncols],
                                                          in1=W5[:, j, 0:ncols], op=ALU.mult),
                      reads=[b_xs[i][j], b_gc], writes=[b_xs[i][j]])
            sc.op("dve", lambda e: e.tensor_tensor(out=acc[:, 0:ncols], in0=xs[i][0][:, 0:ncols], in1=xs[i][1][:, 0:ncols], op=ALU.add),
                  reads=[b_xs[i][0], b_xs[i][1]], writes=[b_acc])
            for j in (2, 3, 4):
                sc.op("dve", lambda e, j=j: e.tensor_tensor(out=acc[:, 0:ncols], in0=acc[:, 0:ncols], in1=xs[i][j][:, 0:ncols], op=ALU.add),
                      reads=[b_xs[i][j], b_acc], writes=[b_acc])
            sc.op("dve", lambda e: e.tensor_tensor(out=acc[:, 0:ncols], in0=acc[:, 0:ncols], in1=cbias[:, 0:ncols], op=ALU.add),
                  reads=[b_acc, b_gc], writes=[b_acc])
            sc.op("act", lambda e: e.activation(out=xc[:, 0:ncols], in_=acc[:, 0:ncols], func=AF.Silu), reads=[b_acc], writes=[b_xc])
            sc.op("dve", lambda e: e.tensor_copy(out=xcb[:, 0:ncols], in_=xc[:, 0:ncols]), reads=[b_xc], writes=[b_xc])

        def dt_stuff(t0, gi, dirs):
            di = xsi[0] % 2
            sc.dma("sp", dtr[di][:], S.DT[t0:t0 + 128, :], b_dtr[di], reads=[g.dbuf["DT"]], writes=[b_dtr[di]])
            R = [b_dtr[di], b_const, b_sm]
            Wr = [b_sm]
            s_ = sm
            sc.op("dve", lambda e: e.tensor_tensor(out=s_["dt"][:], in0=dtr[di][:], in1=dtb[:], op=ALU.add), reads=R, writes=Wr)
            sc.op("act", lambda e: e.activation(out=s_["ab"][:], in_=s_["dt"][:], func=AF.Abs), reads=R, writes=Wr)
            sc.op("act", lambda e: e.activation(out=s_["ex"][:], in_=s_["ab"][:], func=AF.Exp, scale=-1.0), reads=R, writes=Wr)
            sc.op("act", lambda e: e.activation(out=s_["ex"][:], in_=s_["ex"][:], func=AF.Ln, bias=1.0), reads=R, writes=Wr)
            sc.op("dve", lambda e: e.scalar_tensor_tensor(out=s_["dt"][:], in0=s_["dt"][:], scalar=0.0, in1=s_["ex"][:],
                                                          op0=ALU.max, op1=ALU.add), reads=R, writes=Wr)
            sc.op("dve", lambda e: e.tensor_tensor(out=s_["a"][:], in0=s_["dt"][:], in1=Ab[:], op=ALU.mult), reads=R, writes=Wr)
            P6 = g.ps[6]
            c_f = slice(gi * 8, gi * 8 + 8)
            c_b = slice(32 + gi * 8, 32 + gi * 8 + 8)

            def mm(e):
                e.matmul(P6[:, 0:64], lhsT=ones[:], rhs=s_["a"][:], start=True, stop=True)
                e.matmul(P6[:, 64:72], lhsT=Uf[:], rhs=s_["a"][:, c_f], start=True, stop=True)
                return e.matmul(P6[:, 72:80], lhsT=Ub[:], rhs=s_["a"][:, c_b], start=True, stop=True)
            sc.op("pe", mm, reads=[b_sm, b_const], writes=[g.psb[6]])
            R2 = [g.psb[6], b_sm]
            sc.op("dve", lambda e: e.tensor_copy(out=s_["tot"][:, 0:8], in_=P6[:, c_f]), reads=R2, writes=Wr)
            sc.op("dve", lambda e: e.tensor_copy(out=s_["tot"][:, 8:16], in_=P6[:, c_b]), reads=R2, writes=Wr)
            sc.op("dve", lambda e: e.tensor_copy(out=s_["cs"][:, 0:16], in_=P6[:, 64:80]), reads=R2, writes=Wr)
            sc.op("dve", lambda e: e.tensor_scalar(out=s_["ncs"][:, 0:16], in0=s_["cs"][:, 0:16], scalar1=-1.0, scalar2=None, op0=ALU.mult),
                  reads=R2, writes=Wr)
            sc.op("act", lambda e: e.activation(out=s_["ecs"][:, 0:16], in_=s_["cs"][:, 0:16], func=AF.Exp), reads=R2, writes=Wr)
            sc.op("dve", lambda e: e.tensor_tensor(out=s_["dcs"][:, 0:16], in0=s_["tot"][:, 0:16], in1=s_["cs"][:, 0:16], op=ALU.subtract),
                  reads=R2, writes=Wr)
            sc.op("act", lambda e: e.activation(out=s_["dcs"][:, 0:16], in_=s_["dcs"][:, 0:16], func=AF.Exp), reads=R2, writes=Wr)
            sc.op("act", lambda e: e.activation(out=s_["cdec"][:, 0:16], in_=s_["tot"][:, 0:16], func=AF.Exp), reads=R2, writes=Wr)
            sc.op("dve", lambda e: e.tensor_copy(out=s_["w2"][:, 0:8], in_=s_["dt"][:, c_f]), reads=R2, writes=Wr)
            sc.op("dve", lambda e: e.tensor_copy(out=s_["w2"][:, 8:16], in_=s_["dt"][:, c_b]), reads=R2, writes=Wr)
            sc.op("dve", lambda e: e.tensor_tensor(out=s_["w2"][:, 16:32], in0=s_["w2"][:, 0:16], in1=s_["dcs"][:, 0:16], op=ALU.mult),
                  reads=R2, writes=Wr)
            sc.op("dve", lambda e: e.tensor_copy(out=s_["w2"][:, 32:40], in_=s_["a"][:, c_f]), reads=R2, writes=Wr)
            sc.op("dve", lambda e: e.tensor_copy(out=s_["w2"][:, 40:48], in_=s_["a"][:, c_b]), reads=R2, writes=Wr)

        def xmul(dst, b_dst, col0, d):
            sc.op("dve", lambda e: e.tensor_tensor(
                out=dst[:].rearrange("p (h q) -> p h q", h=8), in0=xc[:, 0:512].rearrange("p (h q) -> p h q", h=8),
                in1=sm["w2"][:, col0 + d * 8:col0 + d * 8 + 8].unsqueeze(2).to_broadcast([128, 8, 64]), op=ALU.mult),
                reads=[b_xc, b_sm], writes=[b_dst])

        def chunk_states(d):
            xmul(xdd[d], b_xdd[d], 16, d)
            sc.op("pe", lambda e: e.matmul(g.ps[5][:, :], lhsT=xcb[:, 512:640], rhs=xdd[d][:], start=True, stop=True),
                  reads=[b_xc, b_xdd[d]], writes=[g.psb[5]])

        def state_update(d, S_ap, S_bufs, cdec_ap, cdec_bufs):
            sc.op("dve", lambda e: e.tensor_tensor(
                out=state[d][:].rearrange("p (h q) -> p h q", h=8), in0=state[d][:].rearrange("p (h q) -> p h q", h=8),
                in1=cdec_ap.unsqueeze(2).to_broadcast([128, 8, 64]), op=ALU.mult),
                reads=[b_state[d]] + cdec_bufs, writes=[b_state[d]])
            sc.op("dve", lambda e: e.tensor_tensor(out=state[d][:], in0=state[d][:], in1=S_ap, op=ALU.add),
                  reads=[b_state[d]] + S_bufs, writes=[b_state[d]])
            sc.op("pool", lambda e: e.tensor_copy(out=prevb[d][:], in_=state[d][:]), reads=[b_state[d]], writes=[b_state[d]])

        def init_state(d, src):
            if src is None:
                sc.op("pool", lambda e: e.memset(state[d][:], 0.0), writes=[b_state[d]])
                sc.op("pool", lambda e: e.memset(prevb[d][:], 0.0), writes=[b_state[d]])
                return
            sc.dma("sp", stin[:], src.rearrange("(c p) n -> p c n", p=128), b_stin, writes=[b_stin])

            def tr(e):
                ins = None
                for c in range(4):
                    ins = e.transpose(g.ps[5][:, c * 128:(c + 1) * 128], stin[:, c, :], g.ident_f[:])
                return ins
            sc.op("pe", tr, reads=[b_stin, g.b_ident], writes=[g.psb[5]])
            sc.op("dve", lambda e: e.tensor_copy(out=state[d][:], in_=g.ps[5][:, :]), reads=[g.psb[5]], writes=[b_state[d]])
            sc.op("pool", lambda e: e.tensor_copy(out=prevb[d][:], in_=state[d][:]), reads=[b_state[d]], writes=[b_state[d]])

        def out_state(d, dst):
            def tr(e):
                ins = None
                for c in range(4):
                    ins = e.transpose(g.ps[5][:, c * 128:(c + 1) * 128], state[d][:, c * 128:(c + 1) * 128], g.ident_f[:])
                return ins
            sc.op("pe", tr, reads=[b_state[d], g.b_ident], writes=[g.psb[5]])
            sc.op("dve", lambda e: e.tensor_copy(out=stT[:].rearrange("p c n -> p (c n)"), in_=g.ps[5][:, :]),
                  reads=[g.psb[5]], writes=[b_stT])
            sc.dma("sp", dst.rearrange("(c p) n -> p c n", p=128), stT[:], b_stT, reads=[b_stT], writes=[g.dbuf["out"]])

        def diag_part(d, c):
            U = Uf if d == 0 else Ub
            M = Mf if d == 0 else Mb
            a_col = 32 + d * 8
            sc.op("dve" if d == 0 else "pool", lambda e: e.tensor_tensor(
                out=AU[d][:], in0=sm["w2"][:, a_col:a_col + 8].unsqueeze(2).to_broadcast([128, 8, 128]),
                in1=U[:].unsqueeze(1).to_broadcast([128, 8, 128]), op=ALU.mult),
                reads=[b_sm, b_const], writes=[b_AU[d]])
            pb0 = 0 + 2 * d

            def mm(e):
                ins = None
                for hf in range(2):
                    e.matmul(g.ps[pb0 + hf][:, :], lhsT=ones[:], rhs=AU[d][:, hf * 4:(hf + 1) * 4, :].rearrange("p h l -> p (h l)"),
                             start=True, stop=False)
                    ins = e.matmul(g.ps[pb0 + hf][:, :], lhsT=g.ident_f[:], rhs=M[:, hf * 4:(hf + 1) * 4, :].rearrange("p h l -> p (h l)"),
                                   start=False, stop=True)
                return ins
            sc.op("pe", mm, reads=[b_AU[d], b_const, g.b_ident], writes=[g.psb[pb0], g.psb[pb0 + 1]])
            for h in range(8):
                sc.op("act", lambda e, h=h: e.activation(
                    out=Lm[d][:, h, :], in_=g.ps[pb0 + h // 4][:, (h % 4) * 128:(h % 4 + 1) * 128], func=AF.Exp,
                    bias=sm["ncs"][:, d * 8 + h:d * 8 + h + 1]),
                    reads=[g.psb[pb0], g.psb[pb0 + 1], b_sm], writes=[b_Lm[d]])
            sc.op("dve" if d == 1 else "pool", lambda e: e.tensor_tensor(
                out=wts[d][:], in0=Lm[d][:], in1=cbT[:].unsqueeze(1).to_broadcast([128, 8, 128]), op=ALU.mult),
                reads=[b_Lm[d], b_cbT], writes=[b_wts[d]])
            xmul(xdt[d], b_xdt[d], 0, d)

            def mm2(e):
                ins = None
                for h in range(8):
                    ins = e.matmul(g.ps[4][:, h * 64:(h + 1) * 64], lhsT=wts[d][:, h, :], rhs=xdt[d][:, h * 64:(h + 1) * 64],
                                   start=True, stop=True)
                return ins
            sc.op("pe", mm2, reads=[b_wts[d], b_xdt[d]], writes=[g.psb[4]])

        def yoff_add(d, c, ecs_ap, ecs_bufs):
            sc.op("pe", lambda e: e.matmul(g.ps[5][:, :], lhsT=CT[:, c, :], rhs=prevb[d][:], start=True, stop=True),
                  reads=[b_CT[c], b_state[d]], writes=[g.psb[5]])
            sc.op("dve", lambda e: e.tensor_tensor(
                out=tmp[:].rearrange("p (h q) -> p h q", h=8), in0=g.ps[5][:, :].rearrange("p (h q) -> p h q", h=8),
                in1=ecs_ap.unsqueeze(2).to_broadcast([128, 8, 64]), op=ALU.mult),
                reads=[g.psb[5]] + ecs_bufs, writes=[b_tmp])
            sc.op("pool", lambda e: e.tensor_tensor(out=ysum[:, c, :], in0=ysum[:, c, :], in1=tmp[:], op=ALU.add),
                  reads=[b_tmp, b_ch[c]], writes=[b_ch[c]])

        for gi in groups:
            with nc.allow_non_contiguous_dma(reason="broadcast const loads"):
                for (o0, c0, n) in ((0, gi * 512, 512), (512, 2048 + gi * 128, 128), (640, 2560 + gi * 128, 128)):
                    sc.dma("sp", W5[:, :, o0:o0 + n], I.conv_w[:, c0:c0 + n].partition_broadcast(128), b_gc, writes=[b_gc])
                    sc.dma("sp", cbias[:, o0:o0 + n], I.conv_b[0, c0:c0 + n].partition_broadcast(128), b_gc, writes=[b_gc])
                sc.dma("sp", gssm[:], I.g_ssm[0, gi * 512:(gi + 1) * 512].partition_broadcast(128), b_gc, writes=[b_gc])
            for seq in seqs:
                if seq < 4:
                    base, nown, nrem = seq * 256, 2, 0
                    seq_lo, seq_hi = base, base + 256
                    init_state(0, None)
                    init_state(1, None)
                else:
                    base, nown, nrem = NP_TOK, 16, 16
                    seq_lo, seq_hi = base, base + NS_ALL
                    init_state(0, I.sf[gi * 512:(gi + 1) * 512, :])
                    init_state(1, I.sb[gi * 512:(gi + 1) * 512, :])
                    for c in range(nown + nrem - 1, nown - 1, -1):
                        t0 = base + c * 128
                        i = load_shift(t0, seq_lo, seq_hi, gi, 640)
                        conv_silu(i, 640)
                        dt_stuff(t0, gi, (1,))
                        chunk_states(1)
                        state_update(1, g.ps[5][:, :], [g.psb[5]], sm["cdec"][:, 8:16], [b_sm])
                for c in range(nown):
                    t0 = base + c * 128
                    i = load_shift(t0, seq_lo, seq_hi, gi)
                    conv_silu(i)
                    dt_stuff(t0, gi, (0, 1))
                    TP = g.ps[7][:, :].bitcast(BF16)

                    def tr(e, TP=TP):
                        e.transpose(TP[:, 0:128], xcb[:, 512:640], g.ident_b[:])
                        return e.transpose(TP[:, 128:256], xcb[:, 640:768], g.ident_b[:])
                    sc.op("pe", tr, reads=[b_xc, g.b_ident], writes=[g.psb[7]])
                    sc.op("act", lambda e, TP=TP: e.copy(out=BT[:], in_=TP[:, 0:128]), reads=[g.psb[7]], writes=[b_BT])
                    sc.op("act", lambda e, TP=TP, c=c: e.copy(out=CT[:, c, :], in_=TP[:, 128:256]), reads=[g.psb[7]], writes=[b_CT[c]])
                    sc.op("pe", lambda e, c=c: e.matmul(g.ps[6][:, 128:256], lhsT=BT[:], rhs=CT[:, c, :], start=True, stop=True),
                          reads=[b_BT, b_CT[c]], writes=[g.psb[6]])
                    sc.op("act", lambda e: e.copy(out=cbT[:], in_=g.ps[6][:, 128:256]), reads=[g.psb[6]], writes=[b_cbT])
                    sc.op("dve", lambda e, c=c: e.tensor_tensor(
                        out=ysum[:, c, :].rearrange("p (h q) -> p h q", h=8), in0=xc[:, 0:512].rearrange("p (h q) -> p h q", h=8),
                        in1=dsk[:, gi * 8:(gi + 1) * 8].unsqueeze(2).to_broadcast([128, 8, 64]), op=ALU.mult),
                        reads=[b_xc, b_const], writes=[b_ch[c]])
                    for d in (0, 1):
                        diag_part(d, c)
                        sc.op("dve", lambda e, c=c: e.tensor_tensor(out=ysum[:, c, :], in0=ysum[:, c, :], in1=g.ps[4][:, :], op=ALU.add),
                              reads=[g.psb[4], b_ch[c]], writes=[b_ch[c]])
                    yoff_add(0, c, sm["ecs"][:, 0:8], [b_sm])
                    chunk_states(0)
                    state_update(0, g.ps[5][:, :], [g.psb[5]], sm["cdec"][:, 0:8], [b_sm])
                    chunk_states(1)
                    sc.op("act", lambda e, c=c: e.copy(out=Sb_st[:, c, :], in_=g.ps[5][:, :]), reads=[g.psb[5]], writes=[b_ch[c]])
                    sc.op("pool", lambda e, c=c: e.tensor_copy(out=ecs_st[:, c, :], in_=sm["ecs"][:, 8:16]), reads=[b_sm], writes=[b_ch[c]])
                    sc.op("pool", lambda e, c=c: e.tensor_copy(out=cdec_st[:, c, :], in_=sm["cdec"][:, 8:16]), reads=[b_sm], writes=[b_ch[c]])
                if seq < 4:
                    out_state(0, O.nsf[(seq * 32 + gi * 8) * 64:(seq * 32 + gi * 8 + 8) * 64, :])
                for c in range(nown - 1, -1, -1):
                    t0 = base + c * 128
                    yoff_add(1, c, ecs_st[:, c, :], [b_ch[c]])
                    state_update(1, Sb_st[:, c, :], [b_ch[c]], cdec_st[:, c, :], [b_ch[c]])
                    sc.dma("sp", zt[:], S.Z[t0:t0 + 128, gi * 512:(gi + 1) * 512], b_zt, reads=[g.dbuf["Z"]], writes=[b_zt])
                    sc.op("act", lambda e: e.activation(out=zt[:], in_=zt[:], func=AF.Silu), reads=[b_zt], writes=[b_zt])
                    sc.op("dve", lambda e, c=c: e.tensor_tensor(out=tmp[:], in0=ysum[:, c, :], in1=zt[:], op=ALU.mult),
                          reads=[b_ch[c], b_zt], writes=[b_tmp])
                    sc.op("act", lambda e: e.activation(out=junk[:], in_=tmp[:], func=AF.Square, accum_out=nrm[:, 0:1]),
                          reads=[b_tmp], writes=[b_junk, b_nrm])
                    sc.op("act", lambda e: e.activation(out=nrm[:, 1:2], in_=nrm[:, 0:1], func=AF.Sqrt, scale=1.0 / 512, bias=EPS),
                          reads=[b_nrm], writes=[b_nrm])
                    sc.op("dve", lambda e: e.reciprocal(out=nrm[:, 1:2], in_=nrm[:, 1:2]), reads=[b_nrm], writes=[b_nrm])
                    sc.op("dve", lambda e: e.scalar_tensor_tensor(out=yb[:], in0=tmp[:], scalar=nrm[:, 1:2], in1=gssm[:],
                                                                  op0=ALU.mult, op1=ALU.mult),
                          reads=[b_tmp, b_nrm, b_gc], writes=[b_yb])
                    TP = g.ps[7][:, :].bitcast(BF16)

                    def tr2(e, TP=TP):
                        ins = None
                        for k in range(4):
                            ins = e.transpose(TP[:, k * 128:(k + 1) * 128], yb[:, k * 128:(k + 1) * 128], g.ident_b[:])
                        return ins
                    sc.op("pe", tr2, reads=[b_yb, g.b_ident], writes=[g.psb[7]])
                    sc.op("act", lambda e, TP=TP: e.copy(out=mxs[:].rearrange("p k t -> p (k t)"), in_=TP[:, 0:512]),
                          reads=[g.psb[7]], writes=[b_mxs])
                    kc0 = 16 + gi * 4
                    sc.dma("sp", S.MIXT[kc0:kc0 + 4, :, t0:t0 + 128].rearrange("k d t -> d k t"), mxs[:], b_mxs,
                           reads=[b_mxs], writes=[g.dbuf["MIXT"]])
                if seq < 4:
                    out_state(1, O.nsb[(seq * 32 + gi * 8) * 64:(seq * 32 + gi * 8 + 8) * 64, :])


def stage_attn(g):
    nc, sc, I, S = g.nc, g.sc, g.I, g.S
    seqs = g.cfg.get("attn_seqs", [0, 1, 2, 3, 4])
    heads = g.cfg.get("attn_heads", list(range(8)))
    scale = 128 ** -0.5
    with ExitStack() as st:
        def sb(name, shape, dt=F32):
            return st.enter_context(nc.sbuf_tensor("t_" + name, shape, dt))
        NKMAX = (NS_ALL + PAST) // 128
        K2 = [sb("K2_%d" % i, [128, 2, NKMAX * 128], BF16) for i in range(2)]
        Vt = [sb("Vt_%d" % i, [128, NKMAX, 257], BF16) for i in range(2)]
        Q2 = [sb("Q2_%d" % i, [128, 2, 512], BF16) for i in range(2)]
        PT = [sb("PT_%d" % i, [128, 512], BF16) for i in range(3)]
        osb = sb("osb", [128, 4, 256])
        ob = sb("ob", [128, 256], BF16)
        mx = [sb("mx%d" % i, [128, 2, 512], BF16) for i in range(2)]
        lamt = sb("lamt", [128, 512])
        sm = sb("sm", [128, 8])
        rr = sb("rr", [128, 4])
        gsub = sb("gsub", [128, 256])
        junk = sb("junk", [128, 256])
        b_K2, b_Vt, b_Q2 = [Buf("t_K0"), Buf("t_K1")], [Buf("t_V0"), Buf("t_V1")], [Buf("t_Q0"), Buf("t_Q1")]
        b_PT = [Buf("t_PT%d" % i) for i in range(3)]
        b_osb, b_ob, b_mx = Buf("t_osb"), Buf("t_ob"), [Buf("t_mx0"), Buf("t_mx1")]
        b_c, b_rr, b_junk = Buf("t_c"), Buf("t_rr"), Buf("t_junk")
        sc.dma("sp", lamt[:], I.lam[0, :].partition_broadcast(128), b_c, writes=[b_c])
        sc.dma("sp", gsub[:], I.g_subln[0, :].partition_broadcast(128), b_c, writes=[b_c])
        for i in range(2):
            sc.op("dve", lambda e, i=i: e.tensor_tensor(out=lamt[:, i * 256:i * 256 + 128], in0=lamt[:, i * 256:i * 256 + 128],
                                                        in1=lamt[:, i * 256 + 128:i * 256 + 256], op=ALU.mult), reads=[b_c], writes=[b_c])
            sc.op("act", lambda e, i=i: e.activation(out=junk[:, 0:128], in_=lamt[:, i * 256:i * 256 + 128], func=AF.Identity,
                                                     accum_out=sm[:, i:i + 1]), reads=[b_c], writes=[b_c, b_junk])
        sc.op("act", lambda e: e.activation(out=sm[:, 2:4], in_=sm[:, 0:2], func=AF.Exp), reads=[b_c], writes=[b_c])
        sc.op("dve", lambda e: e.tensor_tensor(out=sm[:, 4:5], in0=sm[:, 3:4], in1=sm[:, 2:3], op=ALU.subtract), reads=[b_c], writes=[b_c])
        sc.op("dve", lambda e: e.tensor_scalar(out=sm[:, 4:5], in0=sm[:, 4:5], scalar1=-LAM_INIT, scalar2=None, op0=ALU.add),
              reads=[b_c], writes=[b_c])
        sc.op("dve", lambda e: e.tensor_scalar(out=gsub[:], in0=gsub[:], scalar1=1.0 - LAM_INIT, scalar2=None, op0=ALU.mult),
              reads=[b_c], writes=[b_c])
        for i in range(2):
            sc.op("pool", lambda e, i=i: e.memset(Vt[i][:, :, 256:257], 1.0), writes=[b_Vt[i]])
        cnt = dict(kv=0, q=0, pt=0, st=0, mx=0)
        for seq in seqs:
            if seq < 4:
                k0, nk, q0, nq = seq * 256, 256, seq * 256, 256
            else:
                k0, nk, q0, nq = NP_TOK, NS_ALL + PAST, NP_TOK, NS_OWN
            nkc = nk // 128
            for h in heads:
                ki = cnt["kv"] % 2
                cnt["kv"] += 1
                for j in range(2):
                    sc.dma("sp", K2[ki][:, j, 0:nk], S.KT[h * 2 + j, :, k0:k0 + nk], b_K2[ki], reads=[g.dbuf["KT"]], writes=[b_K2[ki]])
                sc.dma("sp", Vt[ki][:, 0:nkc, 0:256], S.V[k0:k0 + nk, h * 256:(h + 1) * 256].rearrange("(c p) v -> p c v", p=128),
                       b_Vt[ki], reads=[g.dbuf["V"]], writes=[b_Vt[ki]])
                for qt0 in range(0, nq, 512):
                    nqt = min(512, nq - qt0)
                    nqb = nqt // 128
                    qi = cnt["q"] % 2
                    cnt["q"] += 1
                    for j in range(2):
                        sc.dma("sp", Q2[qi][:, j, 0:nqt], S.QT[h * 2 + j, :, q0 + qt0:q0 + qt0 + nqt], b_Q2[qi],
                               reads=[g.dbuf["QT"]], writes=[b_Q2[qi]])
                    for j in range(2):
                        for kc in range(nkc):
                            pb = 4 + cnt["st"] % 3
                            cnt["st"] += 1
                            pi = cnt["pt"] % 3
                            cnt["pt"] += 1
                            sc.op("pe", lambda e, kc=kc, j=j, pb=pb: e.matmul(
                                g.ps[pb][:, 0:nqt], lhsT=K2[ki][:, j, kc * 128:(kc + 1) * 128], rhs=Q2[qi][:, j, 0:nqt],
                                start=True, stop=True), reads=[b_K2[ki], b_Q2[qi]], writes=[g.psb[pb]])
                            sc.op("act", lambda e, pb=pb, pi=pi: e.activation(out=PT[pi][:, 0:nqt], in_=g.ps[pb][:, 0:nqt],
                                                                              func=AF.Exp, scale=scale),
                                  reads=[g.psb[pb]], writes=[b_PT[pi]])

                            def av(e, kc=kc, pi=pi):
                                ins = None
                                for qb in range(nqb):
                                    ins = e.matmul(g.ps[qb][:, 0:257], lhsT=PT[pi][:, qb * 128:(qb + 1) * 128], rhs=Vt[ki][:, kc, :],
                                                   start=(kc == 0), stop=(kc == nkc - 1))
                                return ins
                            sc.op("pe", av, reads=[b_PT[pi], b_Vt[ki]], writes=[g.psb[qb] for qb in range(nqb)])
                        for qb in range(nqb):
                            A = g.ps[qb]
                            sc.op("dve", lambda e, A=A, j=j: e.reciprocal(out=rr[:, j:j + 1], in_=A[:, 256:257]),
                                  reads=[g.psb[qb]], writes=[b_rr])
                            if j == 0:
                                sc.op("dve", lambda e, A=A, qb=qb: e.tensor_scalar(out=osb[:, qb, :], in0=A[:, 0:256], scalar1=rr[:, 0:1],
                                                                                   scalar2=None, op0=ALU.mult),
                                      reads=[g.psb[qb], b_rr], writes=[b_osb])
                            else:
                                sc.op("dve", lambda e: e.tensor_tensor(out=rr[:, 2:3], in0=rr[:, 1:2], in1=sm[:, 4:5], op=ALU.mult),
                                      reads=[b_rr, b_c], writes=[b_rr])
                                sc.op("dve", lambda e, A=A, qb=qb: e.scalar_tensor_tensor(
                                    out=osb[:, qb, :], in0=A[:, 0:256], scalar=rr[:, 2:3], in1=osb[:, qb, :], op0=ALU.mult, op1=ALU.add),
                                    reads=[g.psb[qb], b_rr, b_osb], writes=[b_osb])
                    mi = cnt["mx"] % 2
                    cnt["mx"] += 1
                    for qb in range(nqb):
                        sc.op("act", lambda e, qb=qb: e.activation(out=junk[:], in_=osb[:, qb, :], func=AF.Square, accum_out=rr[:, 3:4]),
                              reads=[b_osb], writes=[b_rr, b_junk])
                        sc.op("act", lambda e: e.activation(out=rr[:, 3:4], in_=rr[:, 3:4], func=AF.Sqrt, scale=1.0 / 256, bias=EPS),
                              reads=[b_rr], writes=[b_rr])
                        sc.op("dve", lambda e: e.reciprocal(out=rr[:, 3:4], in_=rr[:, 3:4]), reads=[b_rr], writes=[b_rr])
                        sc.op("dve", lambda e, qb=qb: e.scalar_tensor_tensor(out=ob[:], in0=osb[:, qb, :], scalar=rr[:, 3:4], in1=gsub[:],
                                                                             op0=ALU.mult, op1=ALU.mult),
                              reads=[b_osb, b_rr, b_c], writes=[b_ob])
                        TP = g.ps[7][:, :].bitcast(BF16)

                        def tr(e, TP=TP):
                            e.transpose(TP[:, 0:128], ob[:, 0:128], g.ident_b[:])
                            return e.transpose(TP[:, 128:256], ob[:, 128:256], g.ident_b[:])
                        sc.op("pe", tr, reads=[b_ob, g.b_ident], writes=[g.psb[7]])
                        sc.op("act", lambda e, TP=TP, qb=qb, mi=mi: e.copy(out=mx[mi][:, :, qb * 128:(qb + 1) * 128],
                                                                          in_=TP[:, 0:256].rearrange("p (k t) -> p k t", k=2)),
                              reads=[g.psb[7]], writes=[b_mx[mi]])
                    sc.dma("sp", S.MIXT[h * 2:h * 2 + 2, :, q0 + qt0:q0 + qt0 + nqt].rearrange("k d t -> d k t"), mx[mi][:, :, 0:nqt],
                           b_mx[mi], reads=[b_mx[mi]], writes=[g.dbuf["MIXT"]])


def stage_mlp(g):
    nc, sc, I, S, O = g.nc, g.sc, g.I, g.S, g.O
    groups = g.cfg.get("mlp_groups", list(range(NOWN // 512)))
    nfb = g.cfg.get("mlp_fblocks", 16)
    if "MIXO" not in g.dbuf:
        g.dbuf["MIXO"] = Buf("d_MIXO", acc=True)
    MIXO = nc.dram_tensor("MIXO", [NOWN, D], F32, kind="ExternalOutput" if "MIXO" in g.debug else "Internal").ap()

    def load_rowmod(st, name, mode, idx, which):
        t = st.enter_context(nc.sbuf_tensor(name, [128, D], F32))
        b = Buf(name)
        return t, b

    def fill_rowmod(t, b, tmp, b_tmp, mode, idx, which):
        sc.dma("sp", t[:], S.modraw[mode, idx * D:(idx + 1) * D].partition_broadcast(128), b, reads=[g.dbuf["modraw"]], writes=[b])
        sc.dma("sp", tmp[:], I.grows[which, :].partition_broadcast(128), b_tmp, writes=[b_tmp])
        sc.op("dve", lambda e: e.tensor_tensor(out=t[:], in0=t[:], in1=tmp[:], op=ALU.mult), reads=[b, b_tmp], writes=[b])

    with ExitStack() as st0:
        h2T = st0.enter_context(nc.sbuf_tensor("m_h2T", [128, NKC, 512], BF16))
        b_h2T = Buf("m_h2T")
        ss = st0.enter_context(nc.sbuf_tensor("m_ss", [128, 1], F32))
        rstd = st0.enter_context(nc.sbuf_tensor("m_rstd", [128, 1], F32))
        b_scr = Buf("m_scr")
        modc = {}
        modc[0] = load_modcols(g, st0, 0, 2, 4, 3, "m_modp")
        modc[1] = load_modcols(g, st0, 1, 2, 4, 3, "m_mods")
        for grp in groups:
            t0g = grp * 512
            mode = 0 if t0g < NP_TOK else 1
            with ExitStack() as st:
                mixT = st.enter_context(nc.sbuf_tensor("m_mixT_%d" % grp, [128, NKC, 512], BF16))
                W = [st.enter_context(nc.sbuf_tensor("m_Wo%d_%d" % (i, grp), [128, NKC, 512], BF16)) for i in range(2)]
                stg = [st.enter_context(nc.sbuf_tensor("m_stg%d_%d" % (i, grp), [128, 512], F32)) for i in range(4)]
                GA = st.enter_context(nc.sbuf_tensor("m_GA_%d" % grp, [128, D], F32))
                mixt = st.enter_context(nc.sbuf_tensor("m_mix_%d" % grp, [128, D], F32))
                xt = st.enter_context(nc.sbuf_tensor("m_x_%d" % grp, [128, D], F32))
                junk = st.enter_context(nc.sbuf_tensor("m_junk_%d" % grp, [128, D], BF16))
                b_mixT, b_W, b_stg = Buf("m_mixT"), [Buf("m_Wo0"), Buf("m_Wo1")], [Buf("m_stg%d" % i) for i in range(4)]
                b_GA, b_mix, b_x, b_junk = Buf("m_GA"), Buf("m_mix"), Buf("m_x"), Buf("m_junk")
                fill_rowmod(GA, b_GA, mixt, b_mix, mode, 2, 1)
                sc.dma("sp", mixT[:], S.MIXT[:, :, t0g:t0g + 512].rearrange("k d t -> d k t"), b_mixT,
                       reads=[g.dbuf["MIXT"]], writes=[b_mixT])
                wv = I.w_out.rearrange("(kc p) n -> p kc n", p=128)
                k = 0
                for c in range(8):
                    wi = c % 2
                    sc.dma("pool", W[wi][:], wv[:, :, c * 512:(c + 1) * 512], b_W[wi], writes=[b_W[wi]])
                    for tt in range(4):
                        pb = k % 4
                        si = k % 4
                        k += 1

                        def mm(e, tt=tt, wi=wi, pb=pb):
                            ins = None
                            for kc in range(NKC):
                                ins = e.matmul(g.ps[pb][:, :], lhsT=mixT[:, kc, tt * 128:(tt + 1) * 128], rhs=W[wi][:, kc, :],
                                               start=(kc == 0), stop=(kc == NKC - 1))
                            return ins
                        sc.op("pe", mm, reads=[b_mixT, b_W[wi]], writes=[g.psb[pb]])
                        if k % 2 == 0:
                            sc.op("act", lambda e, si=si, pb=pb: e.copy(out=stg[si][:], in_=g.ps[pb][:, :]), reads=[g.psb[pb]], writes=[b_stg[si]])
                        else:
                            sc.op("dve", lambda e, si=si, pb=pb: e.tensor_copy(out=stg[si][:], in_=g.ps[pb][:, :]), reads=[g.psb[pb]], writes=[b_stg[si]])
                        sc.dma("sp", MIXO[t0g + tt * 128:t0g + (tt + 1) * 128, c * 512:(c + 1) * 512], stg[si][:], b_stg[si],
                               reads=[b_stg[si]], writes=[g.dbuf["MIXO"]])
                for tt in range(4):
                    t0 = t0g + tt * 128
                    sc.dma("sp", mixt[:], MIXO[t0:t0 + 128, :], b_mix, reads=[g.dbuf["MIXO"]], writes=[b_mix])
                    sc.dma("sp", xt[:], tok_src(g, t0), b_x, writes=[b_x])
                    sc.op("act", lambda e: e.activation(out=junk[:], in_=mixt[:], func=AF.Square, accum_out=ss[:]),
                          reads=[b_mix], writes=[b_scr, b_junk])
                    sc.op("act", lambda e: e.activation(out=rstd[:], in_=ss[:], func=AF.Sqrt, scale=1.0 / D, bias=EPS),
                          reads=[b_scr], writes=[b_scr])
                    sc.op("dve", lambda e: e.reciprocal(out=rstd[:], in_=rstd[:]), reads=[b_scr], writes=[b_scr])
                    sc.op("dve", lambda e: e.scalar_tensor_tensor(out=mixt[:], in0=mixt[:], scalar=rstd[:, 0:1], in1=GA[:],
                                                                  op0=ALU.mult, op1=ALU.mult), reads=[b_mix, b_scr, b_GA], writes=[b_mix])
                    sc.op("pool", lambda e: e.tensor_tensor(out=xt[:], in0=xt[:], in1=mixt[:], op=ALU.add), reads=[b_mix, b_x], writes=[b_x])
                    sc.dma("sp", S.X1[t0:t0 + 128, :], xt[:], b_x, reads=[b_x], writes=[g.dbuf["X1"]])
                    norm_transpose(g, xt, b_x, modc[mode][0], modc[mode][1], h2T, b_h2T, tt, (ss, rstd), b_scr, junk, b_junk)
            sc.barrier()
            with ExitStack() as st:
                macc = st.enter_context(nc.sbuf_tensor("m_acc_%d" % grp, [128, 4, D], F32))
                Wu = [st.enter_context(nc.sbuf_tensor("m_Wu%d_%d" % (i, grp), [128, NKC, 128], BF16)) for i in range(3)]
                Wd = [st.enter_context(nc.sbuf_tensor("m_Wd%d_%d" % (i, grp), [128, 8, 512], BF16)) for i in range(2)]
                uT = [st.enter_context(nc.sbuf_tensor("m_uT%d_%d" % (i, grp), [128, 8, 512], BF16)) for i in range(2)]
                r32 = [st.enter_context(nc.sbuf_tensor("m_r%d_%d" % (i, grp), [128, 512], F32)) for i in range(2)]
                GM = st.enter_context(nc.sbuf_tensor("m_GM_%d" % grp, [128, D], F32))
                xt = st.enter_context(nc.sbuf_tensor("m_x1_%d" % grp, [128, D], F32))
                junk = st.enter_context(nc.sbuf_tensor("m_junk2_%d" % grp, [128, D], BF16))
                b_macc = [[Buf("m_acc%d_%d" % (a, c)) for c in range(8)] for a in range(4)]
                b_Wu, b_Wd, b_uT = [Buf("m_Wu%d" % i) for i in range(3)], [Buf("m_Wd0"), Buf("m_Wd1")], [Buf("m_uT0"), Buf("m_uT1")]
                b_r, b_GM, b_x, b_junk = [Buf("m_r0"), Buf("m_r1")], Buf("m_GM"), Buf("m_x1"), Buf("m_junk2")
                fill_rowmod(GM, b_GM, xt, b_x, mode, 5, 3)
                wuv = I.w_up.rearrange("(kc p) n -> p kc n", p=128)
                wdv = I.w_down.rearrange("(fc p) n -> p fc n", p=128)
                cnt = dict(wu=0, wd=0, ps=0, r=0)
                for fb in range(nfb):
                    ui = fb % 2
                    for ch in range(8):
                        fc = fb * 8 + ch
                        wi = cnt["wu"] % 3
                        cnt["wu"] += 1
                        sc.dma("pool", Wu[wi][:], wuv[:, :, fc * 128:(fc + 1) * 128], b_Wu[wi], writes=[b_Wu[wi]])
                        pb = 4 + cnt["ps"] % 2

                        def mm(e, wi=wi, pb=pb):
                            ins = None
                            for kc in range(NKC):
                                ins = e.matmul(g.ps[pb][:, :], lhsT=Wu[wi][:, kc, :], rhs=h2T[:, kc, :],
                                               start=(kc == 0), stop=(kc == NKC - 1))
                            return ins
                        sc.op("pe", mm, reads=[b_Wu[wi], b_h2T], writes=[g.psb[pb]])
                        ri = cnt["r"] % 2
                        cnt["r"] += 1
                        cnt["ps"] += 1
                        sc.op("act", lambda e, ri=ri, pb=pb: e.activation(out=r32[ri][:], in_=g.ps[pb][:, :], func=AF.Relu),
                              reads=[g.psb[pb]], writes=[b_r[ri]])
                        sc.op("pool", lambda e, ri=ri, ui=ui, ch=ch: e.tensor_tensor(out=uT[ui][:, ch, :], in0=r32[ri][:], in1=r32[ri][:], op=ALU.mult),
                              reads=[b_r[ri]], writes=[b_uT[ui]])
                    for c in range(8):
                        di = cnt["wd"] % 2
                        cnt["wd"] += 1
                        sc.dma("pool", Wd[di][:], wdv[:, fb * 8:(fb + 1) * 8, c * 512:(c + 1) * 512], b_Wd[di], writes=[b_Wd[di]])
                        for tt in range(4):
                            pb = cnt["ps"] % 4
                            cnt["ps"] += 1

                            def mm2(e, tt=tt, di=di, pb=pb):
                                ins = None
                                for ch in range(8):
                                    ins = e.matmul(g.ps[pb][:, :], lhsT=uT[ui][:, ch, tt * 128:(tt + 1) * 128], rhs=Wd[di][:, ch, :],
                                                   start=(ch == 0), stop=(ch == 7))
                                return ins
                            sc.op("pe", mm2, reads=[b_uT[ui], b_Wd[di]], writes=[g.psb[pb]])
                            dst = macc[:, tt, c * 512:(c + 1) * 512]
                            if fb == 0:
                                sc.op("act", lambda e, dst=dst, pb=pb: e.copy(out=dst, in_=g.ps[pb][:, :]), reads=[g.psb[pb]], writes=[b_macc[tt][c]])
                            else:
                                sc.op("dve", lambda e, dst=dst, pb=pb: e.tensor_tensor(out=dst, in0=dst, in1=g.ps[pb][:, :], op=ALU.add),
                                      reads=[g.psb[pb], b_macc[tt][c]], writes=[b_macc[tt][c]])
                for tt in range(4):
                    t0 = t0g + tt * 128
                    mt = macc[:, tt, :]
                    sc.dma("sp", xt[:], S.X1[t0:t0 + 128, :], b_x, reads=[g.dbuf["X1"]], writes=[b_x])
                    sc.op("act", lambda e, mt=mt: e.activation(out=junk[:], in_=mt, func=AF.Square, accum_out=ss[:]),
                          reads=b_macc[tt], writes=[b_scr, b_junk])
                    sc.op("act", lambda e: e.activation(out=rstd[:], in_=ss[:], func=AF.Sqrt, scale=1.0 / D, bias=EPS),
                          reads=[b_scr], writes=[b_scr])
                    sc.op("dve", lambda e: e.reciprocal(out=rstd[:], in_=rstd[:]), reads=[b_scr], writes=[b_scr])
                    sc.op("dve", lambda e, mt=mt: e.scalar_tensor_tensor(out=mt, in0=mt, scalar=rstd[:, 0:1], in1=GM[:],
                                                                         op0=ALU.mult, op1=ALU.mult),
                          reads=b_macc[tt] + [b_scr, b_GM], writes=b_macc[tt])
                    sc.op("pool", lambda e, mt=mt: e.tensor_tensor(out=xt[:], in0=xt[:], in1=mt, op=ALU.add), reads=b_macc[tt] + [b_x], writes=[b_x])
                    dst = O.yp[t0:t0 + 128, :] if t0 < NP_TOK else O.ys[t0 - NP_TOK:t0 - NP_TOK + 128, :]
                    sc.dma("sp", dst, xt[:], b_x, reads=[b_x], writes=[g.dbuf["out"]])
            sc.barrier()
```

```python
import math
from contextlib import ExitStack
import numpy as np
import concourse.bass as bass
import concourse.mybir as mybir
from concourse.bass_utils import run_bass_kernel_spmd

F32 = mybir.dt.float32
BF16 = mybir.dt.bfloat16
AF = mybir.ActivationFunctionType
ALU = mybir.AluOpType
AX = mybir.AxisListType

D = 4096
NKC = 32
IN_COLS = 11328
NP_TOK = 1024
NS_OWN = 2048
NS_ALL = 4096
PAST = 512
NT = NP_TOK + NS_ALL
NOWN = NP_TOK + NS_OWN
DFF = 16384
EPS = 1e-6
LAM_INIT = 0.8 - 0.6 * math.exp(-0.3 * 0)


class Buf:
    __slots__ = ("name", "acc", "w", "r", "dsem", "dkey", "dcnt")

    def __init__(self, name, acc=False):
        self.name = name
        self.acc = acc
        self.w = {}
        self.r = {}
        self.dsem = None
        self.dkey = None
        self.dcnt = 0


class Sched:
    def __init__(self, nc, stack):
        self.nc = nc
        self.stack = stack
        self.engs = dict(pe=nc.tensor, act=nc.scalar, dve=nc.vector, pool=nc.gpsimd, sp=nc.sync)
        self.esem = {}
        self.ecnt = {}
        self.waited = {}
        for k in self.engs:
            self.esem[k] = stack.enter_context(nc.semaphore("es_" + k))
            self.ecnt[k] = 0
            self.waited[k] = {}
        self.dsems = []
        self.free_sems = []
        self.nsem = 0
        self.nops = 0

    def _deps(self, ek, reads, writes, is_dma):
        deps = {}
        own = None if is_dma else "es_" + ek
        for b in reads:
            for k, sv in b.w.items():
                if k not in deps or deps[k][1] < sv[1]:
                    deps[k] = sv
        for b in writes:
            if b.acc:
                continue
            for dd in (b.w, b.r):
                for k, sv in dd.items():
                    if k == own:
                        continue
                    if k not in deps or deps[k][1] < sv[1]:
                        deps[k] = sv
        return deps

    def _wait(self, ek, deps):
        e = self.engs[ek]
        wd = self.waited[ek]
        for k, (sem, val) in deps.items():
            if wd.get(k, 0) < val:
                e.wait_ge(sem, val)
                wd[k] = val

    def _record(self, k, sv, reads, writes):
        for b in writes:
            if b.acc:
                if k not in b.w or b.w[k][1] < sv[1]:
                    b.w[k] = sv
            else:
                b.w = {k: sv}
                b.r = {}
        for b in reads:
            if k not in b.r or b.r[k][1] < sv[1]:
                b.r[k] = sv

    def op(self, ek, fn, reads=(), writes=()):
        deps = self._deps(ek, reads, writes, False)
        self._wait(ek, deps)
        ins = fn(self.engs[ek])
        self.ecnt[ek] += 1
        ins.then_inc(self.esem[ek], 1)
        self._record("es_" + ek, (self.esem[ek], self.ecnt[ek]), reads, writes)
        self.nops += 1

    def dma(self, qk, out, in_, sb, reads=(), writes=()):
        deps = self._deps(qk, reads, writes, True)
        if sb.dsem is None:
            if self.free_sems:
                sb.dkey, sb.dsem, sb.dcnt = self.free_sems.pop()
            else:
                sb.dkey = "ds%d" % self.nsem
                self.nsem += 1
                sb.dsem = self.stack.enter_context(self.nc.semaphore(sb.dkey))
                sb.dcnt = 0
            self.dsems.append(sb)
        if sb.dcnt > 0:
            k = sb.dkey
            if k not in deps or deps[k][1] < sb.dcnt:
                deps[k] = (sb.dsem, sb.dcnt)
        self._wait(qk, deps)
        self.engs[qk].dma_start(out=out, in_=in_).then_inc(sb.dsem, 16)
        sb.dcnt += 16
        self._record(sb.dkey, (sb.dsem, sb.dcnt), reads, writes)
        self.nops += 1

    def barrier(self):
        deps = {}
        for k in self.engs:
            if self.ecnt[k] > 0:
                deps["es_" + k] = (self.esem[k], self.ecnt[k])
        for sb in self.dsems:
            if sb.dcnt > 0:
                deps[sb.dkey] = (sb.dsem, sb.dcnt)
        for k in self.engs:
            d2 = {kk: v for kk, v in deps.items() if kk != "es_" + k}
            self._wait(k, d2)
        for sb in self.dsems:
            self.free_sems.append((sb.dkey, sb.dsem, sb.dcnt))
            sb.dsem = None
        self.dsems = []

    def finish(self):
        deps = {}
        for k in self.engs:
            if self.ecnt[k] > 0 and k != "sp":
                deps["es_" + k] = (self.esem[k], self.ecnt[k])
        for sb in self.dsems:
            if sb.dcnt > 0:
                deps[sb.dkey] = (sb.dsem, sb.dcnt)
        self._wait("sp", deps)


class Ctx:
    pass


def build(debug=None, cfg=None):
    cfg = cfg or {}
    nc = bass.Bass("TRN2", target_bir_lowering=False)
    g = Ctx()
    g.nc = nc
    g.cfg = cfg
    g.debug = debug or ()

    def din(name, shape, dt=F32):
        return nc.dram_tensor(name, list(shape), dt, kind="ExternalInput").ap()

    def dout(name, shape, dt=F32):
        return nc.dram_tensor(name, list(shape), dt, kind="ExternalOutput").ap()

    def dscr(name, shape, dt=F32):
        kind = "ExternalOutput" if name in g.debug else "Internal"
        return nc.dram_tensor(name, list(shape), dt, kind=kind).ap()

    IN_SHAPES = dict(xp=[NP_TOK, D], xs=[NS_ALL, D], cT=[128, NKC * 2], ck=[PAST, 2048], cv=[PAST, 2048],
                     sf=[32 * 64, 128], sb=[32 * 64, 128], w_ada=[D, 6 * D], b_ada=[1, 6 * D], gcols=[128, 4 * NKC],
                     grows=[4, D], w_in=[D, IN_COLS], lam=[1, 4 * 128], g_subln=[1, 256], conv_w=[5, 3072],
                     conv_b=[1, 3072], a_log=[1, 64], dt_bias=[1, 64], d_skip=[1, 32], g_ssm=[1, 2048],
                     w_out=[D, D], w_up=[D, DFF], w_down=[DFF, D], ropec=[NS_ALL, 128], ropes=[NS_ALL, 128],
                     ident=[128, 128])

    class LazyIn:
        def __getattr__(self, name):
            ap = din(name, IN_SHAPES[name])
            object.__setattr__(self, name, ap)
            g.used_inputs.append(name)
            return ap
    g.used_inputs = []
    I = LazyIn()
    g.I = I
    if not cfg.get("lazy_inputs", False):
        for n_ in IN_SHAPES:
            getattr(I, n_)

    O = Ctx()
    g.O = O
    O.yp = dout("o_yp", [NP_TOK, D])
    O.ys = dout("o_ys", [NS_OWN, D])
    O.nk = dout("o_nk", [NP_TOK, 2048])
    O.nv = dout("o_nv", [NP_TOK, 2048])
    O.nsf = dout("o_nsf", [4 * 32 * 64, 128])
    O.nsb = dout("o_nsb", [4 * 32 * 64, 128])

    S = Ctx()
    g.S = S
    S.modraw = dscr("modraw", [2, 6 * D])
    S.QT = dscr("QT", [16, 128, NOWN], BF16)
    S.KT = dscr("KT", [16, 128, NT + PAST], BF16)
    S.V = dscr("V", [NT + PAST, 2048], BF16)
    S.Z = dscr("Zs", [NOWN, 2048 + g.cfg.get("zpad", 0)])
    S.XBC = dscr("XBC", [NT, 4 * 768])
    S.DT = dscr("DT", [NT, 64])
    S.MIXT = dscr("MIXT", [NKC, 128, NOWN], BF16)
    S.X1 = dscr("X1", [NOWN, D])

    with ExitStack() as stack:
        sc = Sched(nc, stack)
        g.sc = sc
        g.stack = stack
        g.ps = []
        g.psb = []
        for i in range(8):
            t = stack.enter_context(nc.psum_tensor("ps%d" % i, [128, 512], F32))
            g.ps.append(t)
            g.psb.append(Buf("ps%d" % i))
        g.ident_f = stack.enter_context(nc.sbuf_tensor("ident_f", [128, 128], F32))
        g.ident_b = stack.enter_context(nc.sbuf_tensor("ident_b", [128, 128], BF16))
        g.b_ident = Buf("ident")
        sc.dma("sp", g.ident_f[:], I.ident[:, :], g.b_ident, writes=[g.b_ident])
        sc.op("dve", lambda e: e.tensor_copy(g.ident_b[:], g.ident_f[:]), reads=[g.b_ident], writes=[g.b_ident])
        g.dbuf = {n: Buf("d_" + n, acc=True) for n in
                  ("modraw", "QT", "KT", "V", "Z", "XBC", "DT", "MIXT", "X1", "out")}

        S.WOB = dscr("WOB", [D, D], BF16)
        S.WUB = dscr("WUB", [D, DFF], BF16)
        S.WDB = dscr("WDB", [DFF, D], BF16)
        g.dbuf["WB"] = Buf("d_WB", acc=True)
        g.conv_list = []
        g.conv_bufs = [Buf("cv%d" % i) for i in range(6)]
        g.conv_i = [0]

        def conv_step(n=1):
            if not g.conv_list:
                for r in range(0, D, 512):
                    g.conv_list.append((S.WOB[r:r + 512, :], I.w_out[r:r + 512, :]))
                for r in range(0, D, 128):
                    g.conv_list.append((S.WUB[r:r + 128, :], I.w_up[r:r + 128, :]))
                for r in range(0, DFF, 512):
                    g.conv_list.append((S.WDB[r:r + 512, :], I.w_down[r:r + 512, :]))
            for _ in range(n):
                i = g.conv_i[0]
                if i >= len(g.conv_list):
                    return
                g.conv_i[0] += 1
                o_, i_ = g.conv_list[i]
                sc.dma("pool", o_, i_, g.conv_bufs[i % len(g.conv_bufs)], writes=[g.dbuf["WB"]])
        g.conv_step = conv_step

        stages = cfg.get("stages", "0123456")
        if "0" in stages:
            stage_adaln(g)
            sc.barrier()
        if "1" in stages:
            stage_inproj(g)
            sc.barrier()
        if "2" in stages:
            stage_ctxkv(g)
            sc.barrier()
        if "3" in stages:
            stage_ssd(g)
            sc.barrier()
        if "4" in stages:
            stage_attn(g)
            sc.barrier()
        if "5" in stages:
            g.conv_step(1000)
            sc.barrier()
            stage_mlp(g)
        sc.finish()
    nc.used_inputs = g.used_inputs
    return nc


def stage_adaln(g):
    nc, sc, I, S = g.nc, g.sc, g.I, g.S
    ncol = g.cfg.get("ada_tiles", 48)
    with ExitStack() as st:
        cT = st.enter_context(nc.sbuf_tensor("a_cT", [128, NKC * 2], F32))
        sg = st.enter_context(nc.sbuf_tensor("a_sg", [128, NKC * 2], F32))
        L = st.enter_context(nc.sbuf_tensor("a_L", [128, NKC * 2, 128], BF16))
        brow = st.enter_context(nc.sbuf_tensor("a_brow", [1, 6 * D], F32))
        W = [st.enter_context(nc.sbuf_tensor("a_W%d" % i, [128, NKC, 512], BF16)) for i in range(2)]
        ot = [st.enter_context(nc.sbuf_tensor("a_ot%d" % i, [1, 512], F32)) for i in range(4)]
        b_c, b_L, b_brow = Buf("a_c"), Buf("a_L"), Buf("a_brow")
        b_W = [Buf("a_W0"), Buf("a_W1")]
        b_ot = [Buf("a_ot%d" % i) for i in range(4)]
        sc.dma("sp", cT[:], I.cT[:, :], b_c, writes=[b_c])
        sc.dma("sp", brow[:], I.b_ada[:, :], b_brow, writes=[b_brow])
        sc.op("act", lambda e: e.activation(out=sg[:], in_=cT[:], func=AF.Sigmoid), reads=[b_c], writes=[b_L])
        sc.op("dve", lambda e: e.tensor_tensor(out=sg[:], in0=sg[:], in1=cT[:], op=ALU.mult), reads=[b_c, b_L], writes=[b_L])
        sc.op("dve", lambda e: e.tensor_copy(out=L[:], in_=sg[:].unsqueeze(2).to_broadcast([128, NKC * 2, 128])),
              reads=[b_L], writes=[b_L])
        wv = I.w_ada.rearrange("(kc p) n -> p kc n", p=128)
        k = 0
        for n in range(ncol):
            wi = n % 2
            sc.dma("pool", W[wi][:], wv[:, :, n * 512:(n + 1) * 512], b_W[wi], writes=[b_W[wi]])
            for m in range(2):
                pb = 4 + (k % 2)
                oi = k % 4
                k += 1

                def mm(e, m=m, wi=wi, pb=pb):
                    ins = None
                    for kc in range(NKC):
                        ins = e.matmul(g.ps[pb][:, :], lhsT=L[:, kc * 2 + m, :], rhs=W[wi][:, kc, :],
                                       start=(kc == 0), stop=(kc == NKC - 1))
                    return ins
                sc.op("pe", mm, reads=[b_L, b_W[wi]], writes=[g.psb[pb]])
                sc.op("dve", lambda e, pb=pb, oi=oi, n=n: e.tensor_tensor(
                    out=ot[oi][:], in0=g.ps[pb][0:1, :], in1=brow[0:1, n * 512:(n + 1) * 512], op=ALU.add),
                    reads=[g.psb[pb], b_brow], writes=[b_ot[oi]])
                sc.dma("sp", S.modraw[m:m + 1, n * 512:(n + 1) * 512], ot[oi][:], b_ot[oi],
                       reads=[b_ot[oi]], writes=[g.dbuf["modraw"]])


def load_modcols(g, st, mode, which_g, i_sc, i_sh, name):
    nc, sc, I, S = g.nc, g.sc, g.I, g.S
    t = st.enter_context(nc.sbuf_tensor(name, [128, 3, NKC], F32))
    rr = st.enter_context(nc.sbuf_tensor(name + "_r", [32, 2, 128], F32))
    b = Buf(name)
    b_rr = Buf(name + "_r")
    sc.dma("sp", rr[:, 0, :], S.modraw[mode, i_sc * D:(i_sc + 1) * D].rearrange("(kc p) -> kc p", p=128), b_rr,
           reads=[g.dbuf["modraw"]], writes=[b_rr])
    sc.dma("sp", rr[:, 1, :], S.modraw[mode, i_sh * D:(i_sh + 1) * D].rearrange("(kc p) -> kc p", p=128), b_rr,
           reads=[g.dbuf["modraw"]], writes=[b_rr])

    def tr(e):
        e.transpose(g.ps[7][:, 0:32], rr[:, 0, :], g.ident_f[0:32, 0:32])
        return e.transpose(g.ps[7][:, 32:64], rr[:, 1, :], g.ident_f[0:32, 0:32])
    sc.op("pe", tr, reads=[b_rr, g.b_ident], writes=[g.psb[7]])
    sc.op("dve", lambda e: e.tensor_copy(out=t[:, 0:2, :].rearrange("p a k -> p (a k)"), in_=g.ps[7][:, 0:64]),
          reads=[g.psb[7]], writes=[b])
    sc.dma("sp", t[:, 2, :], I.gcols[:, which_g * NKC:(which_g + 1) * NKC], b, writes=[b])
    sc.op("dve", lambda e: e.scalar_tensor_tensor(out=t[:, 0, :], in0=t[:, 0, :], scalar=1.0, in1=t[:, 2, :],
                                                  op0=ALU.add, op1=ALU.mult), reads=[b], writes=[b])
    return t, b


def norm_transpose(g, xt, b_x, modc, b_modc, hT, b_hT, tt, scratch, b_scr, junk, b_junk):
    sc = g.sc
    ss, rstd = scratch
    sc.op("act", lambda e: e.activation(out=junk[:], in_=xt[:], func=AF.Square, accum_out=ss[:]),
          reads=[b_x], writes=[b_scr, b_junk])
    sc.op("act", lambda e: e.activation(out=rstd[:], in_=ss[:], func=AF.Sqrt, scale=1.0 / D, bias=EPS),
          reads=[b_scr], writes=[b_scr])
    sc.op("dve", lambda e: e.reciprocal(out=rstd[:], in_=rstd[:]), reads=[b_scr], writes=[b_scr])
    sc.op("dve", lambda e: e.tensor_scalar(out=xt[:], in0=xt[:], scalar1=rstd[:, 0:1], scalar2=None, op0=ALU.mult),
          reads=[b_x, b_scr], writes=[b_x])
    for q in range(8):
        pb = 4 + (q % 2)

        def tr(e, q=q, pb=pb):
            ins = None
            for i in range(4):
                kc = q * 4 + i
                ins = e.transpose(g.ps[pb][:, i * 128:(i + 1) * 128], xt[:, kc * 128:(kc + 1) * 128], g.ident_f[:])
            return ins
        sc.op("pe", tr, reads=[b_x, g.b_ident], writes=[g.psb[pb]])
        for i in range(4):
            kc = q * 4 + i
            ek = "act" if (i % 2 == 0) else "dve"
            if ek == "act":
                sc.op("act", lambda e, kc=kc, i=i, pb=pb: e.activation(
                    out=hT[:, kc, tt * 128:(tt + 1) * 128], in_=g.ps[pb][:, i * 128:(i + 1) * 128],
                    func=AF.Identity, scale=modc[:, 0, kc:kc + 1], bias=modc[:, 1, kc:kc + 1]),
                    reads=[g.psb[pb], b_modc], writes=[b_hT])
            else:
                sc.op("dve", lambda e, kc=kc, i=i, pb=pb: e.tensor_scalar(
                    out=hT[:, kc, tt * 128:(tt + 1) * 128], in0=g.ps[pb][:, i * 128:(i + 1) * 128],
                    scalar1=modc[:, 0, kc:kc + 1], scalar2=modc[:, 1, kc:kc + 1], op0=ALU.mult, op1=ALU.add),
                    reads=[g.psb[pb], b_modc], writes=[b_hT])


def tok_src(g, t0):
    if t0 < NP_TOK:
        return g.I.xp[t0:t0 + 128, :]
    return g.I.xs[t0 - NP_TOK:t0 - NP_TOK + 128, :]


def stage_inproj(g):
    nc, sc, I, S, O = g.nc, g.sc, g.I, g.S, g.O
    GT = 1024
    NTT = GT // 128
    ngroups = g.cfg.get("in_groups", list(range(NT // GT)))
    with ExitStack() as st:
        xt = st.enter_context(nc.sbuf_tensor("i_x", [128, D], F32))
        junk = st.enter_context(nc.sbuf_tensor("i_junk", [128, D], BF16))
        hT = st.enter_context(nc.sbuf_tensor("i_hT", [128, NKC, GT], BF16))
        W = [st.enter_context(nc.sbuf_tensor("i_W%d" % i, [128, NKC, 512], BF16)) for i in range(2)]
        ss = st.enter_context(nc.sbuf_tensor("i_ss", [128, 1], F32))
        rstd = st.enter_context(nc.sbuf_tensor("i_rstd", [128, 1], F32))
        NF = 4
        sf32 = [st.enter_context(nc.sbuf_tensor("i_sf%d" % i, [128, 512], F32)) for i in range(NF)]
        sb16 = [st.enter_context(nc.sbuf_tensor("i_sb%d" % i, [128, 512], BF16)) for i in range(NF)]
        rt1 = st.enter_context(nc.sbuf_tensor("i_rt1", [128, 512], F32))
        rt2 = st.enter_context(nc.sbuf_tensor("i_rt2", [128, 512], F32))
        qts = [st.enter_context(nc.sbuf_tensor("i_qts%d" % i, [128, 4, GT], BF16)) for i in range(1)]
        rc = st.enter_context(nc.sbuf_tensor("i_rc", [128, NTT, 128], F32))
        rs = st.enter_context(nc.sbuf_tensor("i_rs", [128, NTT, 128], F32))
        b_x, b_junk, b_hT, b_scr = Buf("i_x"), Buf("i_junk"), Buf("i_hT"), Buf("i_scr")
        b_W = [Buf("i_W0"), Buf("i_W1")]
        b_sf = [Buf("i_sf%d" % i) for i in range(NF)]
        b_sb = [Buf("i_sb%d" % i) for i in range(NF)]
        b_rt, b_rope = Buf("i_rt"), Buf("i_rope")
        b_qts = [Buf("i_qts0")]
        modc = {}
        modc[0] = load_modcols(g, st, 0, 0, 1, 0, "i_modp")
        modc[1] = load_modcols(g, st, 1, 0, 1, 0, "i_mods")
        wv = I.w_in.rearrange("(kc p) n -> p kc n", p=128)
        cnt = dict(w=0, ps=0, sf=0, sb=0, q=0)
        for grp in ngroups:
            t0g = grp * GT
            prompt = t0g < NP_TOK
            remote = t0g >= NOWN
            mode = 0 if prompt else 1
            for tt in range(NTT):
                sc.dma("sp", xt[:], tok_src(g, t0g + tt * 128), b_x, writes=[b_x])
                norm_transpose(g, xt, b_x, modc[mode][0], modc[mode][1], hT, b_hT, tt, (ss, rstd), b_scr, junk, b_junk)
            if not prompt:
                ls = t0g - NP_TOK
                sc.dma("sp", rc[:], I.ropec[ls:ls + GT, :].rearrange("(t p) d -> p t d", p=128), b_rope, writes=[b_rope])
                sc.dma("sp", rs[:], I.ropes[ls:ls + GT, :].rearrange("(t p) d -> p t d", p=128), b_rope, writes=[b_rope])
            tiles = []
            if not remote:
                tiles += [("q", c * 512, 512, c) for c in range(4)]
            tiles += [("k", 2048 + c * 512, 512, c) for c in range(4)]
            tiles += [("v", 4096 + c * 512, 512, c) for c in range(4)]
            if not remote:
                tiles += [("z", 6144 + c * 512, 512, c) for c in range(4)]
            tiles += [("x", 8192 + c * 512, 512, c) for c in range(4)]
            tiles += [("B", 8192 + 2048, 512, 0)]
            tiles += [("C", 8192 + 2560, 512, 0)]
            tiles += [("dt", 11264, 64, 0)]
            kinds_ok = g.cfg.get('in_kinds')
            for (kind, c0, ncols, ci) in tiles:
                if kinds_ok is not None and kind not in kinds_ok:
                    continue
                wi = cnt["w"] % 2
                cnt["w"] += 1
                sc.dma("pool", W[wi][:, :, 0:ncols], wv[:, :, c0:c0 + ncols], b_W[wi], writes=[b_W[wi]])
                qi = None
                if kind in ("q", "k"):
                    qi = 0
                    cnt["q"] += 1
                for tt in range(NTT):
                    t0 = t0g + tt * 128
                    pb = cnt["ps"] % 4
                    cnt["ps"] += 1

                    def mm(e, tt=tt, wi=wi, pb=pb, ncols=ncols):
                        ins = None
                        for kc in range(NKC):
                            ins = e.matmul(g.ps[pb][:, 0:ncols], lhsT=hT[:, kc, tt * 128:(tt + 1) * 128],
                                           rhs=W[wi][:, kc, 0:ncols], start=(kc == 0), stop=(kc == NKC - 1))
                        return ins
                    sc.op("pe", mm, reads=[b_hT, b_W[wi]], writes=[g.psb[pb]])
                    P = g.ps[pb]
                    bP = g.psb[pb]
                    if kind in ("q", "k"):
                        bi = cnt["sb"] % NF
                        cnt["sb"] += 1
                        xb = sb16[bi]
                        if prompt:
                            if kind == "k":
                                fi = cnt["sf"] % NF
                                cnt["sf"] += 1
                                sc.op("act", lambda e, fi=fi, P=P: e.copy(out=sf32[fi][:], in_=P[:, :]),
                                      reads=[bP], writes=[b_sf[fi]])
                                sc.dma("sp", O.nk[t0:t0 + 128, ci * 512:(ci + 1) * 512], sf32[fi][:], b_sf[fi],
                                       reads=[b_sf[fi]], writes=[g.dbuf["out"]])
                                sc.op("dve", lambda e, xb=xb, fi=fi: e.tensor_copy(out=xb[:], in_=sf32[fi][:]),
                                      reads=[b_sf[fi]], writes=[b_sb[bi]])
                            else:
                                sc.op("dve", lambda e, xb=xb, P=P: e.tensor_copy(out=xb[:], in_=P[:, :]),
                                      reads=[bP], writes=[b_sb[bi]])
                        else:
                            Cb = rc[:, tt, :].unsqueeze(1).to_broadcast([128, 4, 128])
                            sc.op("dve", lambda e, P=P, Cb=Cb: e.tensor_tensor(
                                out=rt1[:].rearrange("p (b d) -> p b d", b=4), in0=P[:, :].rearrange("p (b d) -> p b d", b=4),
                                in1=Cb, op=ALU.mult), reads=[bP, b_rope], writes=[b_rt])
                            Pv = P[:, :].rearrange("p (b a h d) -> p b a h d", b=4, a=2, h=2)
                            t2v = rt2[:].rearrange("p (b a h d) -> p b a h d", b=4, a=2, h=2)
                            Sv = rs[:, tt, :].rearrange("p (a h d) -> p a h d", a=2, h=2)
                            for hh in range(2):
                                sc.op("dve", lambda e, hh=hh, Pv=Pv, t2v=t2v, Sv=Sv: e.tensor_tensor(
                                    out=t2v[:, :, :, hh, :], in0=Pv[:, :, :, 1 - hh, :],
                                    in1=Sv[:, :, hh, :].unsqueeze(1).to_broadcast([128, 4, 2, 32]), op=ALU.mult),
                                    reads=[bP, b_rope, b_rt], writes=[b_rt])
                            sc.op("pool", lambda e, xb=xb: e.tensor_tensor(out=xb[:], in0=rt1[:], in1=rt2[:], op=ALU.add),
                                  reads=[b_rt], writes=[b_sb[bi]])
                        tb = 6 + (cnt["ps"] % 2)
                        TP = g.ps[tb][:, :].bitcast(BF16)

                        def tr(e, xb=xb, TP=TP):
                            ins = None
                            for bk in range(4):
                                ins = e.transpose(TP[:, bk * 128:(bk + 1) * 128], xb[:, bk * 128:(bk + 1) * 128], g.ident_b[:])
                            return ins
                        sc.op("pe", tr, reads=[b_sb[bi], g.b_ident], writes=[g.psb[tb]])
                        sc.op("act", lambda e, TP=TP, qi=qi, tt=tt: e.copy(
                            out=qts[qi][:, :, tt * 128:(tt + 1) * 128],
                            in_=TP[:, 0:512].rearrange("p (b t) -> p b t", b=4)),
                            reads=[g.psb[tb]], writes=[b_qts[qi]])
                        if tt == NTT - 1:
                            dst = S.QT if kind == "q" else S.KT
                            sc.dma("sp", dst[ci * 4:(ci + 1) * 4, :, t0g:t0g + GT].rearrange("b d t -> d b t"),
                                   qts[qi][:], b_qts[qi], reads=[b_qts[qi]], writes=[g.dbuf["QT" if kind == "q" else "KT"]])
                    elif kind == "v":
                        if prompt and not g.cfg.get("skip_nv"):
                            fi = cnt["sf"] % NF
                            cnt["sf"] += 1
                            sc.op("act", lambda e, fi=fi, P=P: e.copy(out=sf32[fi][:], in_=P[:, :]),
                                  reads=[bP], writes=[b_sf[fi]])
                            sc.dma("sp", O.nv[t0:t0 + 128, ci * 512:(ci + 1) * 512], sf32[fi][:], b_sf[fi],
                                   reads=[b_sf[fi]], writes=[g.dbuf["out"]])
                        bi = cnt["sb"] % NF
                        cnt["sb"] += 1
                        if prompt and not g.cfg.get("skip_nv"):
                            sc.op("dve", lambda e, bi=bi, fi=fi: e.tensor_copy(out=sb16[bi][:], in_=sf32[fi][:]),
                                  reads=[b_sf[fi]], writes=[b_sb[bi]])
                        else:
                            sc.op("dve", lambda e, bi=bi, P=P: e.tensor_copy(out=sb16[bi][:], in_=P[:, :]),
                                  reads=[bP], writes=[b_sb[bi]])
                        if not g.cfg.get("skip_V"):
                            sc.dma("sp", S.V[t0:t0 + 128, ci * 512:(ci + 1) * 512], sb16[bi][:], b_sb[bi],
                                   reads=[b_sb[bi]], writes=[g.dbuf["V"]])
                    else:
                        fi = cnt["sf"] % NF
                        cnt["sf"] += 1
                        ek = "act" if (cnt["sf"] % 2 == 0) else "dve"
                        if ek == "act":
                            sc.op("act", lambda e, fi=fi, P=P, ncols=ncols: e.copy(out=sf32[fi][:, 0:ncols], in_=P[:, 0:ncols]),
                                  reads=[bP], writes=[b_sf[fi]])
                        else:
                            sc.op("dve", lambda e, fi=fi, P=P, ncols=ncols: e.tensor_copy(out=sf32[fi][:, 0:ncols], in_=P[:, 0:ncols]),
                                  reads=[bP], writes=[b_sf[fi]])
                        if kind == "z":
                            dst = S.Z[t0:t0 + 128, ci * 512:(ci + 1) * 512]
                            src = sf32[fi][:]
                            dn = "Z"
                        elif kind == "x":
                            dst = S.XBC[t0:t0 + 128, ci * 768:ci * 768 + 512]
                            src = sf32[fi][:]
                            dn = "XBC"
                        elif kind in ("B", "C"):
                            off = 512 if kind == "B" else 640
                            dst = S.XBC[t0:t0 + 128, :].rearrange("t (g c) -> t g c", g=4)[:, :, off:off + 128]
                            src = sf32[fi][:].rearrange("p (g c) -> p g c", g=4)
                            dn = "XBC"
                        else:
                            dst = S.DT[t0:t0 + 128, :]
                            src = sf32[fi][:, 0:64]
                            dn = "DT"
                        sc.dma(g.cfg.get("zq", "sp"), dst, src, b_sf[fi], reads=[b_sf[fi]], writes=[g.dbuf[dn]])


def _rope_tables(pos):
    pos = np.asarray(pos)
    row = (pos // 64).astype(np.float32)
    col = (pos % 64).astype(np.float32)
    inv = (1.0 / (np.float32(10000.0) ** (np.arange(0, 64, 2, dtype=np.float32) / np.float32(64)))).astype(np.float32)
    ar = row[:, None] * inv[None, :]
    ac = col[:, None] * inv[None, :]
    cr, sr, cc, s_c = np.cos(ar), np.sin(ar), np.cos(ac), np.sin(ac)
    C = np.concatenate([cr, cr, cc, cc], axis=1).astype(np.float32)
    Ssg = np.concatenate([-sr, sr, -s_c, s_c], axis=1).astype(np.float32)
    return np.ascontiguousarray(C), np.ascontiguousarray(Ssg)


def prep_inputs(inp):
    f = lambda a: np.ascontiguousarray(np.asarray(a, dtype=np.float32))
    x_prompt, x_sample = f(inp["x_prompt"]), f(inp["x_sample"])
    w_in = f(inp["w_in"])[0]
    w_in_odd = w_in.copy()
    w_in_odd[:, 11264:11296] = w_in[:, 11296:11328]
    w_in_odd[:, 11296:11328] = w_in[:, 11264:11296]
    shared = dict(
        w_ada=f(inp["w_ada"])[0], b_ada=f(inp["b_ada"]).reshape(1, -1),
        w_out=f(inp["w_out"])[0], w_up=f(inp["w_up"])[0], w_down=f(inp["w_down"])[0],
        lam=np.concatenate([f(inp[k]).reshape(-1) for k in ("lambda_q1", "lambda_k1", "lambda_q2", "lambda_k2")]).reshape(1, -1),
        g_subln=f(inp["g_subln"]).reshape(1, -1), conv_b=f(inp["conv_b"]).reshape(1, -1),
        d_skip=f(inp["d_skip"]).reshape(1, -1), g_ssm=f(inp["g_ssm_norm"]).reshape(1, -1),
        ident=np.eye(128, dtype=np.float32),
    )
    grows = np.stack([f(inp[k])[0] for k in ("g_mix_pre", "g_mix_post", "g_mlp_pre", "g_mlp_post")])
    shared["grows"] = np.ascontiguousarray(grows)
    shared["gcols"] = np.ascontiguousarray(grows.reshape(4, NKC, 128).transpose(2, 0, 1).reshape(128, 4 * NKC))
    conv_w = f(inp["conv_w"])[0]
    a_log, dt_bias = f(inp["a_log"])[0], f(inp["dt_bias"])[0]
    maps = []
    for i in range(8):
        b, hf = i // 2, i % 2
        rev = hf == 1
        m = dict(shared)
        xp = x_prompt[4 * i:4 * i + 4]
        xs = x_sample[b]
        pos = np.arange(NS_ALL)
        if rev:
            xp = xp[:, ::-1]
            xs = xs[::-1]
            pos = pos[::-1]
        m["xp"] = np.ascontiguousarray(xp.reshape(NP_TOK, D))
        m["xs"] = np.ascontiguousarray(xs)
        cc = np.stack([f(inp["c_ctx"]), f(inp["c"])[b]])
        m["cT"] = np.ascontiguousarray(cc.reshape(2, NKC, 128).transpose(2, 1, 0).reshape(128, NKC * 2))
        m["ck"] = np.ascontiguousarray(f(inp["cache_k"])[b, 0].reshape(PAST, 2048))
        m["cv"] = np.ascontiguousarray(f(inp["cache_v"])[b, 0].reshape(PAST, 2048))
        sfw = f(inp["state_ssm_fwd"])[b, 0].reshape(32 * 64, 128)
        sbw = f(inp["state_ssm_bwd"])[b, 0].reshape(32 * 64, 128)
        m["sf"], m["sb"] = (sbw, sfw) if rev else (sfw, sbw)
        m["w_in"] = w_in_odd if rev else w_in
        m["conv_w"] = np.ascontiguousarray(conv_w[::-1]) if rev else conv_w
        m["a_log"] = np.ascontiguousarray((a_log[::-1] if rev else a_log).reshape(1, 64))
        m["dt_bias"] = np.ascontiguousarray((dt_bias[::-1] if rev else dt_bias).reshape(1, 64))
        m["ropec"], m["ropes"] = _rope_tables(pos)
        maps.append(m)
    return maps


_NC_CACHE = {}


def kernel(**inputs):
    maps = prep_inputs(inputs)
    if "nc" not in _NC_CACHE:
        _NC_CACHE["nc"] = build()
    nc = _NC_CACHE["nc"]
    res = run_bass_kernel_spmd(nc, maps, core_ids=list(range(8)))
    R = res.results
    yp = np.zeros((32, 256, D), np.float32)
    ys = np.zeros((4, NS_ALL, D), np.float32)
    nk = np.zeros((32, 1, 256, 8, 2, 128), np.float32)
    nv = np.zeros((32, 1, 256, 8, 256), np.float32)
    nsf = np.zeros((32, 1, 32, 64, 128), np.float32)
    nsb = np.zeros((32, 1, 32, 64, 128), np.float32)
    for i in range(8):
        b, hf = i // 2, i % 2
        r = R[i]
        ypc = np.asarray(r["o_yp"]).reshape(4, 256, D)
        ysc = np.asarray(r["o_ys"])
        nkc = np.asarray(r["o_nk"]).reshape(4, 256, 8, 2, 128)
        nvc = np.asarray(r["o_nv"]).reshape(4, 256, 8, 256)
        f_ = np.asarray(r["o_nsf"]).reshape(4, 32, 64, 128)
        b_ = np.asarray(r["o_nsb"]).reshape(4, 32, 64, 128)
        if hf == 1:
            ypc, nkc, nvc, ysc = ypc[:, ::-1], nkc[:, ::-1], nvc[:, ::-1], ysc[::-1]
            f_, b_ = b_, f_
            ys[b, 2048:] = ysc
        else:
            ys[b, :2048] = ysc
        yp[4 * i:4 * i + 4] = ypc
        nk[4 * i:4 * i + 4, 0] = nkc
        nv[4 * i:4 * i + 4, 0] = nvc
        nsf[4 * i:4 * i + 4, 0] = f_
        nsb[4 * i:4 * i + 4, 0] = b_
    return yp, ys, nk, nv, nsf, nsb


def stage_ctxkv(g):
    nc, sc, I, S = g.nc, g.sc, g.I, g.S
    with ExitStack() as st:
        xf = st.enter_context(nc.sbuf_tensor("c_xf", [128, 2048], F32))
        xb = st.enter_context(nc.sbuf_tensor("c_xb", [128, 2048], BF16))
        kts = st.enter_context(nc.sbuf_tensor("c_kts", [128, 16, 128], BF16))
        b_xf, b_xb, b_kts = Buf("c_xf"), Buf("c_xb"), Buf("c_kts")
        for kt in range(PAST // 128):
            r0 = kt * 128
            sc.dma("sp", xf[:], I.ck[r0:r0 + 128, :], b_xf, writes=[b_xf])
            sc.op("dve", lambda e: e.tensor_copy(out=xb[:], in_=xf[:]), reads=[b_xf], writes=[b_xb])
            for q in range(2):
                tb = 6 + q
                TP = g.ps[tb][:, :].bitcast(BF16)

                def tr(e, q=q, TP=TP):
                    ins = None
                    for i in range(8):
                        bk = q * 8 + i
                        ins = e.transpose(TP[:, i * 128:(i + 1) * 128], xb[:, bk * 128:(bk + 1) * 128], g.ident_b[:])
                    return ins
                sc.op("pe", tr, reads=[b_xb, g.b_ident], writes=[g.psb[tb]])
                sc.op("act", lambda e, q=q, TP=TP: e.copy(out=kts[:, q * 8:(q + 1) * 8, :],
                                                          in_=TP[:, :].rearrange("p (b t) -> p b t", b=8)),
                      reads=[g.psb[tb]], writes=[b_kts])
            sc.dma("sp", S.KT[:, :, NT + r0:NT + r0 + 128].rearrange("b d t -> d b t"), kts[:], b_kts,
                   reads=[b_kts], writes=[g.dbuf["KT"]])
            sc.dma("sp", xf[:], I.cv[r0:r0 + 128, :], b_xf, writes=[b_xf])
            sc.op("dve", lambda e: e.tensor_copy(out=xb[:], in_=xf[:]), reads=[b_xf], writes=[b_xb])
            sc.dma("sp", S.V[NT + r0:NT + r0 + 128, :], xb[:], b_xb, reads=[b_xb], writes=[g.dbuf["V"]])


def stage_ssd(g):
    nc, sc, I, S, O = g.nc, g.sc, g.I, g.S, g.O
    seqs = g.cfg.get("ssd_seqs", [0, 1, 2, 3, 4])
    groups = g.cfg.get("ssd_groups", [0, 1, 2, 3])
    NCH = 16
    with ExitStack() as st:
        def sb(name, shape, dt=F32):
            return st.enter_context(nc.sbuf_tensor("s_" + name, shape, dt))
        Uf, Ub = sb("Uf", [128, 128]), sb("Ub", [128, 128])
        Mf, Mb = sb("Mf", [128, 8, 128]), sb("Mb", [128, 8, 128])
        ones = sb("ones", [128, 128])
        b_const = Buf("s_const")

        def mk_consts(e):
            e.memset(ones[:], 1.0)
            e.memset(Uf[:], 1.0)
            e.memset(Ub[:], 1.0)
            e.memset(Mf[:], 0.0)
            e.memset(Mb[:], 0.0)
            e.affine_select(out=Uf[:], in_=Uf[:], pattern=[[1, 128]], compare_op=ALU.is_ge, fill=0.0, base=0, channel_multiplier=-1)
            e.affine_select(out=Ub[:], in_=Ub[:], pattern=[[-1, 128]], compare_op=ALU.is_ge, fill=0.0, base=0, channel_multiplier=1)
            e.affine_select(out=Mf[:], in_=Mf[:], pattern=[[0, 8], [1, 128]], compare_op=ALU.is_ge, fill=-30000.0, base=0, channel_multiplier=-1)
            return e.affine_select(out=Mb[:], in_=Mb[:], pattern=[[0, 8], [-1, 128]], compare_op=ALU.is_ge, fill=-30000.0, base=0, channel_multiplier=1)
        sc.op("pool", mk_consts, writes=[b_const])
        dtb, Ab, dsk = sb("dtb", [128, 64]), sb("Ab", [128, 64]), sb("dsk", [128, 32])
        sc.dma("sp", dtb[:], I.dt_bias[0, :].partition_broadcast(128), b_const, writes=[b_const])
        sc.dma("sp", Ab[:], I.a_log[0, :].partition_broadcast(128), b_const, writes=[b_const])
        sc.dma("sp", dsk[:], I.d_skip[0, :].partition_broadcast(128), b_const, writes=[b_const])
        sc.op("act", lambda e: e.activation(out=Ab[:], in_=Ab[:], func=AF.Exp), reads=[b_const], writes=[b_const])
        sc.op("dve", lambda e: e.tensor_scalar(out=Ab[:], in0=Ab[:], scalar1=-1.0, scalar2=None, op0=ALU.mult),
              reads=[b_const], writes=[b_const])
        W5, cbias, gssm = sb("W5", [128, 5, 768]), sb("cbias", [128, 768]), sb("gssm", [128, 512])
        b_gc = Buf("s_gc")
        xs = [[sb("xs%d_%d" % (i, j), [128, 768]) for j in range(5)] for i in range(2)]
        b_xs = [[Buf("s_xs%d_%d" % (i, j)) for j in range(5)] for i in range(2)]
        acc, xc = sb("acc", [128, 768]), sb("xc", [128, 768])
        xcb = sb("xcb", [128, 768], BF16)
        b_acc, b_xc = Buf("s_acc"), Buf("s_xc")
        dtr = [sb("dtr%d" % i, [128, 64]) for i in range(2)]
        b_dtr = [Buf("s_dtr0"), Buf("s_dtr1")]
        sm = {n: sb(n, [128, 64]) for n in ("ab", "ex", "dt", "a", "tot", "cs", "ncs", "ecs", "dcs", "cdec", "w2")}
        b_sm = Buf("s_sm")
        BT, CT = sb("BT", [128, 128], BF16), sb("CT", [128, NCH, 128], BF16)
        b_BT, b_CT = Buf("s_BT"), [Buf("s_CT%d" % i) for i in range(NCH)]
        AU = [sb("AU%d" % d, [128, 8, 128]) for d in range(2)]
        b_AU = [Buf("s_AU0"), Buf("s_AU1")]
        Lm = [sb("Lm%d" % d, [128, 8, 128]) for d in range(2)]
        b_Lm = [Buf("s_Lm0"), Buf("s_Lm1")]
        wts = [sb("wts%d" % d, [128, 8, 128], BF16) for d in range(2)]
        b_wts = [Buf("s_wts0"), Buf("s_wts1")]
        cbT = sb("cbT", [128, 128])
        b_cbT = Buf("s_cbT")
        xdt = [sb("xdt%d" % d, [128, 512], BF16) for d in range(2)]
        xdd = [sb("xdd%d" % d, [128, 512], BF16) for d in range(2)]
        b_xdt, b_xdd = [Buf("s_xdt0"), Buf("s_xdt1")], [Buf("s_xdd0"), Buf("s_xdd1")]
        state = [sb("state%d" % d, [128, 512]) for d in range(2)]
        prevb = [sb("prevb%d" % d, [128, 512], BF16) for d in range(2)]
        b_state = [Buf("s_state0"), Buf("s_state1")]
        ysum = sb("ysum", [128, NCH, 512])
        Sb_st = sb("Sbst", [128, NCH, 512])
        ecs_st = sb("ecsst", [128, NCH, 8])
        cdec_st = sb("cdecst", [128, NCH, 8])
        b_ch = [Buf("s_ch%d" % i) for i in range(NCH)]
        tmp = sb("tmp", [128, 512])
        b_tmp = Buf("s_tmp")
        zt = sb("zt", [128, 512])
        b_zt = Buf("s_zt")
        yb = sb("yb", [128, 512], BF16)
        b_yb = Buf("s_yb")
        mxs = sb("mxs", [128, 4, 128], BF16)
        b_mxs = Buf("s_mxs")
        stT = sb("stT", [128, 4, 128])
        b_stT = Buf("s_stT")
        stin = sb("stin", [128, 4, 128])
        b_stin = Buf("s_stin")
        nrm = sb("nrm", [128, 2])
        b_nrm = Buf("s_nrm")
        junk = sb("junk", [128, 512])
        b_junk = Buf("s_junk")
        xsi = [0]

        def load_shift(t0, seq_lo, seq_hi, gi, ncols=768):
            i = xsi[0] % 2
            xsi[0] += 1
            for j in range(5):
                lo = t0 + j - 2
                hi = lo + 128
                p0 = max(0, seq_lo - lo)
                p1 = 128 - max(0, hi - seq_hi)
                if p0 > 0 or p1 < 128:
                    sc.op("pool", lambda e, i=i, j=j: e.memset(xs[i][j][:], 0.0), writes=[b_xs[i][j]])
                sc.dma("sp", xs[i][j][p0:p1, 0:ncols], S.XBC[lo + p0:lo + p1, gi * 768:gi * 768 + ncols], b_xs[i][j],
                       reads=[g.dbuf["XBC"]], writes=[b_xs[i][j]])
            return i

        def conv_silu(i, ncols=768):
            for j in range(5):
                ek = "pool" if j % 2 == 0 else "dve"
                sc.op(ek, lambda e, j=j: e.tensor_tensor(out=xs[i][j][:, 0:ncols], in0=xs[i][j][:, 0:ncols],
                                                          in1=W5[:, j, 0:ncols], op=ALU.mult),
                      reads=[b_xs[i][j], b_gc], writes=[b_xs[i][j]])
            sc.op("dve", lambda e: e.tensor_tensor(out=acc[:, 0:ncols], in0=xs[i][0][:, 0:ncols], in1=xs[i][1][:, 0:ncols], op=ALU.add),
                  reads=[b_xs[i][0], b_xs[i][1]], writes=[b_acc])
            for j in (2, 3, 4):
                sc.op("dve", lambda e, j=j: e.tensor_tensor(out=acc[:, 0:ncols], in0=acc[:, 0:ncols], in1=xs[i][j][:, 0:ncols], op=ALU.add),
                      reads=[b_xs[i][j], b_acc], writes=[b_acc])
            sc.op("dve", lambda e: e.tensor_tensor(out=acc[:, 0:ncols], in0=acc[:, 0:ncols], in1=cbias[:, 0:ncols], op=ALU.add),
                  reads=[b_acc, b_gc], writes=[b_acc])
            sc.op("act", lambda e: e.activation(out=xc[:, 0:ncols], in_=acc[:, 0:ncols], func=AF.Silu), reads=[b_acc], writes=[b_xc])
            sc.op("dve", lambda e: e.tensor_copy(out=xcb[:, 0:ncols], in_=xc[:, 0:ncols]), reads=[b_xc], writes=[b_xc])

        def dt_stuff(t0, gi, dirs):
            di = xsi[0] % 2
            sc.dma("sp", dtr[di][:], S.DT[t0:t0 + 128, :], b_dtr[di], reads=[g.dbuf["DT"]], writes=[b_dtr[di]])
            R = [b_dtr[di], b_const, b_sm]
            Wr = [b_sm]
            s_ = sm
            sc.op("dve", lambda e: e.tensor_tensor(out=s_["dt"][:], in0=dtr[di][:], in1=dtb[:], op=ALU.add), reads=R, writes=Wr)
            sc.op("act", lambda e: e.activation(out=s_["ab"][:], in_=s_["dt"][:], func=AF.Abs), reads=R, writes=Wr)
            sc.op("act", lambda e: e.activation(out=s_["ex"][:], in_=s_["ab"][:], func=AF.Exp, scale=-1.0), reads=R, writes=Wr)
            sc.op("act", lambda e: e.activation(out=s_["ex"][:], in_=s_["ex"][:], func=AF.Ln, bias=1.0), reads=R, writes=Wr)
            sc.op("dve", lambda e: e.scalar_tensor_tensor(out=s_["dt"][:], in0=s_["dt"][:], scalar=0.0, in1=s_["ex"][:],
                                                          op0=ALU.max, op1=ALU.add), reads=R, writes=Wr)
            sc.op("dve", lambda e: e.tensor_tensor(out=s_["a"][:], in0=s_["dt"][:], in1=Ab[:], op=ALU.mult), reads=R, writes=Wr)
            P6 = g.ps[6]
            c_f = slice(gi * 8, gi * 8 + 8)
            c_b = slice(32 + gi * 8, 32 + gi * 8 + 8)

            def mm(e):
                e.matmul(P6[:, 0:64], lhsT=ones[:], rhs=s_["a"][:], start=True, stop=True)
                e.matmul(P6[:, 64:72], lhsT=Uf[:], rhs=s_["a"][:, c_f], start=True, stop=True)
                return e.matmul(P6[:, 72:80], lhsT=Ub[:], rhs=s_["a"][:, c_b], start=True, stop=True)
            sc.op("pe", mm, reads=[b_sm, b_const], writes=[g.psb[6]])
            R2 = [g.psb[6], b_sm]
            sc.op("dve", lambda e: e.tensor_copy(out=s_["tot"][:, 0:8], in_=P6[:, c_f]), reads=R2, writes=Wr)
            sc.op("dve", lambda e: e.tensor_copy(out=s_["tot"][:, 8:16], in_=P6[:, c_b]), reads=R2, writes=Wr)
            sc.op("dve", lambda e: e.tensor_copy(out=s_["cs"][:, 0:16], in_=P6[:, 64:80]), reads=R2, writes=Wr)
            sc.op("dve", lambda e: e.tensor_scalar(out=s_["ncs"][:, 0:16], in0=s_["cs"][:, 0:16], scalar1=-1.0, scalar2=None, op0=ALU.mult),
                  reads=R2, writes=Wr)
            sc.op("act", lambda e: e.activation(out=s_["ecs"][:, 0:16], in_=s_["cs"][:, 0:16], func=AF.Exp), reads=R2, writes=Wr)
            sc.op("dve", lambda e: e.tensor_tensor(out=s_["dcs"][:, 0:16], in0=s_["tot"][:, 0:16], in1=s_["cs"][:, 0:16], op=ALU.subtract),
                  reads=R2, writes=Wr)
            sc.op("act", lambda e: e.activation(out=s_["dcs"][:, 0:16], in_=s_["dcs"][:, 0:16], func=AF.Exp), reads=R2, writes=Wr)
            sc.op("act", lambda e: e.activation(out=s_["cdec"][:, 0:16], in_=s_["tot"][:, 0:16], func=AF.Exp), reads=R2, writes=Wr)
            sc.op("dve", lambda e: e.tensor_copy(out=s_["w2"][:, 0:8], in_=s_["dt"][:, c_f]), reads=R2, writes=Wr)
            sc.op("dve", lambda e: e.tensor_copy(out=s_["w2"][:, 8:16], in_=s_["dt"][:, c_b]), reads=R2, writes=Wr)
            sc.op("dve", lambda e: e.tensor_tensor(out=s_["w2"][:, 16:32], in0=s_["w2"][:, 0:16], in1=s_["dcs"][:, 0:16], op=ALU.mult),
                  reads=R2, writes=Wr)
            sc.op("dve", lambda e: e.tensor_copy(out=s_["w2"][:, 32:40], in_=s_["a"][:, c_f]), reads=R2, writes=Wr)
            sc.op("dve", lambda e: e.tensor_copy(out=s_["w2"][:, 40:48], in_=s_["a"][:, c_b]), reads=R2, writes=Wr)

        def xmul(dst, b_dst, col0, d):
            sc.op("dve", lambda e: e.tensor_tensor(
                out=dst[:].rearrange("p (h q) -> p h q", h=8), in0=xc[:, 0:512].rearrange("p (h q) -> p h q", h=8),
                in1=sm["w2"][:, col0 + d * 8:col0 + d * 8 + 8].unsqueeze(2).to_broadcast([128, 8, 64]), op=ALU.mult),
                reads=[b_xc, b_sm], writes=[b_dst])

        def chunk_states(d):
            xmul(xdd[d], b_xdd[d], 16, d)
            sc.op("pe", lambda e: e.matmul(g.ps[5][:, :], lhsT=xcb[:, 512:640], rhs=xdd[d][:], start=True, stop=True),
                  reads=[b_xc, b_xdd[d]], writes=[g.psb[5]])

        def state_update(d, S_ap, S_bufs, cdec_ap, cdec_bufs):
            sc.op("dve", lambda e: e.tensor_tensor(
                out=state[d][:].rearrange("p (h q) -> p h q", h=8), in0=state[d][:].rearrange("p (h q) -> p h q", h=8),
                in1=cdec_ap.unsqueeze(2).to_broadcast([128, 8, 64]), op=ALU.mult),
                reads=[b_state[d]] + cdec_bufs, writes=[b_state[d]])
            sc.op("dve", lambda e: e.tensor_tensor(out=state[d][:], in0=state[d][:], in1=S_ap, op=ALU.add),
                  reads=[b_state[d]] + S_bufs, writes=[b_state[d]])
            sc.op("pool", lambda e: e.tensor_copy(out=prevb[d][:], in_=state[d][:]), reads=[b_state[d]], writes=[b_state[d]])

        def init_state(d, src):
            if src is None:
                sc.op("pool", lambda e: e.memset(state[d][:], 0.0), writes=[b_state[d]])
                sc.op("pool", lambda e: e.memset(prevb[d][:], 0.0), writes=[b_state[d]])
                return
            sc.dma("sp", stin[:], src.rearrange("(c p) n -> p c n", p=128), b_stin, writes=[b_stin])

            def tr(e):
                ins = None
                for c in range(4):
                    ins = e.transpose(g.ps[5][:, c * 128:(c + 1) * 128], stin[:, c, :], g.ident_f[:])
                return ins
            sc.op("pe", tr, reads=[b_stin, g.b_ident], writes=[g.psb[5]])
            sc.op("dve", lambda e: e.tensor_copy(out=state[d][:], in_=g.ps[5][:, :]), reads=[g.psb[5]], writes=[b_state[d]])
            sc.op("pool", lambda e: e.tensor_copy(out=prevb[d][:], in_=state[d][:]), reads=[b_state[d]], writes=[b_state[d]])

        def out_state(d, dst):
            def tr(e):
                ins = None
                for c in range(4):
                    ins = e.transpose(g.ps[5][:, c * 128:(c + 1) * 128], state[d][:, c * 128:(c + 1) * 128], g.ident_f[:])
                return ins
            sc.op("pe", tr, reads=[b_state[d], g.b_ident], writes=[g.psb[5]])
            sc.op("dve", lambda e: e.tensor_copy(out=stT[:].rearrange("p c n -> p (c n)"), in_=g.ps[5][:, :]),
                  reads=[g.psb[5]], writes=[b_stT])
            sc.dma("sp", dst.rearrange("(c p) n -> p c n", p=128), stT[:], b_stT, reads=[b_stT], writes=[g.dbuf["out"]])

        def diag_part(d, c):
            U = Uf if d == 0 else Ub
            M = Mf if d == 0 else Mb
            a_col = 32 + d * 8
            sc.op("dve" if d == 0 else "pool", lambda e: e.tensor_tensor(
                out=AU[d][:], in0=sm["w2"][:, a_col:a_col + 8].unsqueeze(2).to_broadcast([128, 8, 128]),
                in1=U[:].unsqueeze(1).to_broadcast([128, 8, 128]), op=ALU.mult),
                reads=[b_sm, b_const], writes=[b_AU[d]])
            pb0 = 0 + 2 * d

            def mm(e):
                ins = None
                for hf in range(2):
                    e.matmul(g.ps[pb0 + hf][:, :], lhsT=ones[:], rhs=AU[d][:, hf * 4:(hf + 1) * 4, :].rearrange("p h l -> p (h l)"),
                             start=True, stop=False)
                    ins = e.matmul(g.ps[pb0 + hf][:, :], lhsT=g.ident_f[:], rhs=M[:, hf * 4:(hf + 1) * 4, :].rearrange("p h l -> p (h l)"),
                                   start=False, stop=True)
                return ins
            sc.op("pe", mm, reads=[b_AU[d], b_const, g.b_ident], writes=[g.psb[pb0], g.psb[pb0 + 1]])
            for h in range(8):
                sc.op("act", lambda e, h=h: e.activation(
                    out=Lm[d][:, h, :], in_=g.ps[pb0 + h // 4][:, (h % 4) * 128:(h % 4 + 1) * 128], func=AF.Exp,
                    bias=sm["ncs"][:, d * 8 + h:d * 8 + h + 1]),
                    reads=[g.psb[pb0], g.psb[pb0 + 1], b_sm], writes=[b_Lm[d]])
            sc.op("dve" if d == 1 else "pool", lambda e: e.tensor_tensor(
                out=wts[d][:], in0=Lm[d][:], in1=cbT[:].unsqueeze(1).to_broadcast([128, 8, 128]), op=ALU.mult),
                reads=[b_Lm[d], b_cbT], writes=[b_wts[d]])
            xmul(xdt[d], b_xdt[d], 0, d)

            def mm2(e):
                ins = None
                for h in range(8):
                    ins = e.matmul(g.ps[4][:, h * 64:(h + 1) * 64], lhsT=wts[d][:, h, :], rhs=xdt[d][:, h * 64:(h + 1) * 64],
                                   start=True, stop=True)
                return ins
            sc.op("pe", mm2, reads=[b_wts[d], b_xdt[d]], writes=[g.psb[4]])

        def yoff_add(d, c, ecs_ap, ecs_bufs):
            sc.op("pe", lambda e: e.matmul(g.ps[5][:, :], lhsT=CT[:, c, :], rhs=prevb[d][:], start=True, stop=True),
                  reads=[b_CT[c], b_state[d]], writes=[g.psb[5]])
            sc.op("dve", lambda e: e.tensor_tensor(
                out=tmp[:].rearrange("p (h q) -> p h q", h=8), in0=g.ps[5][:, :].rearrange("p (h q) -> p h q", h=8),
                in1=ecs_ap.unsqueeze(2).to_broadcast([128, 8, 64]), op=ALU.mult),
                reads=[g.psb[5]] + ecs_bufs, writes=[b_tmp])
            sc.op("pool", lambda e: e.tensor_tensor(out=ysum[:, c, :], in0=ysum[:, c, :], in1=tmp[:], op=ALU.add),
                  reads=[b_tmp, b_ch[c]], writes=[b_ch[c]])

        for gi in groups:
            with nc.allow_non_contiguous_dma(reason="broadcast const loads"):
                for (o0, c0, n) in ((0, gi * 512, 512), (512, 2048 + gi * 128, 128), (640, 2560 + gi * 128, 128)):
                    sc.dma("sp", W5[:, :, o0:o0 + n], I.conv_w[:, c0:c0 + n].partition_broadcast(128), b_gc, writes=[b_gc])
                    sc.dma("sp", cbias[:, o0:o0 + n], I.conv_b[0, c0:c0 + n].partition_broadcast(128), b_gc, writes=[b_gc])
                sc.dma("sp", gssm[:], I.g_ssm[0, gi * 512:(gi + 1) * 512].partition_broadcast(128), b_gc, writes=[b_gc])
            for seq in seqs:
                if seq < 4:
                    base, nown, nrem = seq * 256, 2, 0
                    seq_lo, seq_hi = base, base + 256
                    init_state(0, None)
                    init_state(1, None)
                else:
                    base, nown, nrem = NP_TOK, 16, 16
                    seq_lo, seq_hi = base, base + NS_ALL
                    init_state(0, I.sf[gi * 512:(gi + 1) * 512, :])
                    init_state(1, I.sb[gi * 512:(gi + 1) * 512, :])
                    for c in range(nown + nrem - 1, nown - 1, -1):
                        t0 = base + c * 128
                        g.conv_step(1)
                        i = load_shift(t0, seq_lo, seq_hi, gi, 640)
                        conv_silu(i, 640)
                        dt_stuff(t0, gi, (1,))
                        chunk_states(1)
                        state_update(1, g.ps[5][:, :], [g.psb[5]], sm["cdec"][:, 8:16], [b_sm])
                for c in range(nown):
                    t0 = base + c * 128
                    g.conv_step(1)
                    i = load_shift(t0, seq_lo, seq_hi, gi)
                    conv_silu(i)
                    dt_stuff(t0, gi, (0, 1))
                    TP = g.ps[7][:, :].bitcast(BF16)

                    def tr(e, TP=TP):
                        e.transpose(TP[:, 0:128], xcb[:, 512:640], g.ident_b[:])
                        return e.transpose(TP[:, 128:256], xcb[:, 640:768], g.ident_b[:])
                    sc.op("pe", tr, reads=[b_xc, g.b_ident], writes=[g.psb[7]])
                    sc.op("act", lambda e, TP=TP: e.copy(out=BT[:], in_=TP[:, 0:128]), reads=[g.psb[7]], writes=[b_BT])
                    sc.op("act", lambda e, TP=TP, c=c: e.copy(out=CT[:, c, :], in_=TP[:, 128:256]), reads=[g.psb[7]], writes=[b_CT[c]])
                    sc.op("pe", lambda e, c=c: e.matmul(g.ps[6][:, 128:256], lhsT=BT[:], rhs=CT[:, c, :], start=True, stop=True),
                          reads=[b_BT, b_CT[c]], writes=[g.psb[6]])
                    sc.op("act", lambda e: e.copy(out=cbT[:], in_=g.ps[6][:, 128:256]), reads=[g.psb[6]], writes=[b_cbT])
                    sc.op("dve", lambda e, c=c: e.tensor_tensor(
                        out=ysum[:, c, :].rearrange("p (h q) -> p h q", h=8), in0=xc[:, 0:512].rearrange("p (h q) -> p h q", h=8),
                        in1=dsk[:, gi * 8:(gi + 1) * 8].unsqueeze(2).to_broadcast([128, 8, 64]), op=ALU.mult),
                        reads=[b_xc, b_const], writes=[b_ch[c]])
                    for d in (0, 1):
                        diag_part(d, c)
                        sc.op("dve", lambda e, c=c: e.tensor_tensor(out=ysum[:, c, :], in0=ysum[:, c, :], in1=g.ps[4][:, :], op=ALU.add),
                              reads=[g.psb[4], b_ch[c]], writes=[b_ch[c]])
                    yoff_add(0, c, sm["ecs"][:, 0:8], [b_sm])
                    chunk_states(0)
                    state_update(0, g.ps[5][:, :], [g.psb[5]], sm["cdec"][:, 0:8], [b_sm])
                    chunk_states(1)
                    sc.op("act", lambda e, c=c: e.copy(out=Sb_st[:, c, :], in_=g.ps[5][:, :]), reads=[g.psb[5]], writes=[b_ch[c]])
                    sc.op("pool", lambda e, c=c: e.tensor_copy(out=ecs_st[:, c, :], in_=sm["ecs"][:, 8:16]), reads=[b_sm], writes=[b_ch[c]])
                    sc.op("pool", lambda e, c=c: e.tensor_copy(out=cdec_st[:, c, :], in_=sm["cdec"][:, 8:16]), reads=[b_sm], writes=[b_ch[c]])
                if seq < 4:
                    out_state(0, O.nsf[(seq * 32 + gi * 8) * 64:(seq * 32 + gi * 8 + 8) * 64, :])
                for c in range(nown - 1, -1, -1):
                    t0 = base + c * 128
                    yoff_add(1, c, ecs_st[:, c, :], [b_ch[c]])
                    state_update(1, Sb_st[:, c, :], [b_ch[c]], cdec_st[:, c, :], [b_ch[c]])
                    sc.dma("sp", zt[:], S.Z[t0:t0 + 128, gi * 512:(gi + 1) * 512], b_zt, reads=[g.dbuf["Z"]], writes=[b_zt])
                    sc.op("act", lambda e: e.activation(out=zt[:], in_=zt[:], func=AF.Silu), reads=[b_zt], writes=[b_zt])
                    sc.op("dve", lambda e, c=c: e.tensor_tensor(out=tmp[:], in0=ysum[:, c, :], in1=zt[:], op=ALU.mult),
                          reads=[b_ch[c], b_zt], writes=[b_tmp])
                    sc.op("act", lambda e: e.activation(out=junk[:], in_=tmp[:], func=AF.Square, accum_out=nrm[:, 0:1]),
                          reads=[b_tmp], writes=[b_junk, b_nrm])
                    sc.op("act", lambda e: e.activation(out=nrm[:, 1:2], in_=nrm[:, 0:1], func=AF.Sqrt, scale=1.0 / 512, bias=EPS),
                          reads=[b_nrm], writes=[b_nrm])
                    sc.op("dve", lambda e: e.reciprocal(out=nrm[:, 1:2], in_=nrm[:, 1:2]), reads=[b_nrm], writes=[b_nrm])
                    sc.op("dve", lambda e: e.scalar_tensor_tensor(out=yb[:], in0=tmp[:], scalar=nrm[:, 1:2], in1=gssm[:],
                                                                  op0=ALU.mult, op1=ALU.mult),
                          reads=[b_tmp, b_nrm, b_gc], writes=[b_yb])
                    TP = g.ps[7][:, :].bitcast(BF16)

                    def tr2(e, TP=TP):
                        ins = None
                        for k in range(4):
                            ins = e.transpose(TP[:, k * 128:(k + 1) * 128], yb[:, k * 128:(k + 1) * 128], g.ident_b[:])
                        return ins
                    sc.op("pe", tr2, reads=[b_yb, g.b_ident], writes=[g.psb[7]])
                    sc.op("act", lambda e, TP=TP: e.copy(out=mxs[:].rearrange("p k t -> p (k t)"), in_=TP[:, 0:512]),
                          reads=[g.psb[7]], writes=[b_mxs])
                    kc0 = 16 + gi * 4
                    sc.dma("sp", S.MIXT[kc0:kc0 + 4, :, t0:t0 + 128].rearrange("k d t -> d k t"), mxs[:], b_mxs,
                           reads=[b_mxs], writes=[g.dbuf["MIXT"]])
                if seq < 4:
                    out_state(1, O.nsb[(seq * 32 + gi * 8) * 64:(seq * 32 + gi * 8 + 8) * 64, :])


def stage_attn(g):
    nc, sc, I, S = g.nc, g.sc, g.I, g.S
    seqs = g.cfg.get("attn_seqs", [0, 1, 2, 3, 4])
    heads = g.cfg.get("attn_heads", list(range(8)))
    scale = 128 ** -0.5
    with ExitStack() as st:
        def sb(name, shape, dt=F32):
            return st.enter_context(nc.sbuf_tensor("t_" + name, shape, dt))
        NKMAX = (NS_ALL + PAST) // 128
        K2 = [sb("K2_%d" % i, [128, 2, NKMAX * 128], BF16) for i in range(2)]
        Vt = [sb("Vt_%d" % i, [128, NKMAX, 257], BF16) for i in range(2)]
        Q2 = [sb("Q2_%d" % i, [128, 2, 512], BF16) for i in range(2)]
        PT = [sb("PT_%d" % i, [128, 512], BF16) for i in range(3)]
        osb = sb("osb", [128, 4, 256])
        ob = sb("ob", [128, 256], BF16)
        mx = [sb("mx%d" % i, [128, 2, 512], BF16) for i in range(2)]
        lamt = sb("lamt", [128, 512])
        sm = sb("sm", [128, 8])
        rr = sb("rr", [128, 4])
        gsub = sb("gsub", [128, 256])
        junk = sb("junk", [128, 256])
        b_K2, b_Vt, b_Q2 = [Buf("t_K0"), Buf("t_K1")], [Buf("t_V0"), Buf("t_V1")], [Buf("t_Q0"), Buf("t_Q1")]
        b_PT = [Buf("t_PT%d" % i) for i in range(3)]
        b_osb, b_ob, b_mx = Buf("t_osb"), Buf("t_ob"), [Buf("t_mx0"), Buf("t_mx1")]
        b_c, b_rr, b_junk = Buf("t_c"), Buf("t_rr"), Buf("t_junk")
        sc.dma("sp", lamt[:], I.lam[0, :].partition_broadcast(128), b_c, writes=[b_c])
        sc.dma("sp", gsub[:], I.g_subln[0, :].partition_broadcast(128), b_c, writes=[b_c])
        for i in range(2):
            sc.op("dve", lambda e, i=i: e.tensor_tensor(out=lamt[:, i * 256:i * 256 + 128], in0=lamt[:, i * 256:i * 256 + 128],
                                                        in1=lamt[:, i * 256 + 128:i * 256 + 256], op=ALU.mult), reads=[b_c], writes=[b_c])
            sc.op("act", lambda e, i=i: e.activation(out=junk[:, 0:128], in_=lamt[:, i * 256:i * 256 + 128], func=AF.Identity,
                                                     accum_out=sm[:, i:i + 1]), reads=[b_c], writes=[b_c, b_junk])
        sc.op("act", lambda e: e.activation(out=sm[:, 2:4], in_=sm[:, 0:2], func=AF.Exp), reads=[b_c], writes=[b_c])
        sc.op("dve", lambda e: e.tensor_tensor(out=sm[:, 4:5], in0=sm[:, 3:4], in1=sm[:, 2:3], op=ALU.subtract), reads=[b_c], writes=[b_c])
        sc.op("dve", lambda e: e.tensor_scalar(out=sm[:, 4:5], in0=sm[:, 4:5], scalar1=-LAM_INIT, scalar2=None, op0=ALU.add),
              reads=[b_c], writes=[b_c])
        sc.op("dve", lambda e: e.tensor_scalar(out=gsub[:], in0=gsub[:], scalar1=1.0 - LAM_INIT, scalar2=None, op0=ALU.mult),
              reads=[b_c], writes=[b_c])
        for i in range(2):
            sc.op("pool", lambda e, i=i: e.memset(Vt[i][:, :, 256:257], 1.0), writes=[b_Vt[i]])
        cnt = dict(kv=0, q=0, pt=0, st=0, mx=0)
        for seq in seqs:
            if seq < 4:
                k0, nk, q0, nq = seq * 256, 256, seq * 256, 256
            else:
                k0, nk, q0, nq = NP_TOK, NS_ALL + PAST, NP_TOK, NS_OWN
            nkc = nk // 128
            for h in heads:
                ki = cnt["kv"] % 2
                cnt["kv"] += 1
                for j in range(2):
                    sc.dma("sp", K2[ki][:, j, 0:nk], S.KT[h * 2 + j, :, k0:k0 + nk], b_K2[ki], reads=[g.dbuf["KT"]], writes=[b_K2[ki]])
                sc.dma("sp", Vt[ki][:, 0:nkc, 0:256], S.V[k0:k0 + nk, h * 256:(h + 1) * 256].rearrange("(c p) v -> p c v", p=128),
                       b_Vt[ki], reads=[g.dbuf["V"]], writes=[b_Vt[ki]])
                for qt0 in range(0, nq, 512):
                    nqt = min(512, nq - qt0)
                    nqb = nqt // 128
                    qi = cnt["q"] % 2
                    cnt["q"] += 1
                    for j in range(2):
                        sc.dma("sp", Q2[qi][:, j, 0:nqt], S.QT[h * 2 + j, :, q0 + qt0:q0 + qt0 + nqt], b_Q2[qi],
                               reads=[g.dbuf["QT"]], writes=[b_Q2[qi]])
                    items = [(j, kc) for j in range(2) for kc in range(nkc)]
                    slots = {}

                    def issue_st(idx):
                        j, kc = items[idx]
                        pb = 4 + cnt["st"] % 3
                        cnt["st"] += 1
                        pi = cnt["pt"] % 3
                        cnt["pt"] += 1
                        slots[idx] = pi
                        sc.op("pe", lambda e, kc=kc, j=j, pb=pb: e.matmul(
                            g.ps[pb][:, 0:nqt], lhsT=K2[ki][:, j, kc * 128:(kc + 1) * 128], rhs=Q2[qi][:, j, 0:nqt],
                            start=True, stop=True), reads=[b_K2[ki], b_Q2[qi]], writes=[g.psb[pb]])
                        sc.op("act", lambda e, pb=pb, pi=pi: e.activation(out=PT[pi][:, 0:nqt], in_=g.ps[pb][:, 0:nqt],
                                                                          func=AF.Exp, scale=scale),
                              reads=[g.psb[pb]], writes=[b_PT[pi]])
                    PRE = 2
                    for idx in range(min(PRE, len(items))):
                        issue_st(idx)
                    for idx, (j, kc) in enumerate(items):
                        if idx + PRE < len(items):
                            issue_st(idx + PRE)
                        pi = slots.pop(idx)

                        def av(e, kc=kc, pi=pi):
                            ins = None
                            for qb in range(nqb):
                                ins = e.matmul(g.ps[qb][:, 0:257], lhsT=PT[pi][:, qb * 128:(qb + 1) * 128], rhs=Vt[ki][:, kc, :],
                                               start=(kc == 0), stop=(kc == nkc - 1))
                            return ins
                        sc.op("pe", av, reads=[b_PT[pi], b_Vt[ki]], writes=[g.psb[qb] for qb in range(nqb)])
                        if kc != nkc - 1:
                            continue
                        for qb in range(nqb):
                            A = g.ps[qb]
                            sc.op("dve", lambda e, A=A, j=j: e.reciprocal(out=rr[:, j:j + 1], in_=A[:, 256:257]),
                                  reads=[g.psb[qb]], writes=[b_rr])
                            if j == 0:
                                sc.op("dve", lambda e, A=A, qb=qb: e.tensor_scalar(out=osb[:, qb, :], in0=A[:, 0:256], scalar1=rr[:, 0:1],
                                                                                   scalar2=None, op0=ALU.mult),
                                      reads=[g.psb[qb], b_rr], writes=[b_osb])
                            else:
                                sc.op("dve", lambda e: e.tensor_tensor(out=rr[:, 2:3], in0=rr[:, 1:2], in1=sm[:, 4:5], op=ALU.mult),
                                      reads=[b_rr, b_c], writes=[b_rr])
                                sc.op("dve", lambda e, A=A, qb=qb: e.scalar_tensor_tensor(
                                    out=osb[:, qb, :], in0=A[:, 0:256], scalar=rr[:, 2:3], in1=osb[:, qb, :], op0=ALU.mult, op1=ALU.add),
                                    reads=[g.psb[qb], b_rr, b_osb], writes=[b_osb])
                    mi = cnt["mx"] % 2
                    cnt["mx"] += 1
                    for qb in range(nqb):
                        sc.op("act", lambda e, qb=qb: e.activation(out=junk[:], in_=osb[:, qb, :], func=AF.Square, accum_out=rr[:, 3:4]),
                              reads=[b_osb], writes=[b_rr, b_junk])
                        sc.op("act", lambda e: e.activation(out=rr[:, 3:4], in_=rr[:, 3:4], func=AF.Sqrt, scale=1.0 / 256, bias=EPS),
                              reads=[b_rr], writes=[b_rr])
                        sc.op("dve", lambda e: e.reciprocal(out=rr[:, 3:4], in_=rr[:, 3:4]), reads=[b_rr], writes=[b_rr])
                        sc.op("dve", lambda e, qb=qb: e.scalar_tensor_tensor(out=ob[:], in0=osb[:, qb, :], scalar=rr[:, 3:4], in1=gsub[:],
                                                                             op0=ALU.mult, op1=ALU.mult),
                              reads=[b_osb, b_rr, b_c], writes=[b_ob])
                        TP = g.ps[7][:, :].bitcast(BF16)

                        def tr(e, TP=TP):
                            e.transpose(TP[:, 0:128], ob[:, 0:128], g.ident_b[:])
                            return e.transpose(TP[:, 128:256], ob[:, 128:256], g.ident_b[:])
                        sc.op("pe", tr, reads=[b_ob, g.b_ident], writes=[g.psb[7]])
                        sc.op("act", lambda e, TP=TP, qb=qb, mi=mi: e.copy(out=mx[mi][:, :, qb * 128:(qb + 1) * 128],
                                                                          in_=TP[:, 0:256].rearrange("p (k t) -> p k t", k=2)),
                              reads=[g.psb[7]], writes=[b_mx[mi]])
                    sc.dma("sp", S.MIXT[h * 2:h * 2 + 2, :, q0 + qt0:q0 + qt0 + nqt].rearrange("k d t -> d k t"), mx[mi][:, :, 0:nqt],
                           b_mx[mi], reads=[b_mx[mi]], writes=[g.dbuf["MIXT"]])


def stage_mlp(g):
    nc, sc, I, S, O = g.nc, g.sc, g.I, g.S, g.O
    groups = g.cfg.get("mlp_groups", list(range(NOWN // 512)))
    nfb = g.cfg.get("mlp_fblocks", 16)
    if "MIXO" not in g.dbuf:
        g.dbuf["MIXO"] = Buf("d_MIXO", acc=True)
    MIXO = nc.dram_tensor("MIXO", [NOWN, D], F32, kind="ExternalOutput" if "MIXO" in g.debug else "Internal").ap()

    def load_rowmod(st, name, mode, idx, which):
        t = st.enter_context(nc.sbuf_tensor(name, [128, D], F32))
        b = Buf(name)
        return t, b

    def fill_rowmod(t, b, tmp, b_tmp, mode, idx, which):
        sc.dma("sp", t[:], S.modraw[mode, idx * D:(idx + 1) * D].partition_broadcast(128), b, reads=[g.dbuf["modraw"]], writes=[b])
        sc.dma("sp", tmp[:], I.grows[which, :].partition_broadcast(128), b_tmp, writes=[b_tmp])
        sc.op("dve", lambda e: e.tensor_tensor(out=t[:], in0=t[:], in1=tmp[:], op=ALU.mult), reads=[b, b_tmp], writes=[b])

    with ExitStack() as st0:
        h2T = st0.enter_context(nc.sbuf_tensor("m_h2T", [128, NKC, 512], BF16))
        b_h2T = Buf("m_h2T")
        ss = st0.enter_context(nc.sbuf_tensor("m_ss", [128, 1], F32))
        rstd = st0.enter_context(nc.sbuf_tensor("m_rstd", [128, 1], F32))
        b_scr = Buf("m_scr")
        modc = {}
        modc[0] = load_modcols(g, st0, 0, 2, 4, 3, "m_modp")
        modc[1] = load_modcols(g, st0, 1, 2, 4, 3, "m_mods")
        for grp in groups:
            t0g = grp * 512
            mode = 0 if t0g < NP_TOK else 1
            with ExitStack() as st:
                mixT = st.enter_context(nc.sbuf_tensor("m_mixT_%d" % grp, [128, NKC, 512], BF16))
                W = [st.enter_context(nc.sbuf_tensor("m_Wo%d_%d" % (i, grp), [128, NKC, 512], BF16)) for i in range(2)]
                stg = [st.enter_context(nc.sbuf_tensor("m_stg%d_%d" % (i, grp), [128, 512], F32)) for i in range(4)]
                GA = st.enter_context(nc.sbuf_tensor("m_GA_%d" % grp, [128, D], F32))
                mixt = st.enter_context(nc.sbuf_tensor("m_mix_%d" % grp, [128, D], F32))
                xt = st.enter_context(nc.sbuf_tensor("m_x_%d" % grp, [128, D], F32))
                junk = st.enter_context(nc.sbuf_tensor("m_junk_%d" % grp, [128, D], BF16))
                b_mixT, b_W, b_stg = Buf("m_mixT"), [Buf("m_Wo0"), Buf("m_Wo1")], [Buf("m_stg%d" % i) for i in range(4)]
                b_GA, b_mix, b_x, b_junk = Buf("m_GA"), Buf("m_mix"), Buf("m_x"), Buf("m_junk")
                fill_rowmod(GA, b_GA, mixt, b_mix, mode, 2, 1)
                sc.dma("sp", mixT[:], S.MIXT[:, :, t0g:t0g + 512].rearrange("k d t -> d k t"), b_mixT,
                       reads=[g.dbuf["MIXT"]], writes=[b_mixT])
                wv = S.WOB.rearrange("(kc p) n -> p kc n", p=128)
                k = 0
                for c in range(8):
                    wi = c % 2
                    sc.dma("pool", W[wi][:], wv[:, :, c * 512:(c + 1) * 512], b_W[wi], reads=[g.dbuf["WB"]], writes=[b_W[wi]])
                    for tt in range(4):
                        pb = k % 4
                        si = k % 4
                        k += 1

                        def mm(e, tt=tt, wi=wi, pb=pb):
                            ins = None
                            for kc in range(NKC):
                                ins = e.matmul(g.ps[pb][:, :], lhsT=mixT[:, kc, tt * 128:(tt + 1) * 128], rhs=W[wi][:, kc, :],
                                               start=(kc == 0), stop=(kc == NKC - 1))
                            return ins
                        sc.op("pe", mm, reads=[b_mixT, b_W[wi]], writes=[g.psb[pb]])
                        if k % 2 == 0:
                            sc.op("act", lambda e, si=si, pb=pb: e.copy(out=stg[si][:], in_=g.ps[pb][:, :]), reads=[g.psb[pb]], writes=[b_stg[si]])
                        else:
                            sc.op("dve", lambda e, si=si, pb=pb: e.tensor_copy(out=stg[si][:], in_=g.ps[pb][:, :]), reads=[g.psb[pb]], writes=[b_stg[si]])
                        sc.dma("sp", MIXO[t0g + tt * 128:t0g + (tt + 1) * 128, c * 512:(c + 1) * 512], stg[si][:], b_stg[si],
                               reads=[b_stg[si]], writes=[g.dbuf["MIXO"]])
                for tt in range(4):
                    t0 = t0g + tt * 128
                    sc.dma("sp", mixt[:], MIXO[t0:t0 + 128, :], b_mix, reads=[g.dbuf["MIXO"]], writes=[b_mix])
                    sc.dma("sp", xt[:], tok_src(g, t0), b_x, writes=[b_x])
                    sc.op("act", lambda e: e.activation(out=junk[:], in_=mixt[:], func=AF.Square, accum_out=ss[:]),
                          reads=[b_mix], writes=[b_scr, b_junk])
                    sc.op("act", lambda e: e.activation(out=rstd[:], in_=ss[:], func=AF.Sqrt, scale=1.0 / D, bias=EPS),
                          reads=[b_scr], writes=[b_scr])
                    sc.op("dve", lambda e: e.reciprocal(out=rstd[:], in_=rstd[:]), reads=[b_scr], writes=[b_scr])
                    sc.op("dve", lambda e: e.scalar_tensor_tensor(out=mixt[:], in0=mixt[:], scalar=rstd[:, 0:1], in1=GA[:],
                                                                  op0=ALU.mult, op1=ALU.mult), reads=[b_mix, b_scr, b_GA], writes=[b_mix])
                    sc.op("pool", lambda e: e.tensor_tensor(out=xt[:], in0=xt[:], in1=mixt[:], op=ALU.add), reads=[b_mix, b_x], writes=[b_x])
                    sc.dma("sp", S.X1[t0:t0 + 128, :], xt[:], b_x, reads=[b_x], writes=[g.dbuf["X1"]])
                    norm_transpose(g, xt, b_x, modc[mode][0], modc[mode][1], h2T, b_h2T, tt, (ss, rstd), b_scr, junk, b_junk)
            sc.barrier()
            with ExitStack() as stB:
                macc = stB.enter_context(nc.sbuf_tensor("m_acc_%d" % grp, [128, 4, D], F32))
                b_macc = [[Buf("m_acc%d_%d" % (a, c)) for c in range(8)] for a in range(4)]
                with ExitStack() as st:
                    Wu = [st.enter_context(nc.sbuf_tensor("m_Wu%d_%d" % (i, grp), [128, NKC, 512], BF16)) for i in range(2)]
                    Wd = [st.enter_context(nc.sbuf_tensor("m_Wd%d_%d" % (i, grp), [128, 8, 512], BF16)) for i in range(2)]
                    uT = [st.enter_context(nc.sbuf_tensor("m_uT%d_%d" % (i, grp), [128, 8, 512], BF16)) for i in range(2)]
                    r32 = [st.enter_context(nc.sbuf_tensor("m_r%d_%d" % (i, grp), [128, 512], F32)) for i in range(2)]
                    b_Wu, b_Wd, b_uT = [Buf("m_Wu%d" % i) for i in range(2)], [Buf("m_Wd0"), Buf("m_Wd1")], [Buf("m_uT0"), Buf("m_uT1")]
                    b_r = [Buf("m_r0"), Buf("m_r1")]
                    wuv = S.WUB.rearrange("(kc p) n -> p kc n", p=128)
                    wdv = S.WDB.rearrange("(fc p) n -> p fc n", p=128)
                    cnt = dict(wu=0, wd=0, ps=0, r=0)
                    for fb in range(nfb):
                        ui = fb % 2
                        for hf in range(2):
                            wi = cnt["wu"] % 2
                            cnt["wu"] += 1
                            c0 = fb * 1024 + hf * 512
                            sc.dma("pool", Wu[wi][:], wuv[:, :, c0:c0 + 512], b_Wu[wi], reads=[g.dbuf["WB"]], writes=[b_Wu[wi]])
                            for c4 in range(4):
                                ch = hf * 4 + c4
                                pb = 4 + cnt["ps"] % 2

                                def mm(e, wi=wi, pb=pb, c4=c4):
                                    ins = None
                                    for kc in range(NKC):
                                        ins = e.matmul(g.ps[pb][:, :], lhsT=Wu[wi][:, kc, c4 * 128:(c4 + 1) * 128], rhs=h2T[:, kc, :],
                                                       start=(kc == 0), stop=(kc == NKC - 1))
                                    return ins
                                sc.op("pe", mm, reads=[b_Wu[wi], b_h2T], writes=[g.psb[pb]])
                                ri = cnt["r"] % 2
                                cnt["r"] += 1
                                cnt["ps"] += 1
                                sc.op("act", lambda e, ri=ri, pb=pb: e.activation(out=r32[ri][:], in_=g.ps[pb][:, :], func=AF.Relu),
                                      reads=[g.psb[pb]], writes=[b_r[ri]])
                                sc.op("pool", lambda e, ri=ri, ui=ui, ch=ch: e.tensor_tensor(out=uT[ui][:, ch, :], in0=r32[ri][:], in1=r32[ri][:], op=ALU.mult),
                                      reads=[b_r[ri]], writes=[b_uT[ui]])
                        for c in range(8):
                            di = cnt["wd"] % 2
                            cnt["wd"] += 1
                            sc.dma("pool", Wd[di][:], wdv[:, fb * 8:(fb + 1) * 8, c * 512:(c + 1) * 512], b_Wd[di],
                                   reads=[g.dbuf["WB"]], writes=[b_Wd[di]])
                            for tt in range(4):
                                pb = cnt["ps"] % 4
                                cnt["ps"] += 1

                                def mm2(e, tt=tt, di=di, pb=pb, ui=ui):
                                    ins = None
                                    for ch in range(8):
                                        ins = e.matmul(g.ps[pb][:, :], lhsT=uT[ui][:, ch, tt * 128:(tt + 1) * 128], rhs=Wd[di][:, ch, :],
                                                       start=(ch == 0), stop=(ch == 7))
                                    return ins
                                sc.op("pe", mm2, reads=[b_uT[ui], b_Wd[di]], writes=[g.psb[pb]])
                                dst = macc[:, tt, c * 512:(c + 1) * 512]
                                if fb == 0:
                                    sc.op("act", lambda e, dst=dst, pb=pb: e.copy(out=dst, in_=g.ps[pb][:, :]), reads=[g.psb[pb]], writes=[b_macc[tt][c]])
                                else:
                                    sc.op("dve", lambda e, dst=dst, pb=pb: e.tensor_tensor(out=dst, in0=dst, in1=g.ps[pb][:, :], op=ALU.add),
                                          reads=[g.psb[pb], b_macc[tt][c]], writes=[b_macc[tt][c]])
                sc.barrier()
                with ExitStack() as st:
                    GM = st.enter_context(nc.sbuf_tensor("m_GM_%d" % grp, [128, D], F32))
                    xt = st.enter_context(nc.sbuf_tensor("m_x1_%d" % grp, [128, D], F32))
                    junk = st.enter_context(nc.sbuf_tensor("m_junk2_%d" % grp, [128, D], BF16))
                    b_GM, b_x, b_junk = Buf("m_GM"), Buf("m_x1"), Buf("m_junk2")
                    fill_rowmod(GM, b_GM, xt, b_x, mode, 5, 3)
                    for tt in range(4):
                        t0 = t0g + tt * 128
                        mt = macc[:, tt, :]
                        sc.dma("sp", xt[:], S.X1[t0:t0 + 128, :], b_x, reads=[g.dbuf["X1"]], writes=[b_x])
                        sc.op("act", lambda e, mt=mt: e.activation(out=junk[:], in_=mt, func=AF.Square, accum_out=ss[:]),
                              reads=b_macc[tt], writes=[b_scr, b_junk])
                        sc.op("act", lambda e: e.activation(out=rstd[:], in_=ss[:], func=AF.Sqrt, scale=1.0 / D, bias=EPS),
                              reads=[b_scr], writes=[b_scr])
                        sc.op("dve", lambda e: e.reciprocal(out=rstd[:], in_=rstd[:]), reads=[b_scr], writes=[b_scr])
                        sc.op("dve", lambda e, mt=mt: e.scalar_tensor_tensor(out=mt, in0=mt, scalar=rstd[:, 0:1], in1=GM[:],
                                                                             op0=ALU.mult, op1=ALU.mult),
                              reads=b_macc[tt] + [b_scr, b_GM], writes=b_macc[tt])
                        sc.op("pool", lambda e, mt=mt: e.tensor_tensor(out=xt[:], in0=xt[:], in1=mt, op=ALU.add), reads=b_macc[tt] + [b_x], writes=[b_x])
                        dst = O.yp[t0:t0 + 128, :] if t0 < NP_TOK else O.ys[t0 - NP_TOK:t0 - NP_TOK + 128, :]
                        sc.dma("sp", dst, xt[:], b_x, reads=[b_x], writes=[g.dbuf["out"]])
                sc.barrier()
```

```python
import math
from contextlib import ExitStack
import numpy as np
import concourse.bass as bass
import concourse.mybir as mybir
from concourse.bass_utils import run_bass_kernel_spmd

F32 = mybir.dt.float32
BF16 = mybir.dt.bfloat16
AF = mybir.ActivationFunctionType
ALU = mybir.AluOpType
AX = mybir.AxisListType

D = 4096
NKC = 32
IN_COLS = 11328
NP_TOK = 1024
NS_OWN = 2048
NS_ALL = 4096
PAST = 512
NT = NP_TOK + NS_ALL
NOWN = NP_TOK + NS_OWN
DFF = 16384
EPS = 1e-6
LAM_INIT = 0.8 - 0.6 * math.exp(-0.3 * 0)


class Buf:
    __slots__ = ("name", "acc", "w", "r", "dsem", "dkey", "dcnt")

    def __init__(self, name, acc=False):
        self.name = name
        self.acc = acc
        self.w = {}
        self.r = {}
        self.dsem = None
        self.dkey = None
        self.dcnt = 0


class Sched:
    def __init__(self, nc, stack):
        self.nc = nc
        self.stack = stack
        self.engs = dict(pe=nc.tensor, act=nc.scalar, dve=nc.vector, pool=nc.gpsimd, sp=nc.sync)
        self.esem = {}
        self.ecnt = {}
        self.waited = {}
        for k in self.engs:
            self.esem[k] = stack.enter_context(nc.semaphore("es_" + k))
            self.ecnt[k] = 0
            self.waited[k] = {}
        self.dsems = []
        self.pinned = []
        self.free_sems = []
        self.nsem = 0
        self.nops = 0

    def _deps(self, ek, reads, writes, is_dma):
        deps = {}
        own = None if is_dma else "es_" + ek
        for b in reads:
            for k, sv in b.w.items():
                if k not in deps or deps[k][1] < sv[1]:
                    deps[k] = sv
        for b in writes:
            if b.acc:
                continue
            for dd in (b.w, b.r):
                for k, sv in dd.items():
                    if k == own:
                        continue
                    if k not in deps or deps[k][1] < sv[1]:
                        deps[k] = sv
        return deps

    def _wait(self, ek, deps):
        e = self.engs[ek]
        wd = self.waited[ek]
        for k, (sem, val) in deps.items():
            if wd.get(k, 0) < val:
                e.wait_ge(sem, val)
                wd[k] = val

    def _record(self, k, sv, reads, writes):
        for b in writes:
            if b.acc:
                if k not in b.w or b.w[k][1] < sv[1]:
                    b.w[k] = sv
            else:
                b.w = {k: sv}
                b.r = {}
        for b in reads:
            if k not in b.r or b.r[k][1] < sv[1]:
                b.r[k] = sv

    def op(self, ek, fn, reads=(), writes=()):
        deps = self._deps(ek, reads, writes, False)
        self._wait(ek, deps)
        ins = fn(self.engs[ek])
        self.ecnt[ek] += 1
        ins.then_inc(self.esem[ek], 1)
        self._record("es_" + ek, (self.esem[ek], self.ecnt[ek]), reads, writes)
        self.nops += 1

    def dma(self, qk, out, in_, sb, reads=(), writes=()):
        deps = self._deps(qk, reads, writes, True)
        if sb.dsem is None:
            if self.free_sems:
                sb.dkey, sb.dsem, sb.dcnt = self.free_sems.pop()
            else:
                sb.dkey = "ds%d" % self.nsem
                self.nsem += 1
                sb.dsem = self.stack.enter_context(self.nc.semaphore(sb.dkey))
                sb.dcnt = 0
            self.dsems.append(sb)
        if sb.dcnt > 0:
            k = sb.dkey
            if k not in deps or deps[k][1] < sb.dcnt:
                deps[k] = (sb.dsem, sb.dcnt)
        self._wait(qk, deps)
        self.engs[qk].dma_start(out=out, in_=in_).then_inc(sb.dsem, 16)
        sb.dcnt += 16
        self._record(sb.dkey, (sb.dsem, sb.dcnt), reads, writes)
        self.nops += 1

    def pin(self, sb):
        sb.dkey = "dp%d" % len(self.pinned)
        sb.dsem = self.stack.enter_context(self.nc.semaphore(sb.dkey))
        sb.dcnt = 0
        self.pinned.append(sb)

    def barrier(self):
        deps = {}
        for k in self.engs:
            if self.ecnt[k] > 0:
                deps["es_" + k] = (self.esem[k], self.ecnt[k])
        for sb in self.dsems + self.pinned:
            if sb.dcnt > 0:
                deps[sb.dkey] = (sb.dsem, sb.dcnt)
        for k in self.engs:
            d2 = {kk: v for kk, v in deps.items() if kk != "es_" + k}
            self._wait(k, d2)
        for sb in self.dsems:
            self.free_sems.append((sb.dkey, sb.dsem, sb.dcnt))
            sb.dsem = None
        self.dsems = []

    def finish(self):
        deps = {}
        for k in self.engs:
            if self.ecnt[k] > 0 and k != "sp":
                deps["es_" + k] = (self.esem[k], self.ecnt[k])
        for sb in self.dsems + self.pinned:
            if sb.dcnt > 0:
                deps[sb.dkey] = (sb.dsem, sb.dcnt)
        self._wait("sp", deps)


class Ctx:
    pass


def build(debug=None, cfg=None):
    cfg = cfg or {}
    nc = bass.Bass("TRN2", target_bir_lowering=False)
    g = Ctx()
    g.nc = nc
    g.cfg = cfg
    g.debug = debug or ()

    def din(name, shape, dt=F32):
        return nc.dram_tensor(name, list(shape), dt, kind="ExternalInput").ap()

    def dout(name, shape, dt=F32):
        return nc.dram_tensor(name, list(shape), dt, kind="ExternalOutput").ap()

    def dscr(name, shape, dt=F32):
        kind = "ExternalOutput" if name in g.debug else "Internal"
        return nc.dram_tensor(name, list(shape), dt, kind=kind).ap()

    IN_SHAPES = dict(xp=[NP_TOK, D], xs=[NS_ALL, D], cT=[128, NKC * 2], ck=[PAST, 2048], cv=[PAST, 2048],
                     sf=[32 * 64, 128], sb=[32 * 64, 128], w_ada=[D, 6 * D], b_ada=[1, 6 * D], gcols=[128, 4 * NKC],
                     grows=[4, D], w_in=[D, IN_COLS], lam=[1, 4 * 128], g_subln=[1, 256], conv_w=[5, 3072],
                     conv_b=[1, 3072], a_log=[1, 64], dt_bias=[1, 64], d_skip=[1, 32], g_ssm=[1, 2048],
                     w_out=[D, D], w_up=[D, DFF], w_down=[DFF, D], ropec=[NS_ALL, 128], ropes=[NS_ALL, 128],
                     ident=[128, 128])

    class LazyIn:
        def __getattr__(self, name):
            ap = din(name, IN_SHAPES[name])
            object.__setattr__(self, name, ap)
            g.used_inputs.append(name)
            return ap
    g.used_inputs = []
    I = LazyIn()
    g.I = I
    if not cfg.get("lazy_inputs", False):
        for n_ in IN_SHAPES:
            getattr(I, n_)

    O = Ctx()
    g.O = O
    O.yp = dout("o_yp", [NP_TOK, D])
    O.ys = dout("o_ys", [NS_OWN, D])
    O.nk = dout("o_nk", [NP_TOK, 2048])
    O.nv = dout("o_nv", [NP_TOK, 2048])
    O.nsf = dout("o_nsf", [4 * 32 * 64, 128])
    O.nsb = dout("o_nsb", [4 * 32 * 64, 128])

    S = Ctx()
    g.S = S
    S.modraw = dscr("modraw", [2, 6 * D])
    S.QT = dscr("QT", [16, 128, NOWN], BF16)
    S.KT = dscr("KT", [16, 128, NT + PAST], BF16)
    S.V = dscr("V", [NT + PAST, 2048], BF16)
    S.Z = dscr("Zs", [NOWN, 2048 + g.cfg.get("zpad", 0)])
    S.XBC = dscr("XBC", [NT, 4 * 768])
    S.DT = dscr("DT", [NT, 64])
    S.MIXT = dscr("MIXT", [NKC, 128, NOWN], BF16)
    S.X1 = dscr("X1", [NOWN, D])

    with ExitStack() as stack:
        sc = Sched(nc, stack)
        g.sc = sc
        g.stack = stack
        g.ps = []
        g.psb = []
        for i in range(8):
            t = stack.enter_context(nc.psum_tensor("ps%d" % i, [128, 512], F32))
            g.ps.append(t)
            g.psb.append(Buf("ps%d" % i))
        g.ident_f = stack.enter_context(nc.sbuf_tensor("ident_f", [128, 128], F32))
        g.ident_b = stack.enter_context(nc.sbuf_tensor("ident_b", [128, 128], BF16))
        g.b_ident = Buf("ident")
        sc.dma("sp", g.ident_f[:], I.ident[:, :], g.b_ident, writes=[g.b_ident])
        sc.op("dve", lambda e: e.tensor_copy(g.ident_b[:], g.ident_f[:]), reads=[g.b_ident], writes=[g.b_ident])
        g.dbuf = {n: Buf("d_" + n, acc=True) for n in
                  ("modraw", "QT", "KT", "V", "Z", "XBC", "DT", "MIXT", "X1", "out")}

        S.WOB = dscr("WOB", [8, 128, NKC * 512], BF16)
        S.WUB = dscr("WUB", [32, 128, NKC * 512], BF16)
        S.WDB = dscr("WDB", [16, 8, 128, 8 * 512], BF16)
        g.dbuf["WB"] = Buf("d_WB", acc=True)
        g.conv_list = []
        g.conv_bufs = [Buf("cv%d" % i) for i in range(6)]
        for cb_ in g.conv_bufs:
            sc.pin(cb_)
        g.conv_i = [0]

        def conv_step(n=1):
            if not g.conv_list:
                for kc in range(NKC):
                    g.conv_list.append((S.WOB[:, :, kc * 512:(kc + 1) * 512].rearrange("c p n -> p c n"),
                                        I.w_out[kc * 128:(kc + 1) * 128, :].rearrange("p (c n) -> p c n", c=8)))
                for kc in range(NKC):
                    g.conv_list.append((S.WUB[:, :, kc * 512:(kc + 1) * 512].rearrange("c p n -> p c n"),
                                        I.w_up[kc * 128:(kc + 1) * 128, :].rearrange("p (c n) -> p c n", c=32)))
                for fb in range(16):
                    for ch in range(8):
                        r0 = (fb * 8 + ch) * 128
                        g.conv_list.append((S.WDB[fb, :, :, ch * 512:(ch + 1) * 512].rearrange("c p n -> p c n"),
                                            I.w_down[r0:r0 + 128, :].rearrange("p (c n) -> p c n", c=8)))
            for _ in range(n):
                i = g.conv_i[0]
                if i >= len(g.conv_list):
                    return
                g.conv_i[0] += 1
                o_, i_ = g.conv_list[i]
                sc.dma("pool", o_, i_, g.conv_bufs[i % len(g.conv_bufs)], writes=[g.dbuf["WB"]])
        g.conv_step = conv_step

        if cfg.get("conv_only"):
            g.conv_step(cfg["conv_only"])
            sc.barrier()
        stages = cfg.get("stages", "0123456")
        if "0" in stages:
            stage_adaln(g)
            sc.barrier()
        if "1" in stages:
            stage_inproj(g)
            sc.barrier()
        if "2" in stages:
            stage_ctxkv(g)
            sc.barrier()
        if "3" in stages:
            stage_ssd(g)
            sc.barrier()
        if "4" in stages:
            stage_attn(g)
            sc.barrier()
        if "5" in stages:
            g.conv_step(1000)
            sc.barrier()
            stage_mlp(g)
        sc.finish()
    nc.used_inputs = g.used_inputs
    return nc


def stage_adaln(g):
    nc, sc, I, S = g.nc, g.sc, g.I, g.S
    ncol = g.cfg.get("ada_tiles", 48)
    with ExitStack() as st:
        cT = st.enter_context(nc.sbuf_tensor("a_cT", [128, NKC * 2], F32))
        sg = st.enter_context(nc.sbuf_tensor("a_sg", [128, NKC * 2], F32))
        L = st.enter_context(nc.sbuf_tensor("a_L", [128, NKC * 2, 128], BF16))
        brow = st.enter_context(nc.sbuf_tensor("a_brow", [1, 6 * D], F32))
        W = [st.enter_context(nc.sbuf_tensor("a_W%d" % i, [128, NKC, 512], BF16)) for i in range(2)]
        ot = [st.enter_context(nc.sbuf_tensor("a_ot%d" % i, [1, 512], F32)) for i in range(4)]
        b_c, b_L, b_brow = Buf("a_c"), Buf("a_L"), Buf("a_brow")
        b_W = [Buf("a_W0"), Buf("a_W1")]
        b_ot = [Buf("a_ot%d" % i) for i in range(4)]
        sc.dma("sp", cT[:], I.cT[:, :], b_c, writes=[b_c])
        sc.dma("sp", brow[:], I.b_ada[:, :], b_brow, writes=[b_brow])
        sc.op("act", lambda e: e.activation(out=sg[:], in_=cT[:], func=AF.Sigmoid), reads=[b_c], writes=[b_L])
        sc.op("dve", lambda e: e.tensor_tensor(out=sg[:], in0=sg[:], in1=cT[:], op=ALU.mult), reads=[b_c, b_L], writes=[b_L])
        sc.op("dve", lambda e: e.tensor_copy(out=L[:], in_=sg[:].unsqueeze(2).to_broadcast([128, NKC * 2, 128])),
              reads=[b_L], writes=[b_L])
        wv = I.w_ada.rearrange("(kc p) n -> p kc n", p=128)
        k = 0
        for n in range(ncol):
            wi = n % 2
            sc.dma("pool", W[wi][:], wv[:, :, n * 512:(n + 1) * 512], b_W[wi], writes=[b_W[wi]])
            for m in range(2):
                pb = 4 + (k % 2)
                oi = k % 4
                k += 1

                def mm(e, m=m, wi=wi, pb=pb):
                    ins = None
                    for kc in range(NKC):
                        ins = e.matmul(g.ps[pb][:, :], lhsT=L[:, kc * 2 + m, :], rhs=W[wi][:, kc, :],
                                       start=(kc == 0), stop=(kc == NKC - 1))
                    return ins
                sc.op("pe", mm, reads=[b_L, b_W[wi]], writes=[g.psb[pb]])
                sc.op("dve", lambda e, pb=pb, oi=oi, n=n: e.tensor_tensor(
                    out=ot[oi][:], in0=g.ps[pb][0:1, :], in1=brow[0:1, n * 512:(n + 1) * 512], op=ALU.add),
                    reads=[g.psb[pb], b_brow], writes=[b_ot[oi]])
                sc.dma("sp", S.modraw[m:m + 1, n * 512:(n + 1) * 512], ot[oi][:], b_ot[oi],
                       reads=[b_ot[oi]], writes=[g.dbuf["modraw"]])


def load_modcols(g, st, mode, which_g, i_sc, i_sh, name):
    nc, sc, I, S = g.nc, g.sc, g.I, g.S
    t = st.enter_context(nc.sbuf_tensor(name, [128, 3, NKC], F32))
    rr = st.enter_context(nc.sbuf_tensor(name + "_r", [32, 2, 128], F32))
    b = Buf(name)
    b_rr = Buf(name + "_r")
    sc.dma("sp", rr[:, 0, :], S.modraw[mode, i_sc * D:(i_sc + 1) * D].rearrange("(kc p) -> kc p", p=128), b_rr,
           reads=[g.dbuf["modraw"]], writes=[b_rr])
    sc.dma("sp", rr[:, 1, :], S.modraw[mode, i_sh * D:(i_sh + 1) * D].rearrange("(kc p) -> kc p", p=128), b_rr,
           reads=[g.dbuf["modraw"]], writes=[b_rr])

    def tr(e):
        e.transpose(g.ps[7][:, 0:32], rr[:, 0, :], g.ident_f[0:32, 0:32])
        return e.transpose(g.ps[7][:, 32:64], rr[:, 1, :], g.ident_f[0:32, 0:32])
    sc.op("pe", tr, reads=[b_rr, g.b_ident], writes=[g.psb[7]])
    sc.op("dve", lambda e: e.tensor_copy(out=t[:, 0:2, :].rearrange("p a k -> p (a k)"), in_=g.ps[7][:, 0:64]),
          reads=[g.psb[7]], writes=[b])
    sc.dma("sp", t[:, 2, :], I.gcols[:, which_g * NKC:(which_g + 1) * NKC], b, writes=[b])
    sc.op("dve", lambda e: e.scalar_tensor_tensor(out=t[:, 0, :], in0=t[:, 0, :], scalar=1.0, in1=t[:, 2, :],
                                                  op0=ALU.add, op1=ALU.mult), reads=[b], writes=[b])
    return t, b


def norm_transpose(g, xt, b_x, modc, b_modc, hT, b_hT, tt, scratch, b_scr, junk, b_junk):
    sc = g.sc
    ss, rstd = scratch
    sc.op("act", lambda e: e.activation(out=junk[:], in_=xt[:], func=AF.Square, accum_out=ss[:]),
          reads=[b_x], writes=[b_scr, b_junk])
    sc.op("act", lambda e: e.activation(out=rstd[:], in_=ss[:], func=AF.Sqrt, scale=1.0 / D, bias=EPS),
          reads=[b_scr], writes=[b_scr])
    sc.op("dve", lambda e: e.reciprocal(out=rstd[:], in_=rstd[:]), reads=[b_scr], writes=[b_scr])
    sc.op("dve", lambda e: e.tensor_scalar(out=xt[:], in0=xt[:], scalar1=rstd[:, 0:1], scalar2=None, op0=ALU.mult),
          reads=[b_x, b_scr], writes=[b_x])
    for q in range(8):
        pb = 4 + (q % 2)

        def tr(e, q=q, pb=pb):
            ins = None
            for i in range(4):
                kc = q * 4 + i
                ins = e.transpose(g.ps[pb][:, i * 128:(i + 1) * 128], xt[:, kc * 128:(kc + 1) * 128], g.ident_f[:])
            return ins
        sc.op("pe", tr, reads=[b_x, g.b_ident], writes=[g.psb[pb]])
        for i in range(4):
            kc = q * 4 + i
            ek = "act" if (i % 2 == 0) else "dve"
            if ek == "act":
                sc.op("act", lambda e, kc=kc, i=i, pb=pb: e.activation(
                    out=hT[:, kc, tt * 128:(tt + 1) * 128], in_=g.ps[pb][:, i * 128:(i + 1) * 128],
                    func=AF.Identity, scale=modc[:, 0, kc:kc + 1], bias=modc[:, 1, kc:kc + 1]),
                    reads=[g.psb[pb], b_modc], writes=[b_hT])
            else:
                sc.op("dve", lambda e, kc=kc, i=i, pb=pb: e.tensor_scalar(
                    out=hT[:, kc, tt * 128:(tt + 1) * 128], in0=g.ps[pb][:, i * 128:(i + 1) * 128],
                    scalar1=modc[:, 0, kc:kc + 1], scalar2=modc[:, 1, kc:kc + 1], op0=ALU.mult, op1=ALU.add),
                    reads=[g.psb[pb], b_modc], writes=[b_hT])


def tok_src(g, t0):
    if t0 < NP_TOK:
        return g.I.xp[t0:t0 + 128, :]
    return g.I.xs[t0 - NP_TOK:t0 - NP_TOK + 128, :]


def stage_inproj(g):
    nc, sc, I, S, O = g.nc, g.sc, g.I, g.S, g.O
    GT = 1024
    NTT = GT // 128
    ngroups = g.cfg.get("in_groups", list(range(NT // GT)))
    with ExitStack() as st:
        xt = st.enter_context(nc.sbuf_tensor("i_x", [128, D], F32))
        junk = st.enter_context(nc.sbuf_tensor("i_junk", [128, D], BF16))
        hT = st.enter_context(nc.sbuf_tensor("i_hT", [128, NKC, GT], BF16))
        W = [st.enter_context(nc.sbuf_tensor("i_W%d" % i, [128, NKC, 512], BF16)) for i in range(2)]
        ss = st.enter_context(nc.sbuf_tensor("i_ss", [128, 1], F32))
        rstd = st.enter_context(nc.sbuf_tensor("i_rstd", [128, 1], F32))
        NF = 4
        sf32 = [st.enter_context(nc.sbuf_tensor("i_sf%d" % i, [128, 512], F32)) for i in range(NF)]
        sb16 = [st.enter_context(nc.sbuf_tensor("i_sb%d" % i, [128, 512], BF16)) for i in range(NF)]
        rt1 = st.enter_context(nc.sbuf_tensor("i_rt1", [128, 512], F32))
        rt2 = st.enter_context(nc.sbuf_tensor("i_rt2", [128, 512], F32))
        qts = [st.enter_context(nc.sbuf_tensor("i_qts%d" % i, [128, 4, GT], BF16)) for i in range(1)]
        rc = st.enter_context(nc.sbuf_tensor("i_rc", [128, NTT, 128], F32))
        rs = st.enter_context(nc.sbuf_tensor("i_rs", [128, NTT, 128], F32))
        b_x, b_junk, b_hT, b_scr = Buf("i_x"), Buf("i_junk"), Buf("i_hT"), Buf("i_scr")
        b_W = [Buf("i_W0"), Buf("i_W1")]
        b_sf = [Buf("i_sf%d" % i) for i in range(NF)]
        b_sb = [Buf("i_sb%d" % i) for i in range(NF)]
        b_rt, b_rope = Buf("i_rt"), Buf("i_rope")
        b_qts = [Buf("i_qts0")]
        modc = {}
        modc[0] = load_modcols(g, st, 0, 0, 1, 0, "i_modp")
        modc[1] = load_modcols(g, st, 1, 0, 1, 0, "i_mods")
        wv = I.w_in.rearrange("(kc p) n -> p kc n", p=128)
        cnt = dict(w=0, ps=0, sf=0, sb=0, q=0)
        pending = []

        def qk_post(xb, bi, tb, TP, qi, tt, kind, ci, t0g):
            def tr(e):
                ins = None
                for bk in range(4):
                    ins = e.transpose(TP[:, bk * 128:(bk + 1) * 128], xb[:, bk * 128:(bk + 1) * 128], g.ident_b[:])
                return ins
            sc.op("pe", tr, reads=[b_sb[bi], g.b_ident], writes=[g.psb[tb]])
            sc.op("act", lambda e: e.copy(out=qts[qi][:, :, tt * 128:(tt + 1) * 128],
                                          in_=TP[:, 0:512].rearrange("p (b t) -> p b t", b=4)),
                  reads=[g.psb[tb]], writes=[b_qts[qi]])
            if tt == NTT - 1:
                dst = S.QT if kind == "q" else S.KT
                sc.dma("sp", dst[ci * 4:(ci + 1) * 4, :, t0g:t0g + GT].rearrange("b d t -> d b t"),
                       qts[qi][:], b_qts[qi], reads=[b_qts[qi]], writes=[g.dbuf["QT" if kind == "q" else "KT"]])
        for grp in ngroups:
            t0g = grp * GT
            prompt = t0g < NP_TOK
            remote = t0g >= NOWN
            mode = 0 if prompt else 1
            for tt in range(NTT):
                sc.dma("sp", xt[:], tok_src(g, t0g + tt * 128), b_x, writes=[b_x])
                norm_transpose(g, xt, b_x, modc[mode][0], modc[mode][1], hT, b_hT, tt, (ss, rstd), b_scr, junk, b_junk)
            if not prompt:
                ls = t0g - NP_TOK
                sc.dma("sp", rc[:], I.ropec[ls:ls + GT, :].rearrange("(t p) d -> p t d", p=128), b_rope, writes=[b_rope])
                sc.dma("sp", rs[:], I.ropes[ls:ls + GT, :].rearrange("(t p) d -> p t d", p=128), b_rope, writes=[b_rope])
            tiles = []
            if not remote:
                tiles += [("q", c * 512, 512, c) for c in range(4)]
            tiles += [("k", 2048 + c * 512, 512, c) for c in range(4)]
            tiles += [("v", 4096 + c * 512, 512, c) for c in range(4)]
            if not remote:
                tiles += [("z", 6144 + c * 512, 512, c) for c in range(4)]
            tiles += [("x", 8192 + c * 512, 512, c) for c in range(4)]
            tiles += [("B", 8192 + 2048, 512, 0)]
            tiles += [("C", 8192 + 2560, 512, 0)]
            tiles += [("dt", 11264, 64, 0)]
            kinds_ok = g.cfg.get('in_kinds')
            for (kind, c0, ncols, ci) in tiles:
                if kinds_ok is not None and kind not in kinds_ok:
                    continue
                wi = cnt["w"] % 2
                cnt["w"] += 1
                sc.dma("pool", W[wi][:, :, 0:ncols], wv[:, :, c0:c0 + ncols], b_W[wi], writes=[b_W[wi]])
                qi = None
                if kind in ("q", "k"):
                    qi = 0
                    cnt["q"] += 1
                for tt in range(NTT):
                    t0 = t0g + tt * 128
                    pb = cnt["ps"] % 4
                    cnt["ps"] += 1

                    def mm(e, tt=tt, wi=wi, pb=pb, ncols=ncols):
                        ins = None
                        for kc in range(NKC):
                            ins = e.matmul(g.ps[pb][:, 0:ncols], lhsT=hT[:, kc, tt * 128:(tt + 1) * 128],
                                           rhs=W[wi][:, kc, 0:ncols], start=(kc == 0), stop=(kc == NKC - 1))
                        return ins
                    sc.op("pe", mm, reads=[b_hT, b_W[wi]], writes=[g.psb[pb]])
                    while pending:
                        pending.pop(0)()
                    P = g.ps[pb]
                    bP = g.psb[pb]
                    if kind in ("q", "k"):
                        bi = cnt["sb"] % NF
                        cnt["sb"] += 1
                        xb = sb16[bi]
                        if prompt:
                            if kind == "k":
                                fi = cnt["sf"] % NF
                                cnt["sf"] += 1
                                sc.op("act", lambda e, fi=fi, P=P: e.copy(out=sf32[fi][:], in_=P[:, :]),
                                      reads=[bP], writes=[b_sf[fi]])
                                sc.dma("sp", O.nk[t0:t0 + 128, ci * 512:(ci + 1) * 512], sf32[fi][:], b_sf[fi],
                                       reads=[b_sf[fi]], writes=[g.dbuf["out"]])
                                sc.op("dve", lambda e, xb=xb, fi=fi: e.tensor_copy(out=xb[:], in_=sf32[fi][:]),
                                      reads=[b_sf[fi]], writes=[b_sb[bi]])
                            else:
                                sc.op("dve", lambda e, xb=xb, P=P: e.tensor_copy(out=xb[:], in_=P[:, :]),
                                      reads=[bP], writes=[b_sb[bi]])
                        else:
                            Cb = rc[:, tt, :].unsqueeze(1).to_broadcast([128, 4, 128])
                            sc.op("dve", lambda e, P=P, Cb=Cb: e.tensor_tensor(
                                out=rt1[:].rearrange("p (b d) -> p b d", b=4), in0=P[:, :].rearrange("p (b d) -> p b d", b=4),
                                in1=Cb, op=ALU.mult), reads=[bP, b_rope], writes=[b_rt])
                            Pv = P[:, :].rearrange("p (b a h d) -> p b a h d", b=4, a=2, h=2)
                            t2v = rt2[:].rearrange("p (b a h d) -> p b a h d", b=4, a=2, h=2)
                            Sv = rs[:, tt, :].rearrange("p (a h d) -> p a h d", a=2, h=2)
                            for hh in range(2):
                                sc.op("dve", lambda e, hh=hh, Pv=Pv, t2v=t2v, Sv=Sv: e.tensor_tensor(
                                    out=t2v[:, :, :, hh, :], in0=Pv[:, :, :, 1 - hh, :],
                                    in1=Sv[:, :, hh, :].unsqueeze(1).to_broadcast([128, 4, 2, 32]), op=ALU.mult),
                                    reads=[bP, b_rope, b_rt], writes=[b_rt])
                            sc.op("pool", lambda e, xb=xb: e.tensor_tensor(out=xb[:], in0=rt1[:], in1=rt2[:], op=ALU.add),
                                  reads=[b_rt], writes=[b_sb[bi]])
                        tb = 6 + (cnt["ps"] % 2)
                        TP = g.ps[tb][:, :].bitcast(BF16)
                        pending.append(lambda xb=xb, bi=bi, tb=tb, TP=TP, qi=qi, tt=tt, kind=kind, ci=ci, t0g=t0g:
                                       qk_post(xb, bi, tb, TP, qi, tt, kind, ci, t0g))
                        continue

                        def tr(e, xb=xb, TP=TP):
                            ins = None
                            for bk in range(4):
                                ins = e.transpose(TP[:, bk * 128:(bk + 1) * 128], xb[:, bk * 128:(bk + 1) * 128], g.ident_b[:])
                            return ins
                        sc.op("pe", tr, reads=[b_sb[bi], g.b_ident], writes=[g.psb[tb]])
                        sc.op("act", lambda e, TP=TP, qi=qi, tt=tt: e.copy(
                            out=qts[qi][:, :, tt * 128:(tt + 1) * 128],
                            in_=TP[:, 0:512].rearrange("p (b t) -> p b t", b=4)),
                            reads=[g.psb[tb]], writes=[b_qts[qi]])
                        if tt == NTT - 1:
                            dst = S.QT if kind == "q" else S.KT
                            sc.dma("sp", dst[ci * 4:(ci + 1) * 4, :, t0g:t0g + GT].rearrange("b d t -> d b t"),
                                   qts[qi][:], b_qts[qi], reads=[b_qts[qi]], writes=[g.dbuf["QT" if kind == "q" else "KT"]])
                    elif kind == "v":
                        if prompt and not g.cfg.get("skip_nv"):
                            fi = cnt["sf"] % NF
                            cnt["sf"] += 1
                            sc.op("act", lambda e, fi=fi, P=P: e.copy(out=sf32[fi][:], in_=P[:, :]),
                                  reads=[bP], writes=[b_sf[fi]])
                            sc.dma("sp", O.nv[t0:t0 + 128, ci * 512:(ci + 1) * 512], sf32[fi][:], b_sf[fi],
                                   reads=[b_sf[fi]], writes=[g.dbuf["out"]])
                        bi = cnt["sb"] % NF
                        cnt["sb"] += 1
                        if prompt and not g.cfg.get("skip_nv"):
                            sc.op("dve", lambda e, bi=bi, fi=fi: e.tensor_copy(out=sb16[bi][:], in_=sf32[fi][:]),
                                  reads=[b_sf[fi]], writes=[b_sb[bi]])
                        else:
                            sc.op("dve", lambda e, bi=bi, P=P: e.tensor_copy(out=sb16[bi][:], in_=P[:, :]),
                                  reads=[bP], writes=[b_sb[bi]])
                        if not g.cfg.get("skip_V"):
                            sc.dma("sp", S.V[t0:t0 + 128, ci * 512:(ci + 1) * 512], sb16[bi][:], b_sb[bi],
                                   reads=[b_sb[bi]], writes=[g.dbuf["V"]])
                    else:
                        fi = cnt["sf"] % NF
                        cnt["sf"] += 1
                        ek = "act" if (cnt["sf"] % 2 == 0) else "dve"
                        if ek == "act":
                            sc.op("act", lambda e, fi=fi, P=P, ncols=ncols: e.copy(out=sf32[fi][:, 0:ncols], in_=P[:, 0:ncols]),
                                  reads=[bP], writes=[b_sf[fi]])
                        else:
                            sc.op("dve", lambda e, fi=fi, P=P, ncols=ncols: e.tensor_copy(out=sf32[fi][:, 0:ncols], in_=P[:, 0:ncols]),
                                  reads=[bP], writes=[b_sf[fi]])
                        if kind == "z":
                            dst = S.Z[t0:t0 + 128, ci * 512:(ci + 1) * 512]
                            src = sf32[fi][:]
                            dn = "Z"
                        elif kind == "x":
                            dst = S.XBC[t0:t0 + 128, ci * 768:ci * 768 + 512]
                            src = sf32[fi][:]
                            dn = "XBC"
                        elif kind in ("B", "C"):
                            off = 512 if kind == "B" else 640
                            dst = S.XBC[t0:t0 + 128, :].rearrange("t (g c) -> t g c", g=4)[:, :, off:off + 128]
                            src = sf32[fi][:].rearrange("p (g c) -> p g c", g=4)
                            dn = "XBC"
                        else:
                            dst = S.DT[t0:t0 + 128, :]
                            src = sf32[fi][:, 0:64]
                            dn = "DT"
                        sc.dma(g.cfg.get("zq", "sp"), dst, src, b_sf[fi], reads=[b_sf[fi]], writes=[g.dbuf[dn]])
            while pending:
                pending.pop(0)()


def _rope_tables(pos):
    pos = np.asarray(pos)
    row = (pos // 64).astype(np.float32)
    col = (pos % 64).astype(np.float32)
    inv = (1.0 / (np.float32(10000.0) ** (np.arange(0, 64, 2, dtype=np.float32) / np.float32(64)))).astype(np.float32)
    ar = row[:, None] * inv[None, :]
    ac = col[:, None] * inv[None, :]
    cr, sr, cc, s_c = np.cos(ar), np.sin(ar), np.cos(ac), np.sin(ac)
    C = np.concatenate([cr, cr, cc, cc], axis=1).astype(np.float32)
    Ssg = np.concatenate([-sr, sr, -s_c, s_c], axis=1).astype(np.float32)
    return np.ascontiguousarray(C), np.ascontiguousarray(Ssg)


def prep_inputs(inp):
    f = lambda a: np.ascontiguousarray(np.asarray(a, dtype=np.float32))
    x_prompt, x_sample = f(inp["x_prompt"]), f(inp["x_sample"])
    w_in = f(inp["w_in"])[0]
    w_in_odd = w_in.copy()
    w_in_odd[:, 11264:11296] = w_in[:, 11296:11328]
    w_in_odd[:, 11296:11328] = w_in[:, 11264:11296]
    shared = dict(
        w_ada=f(inp["w_ada"])[0], b_ada=f(inp["b_ada"]).reshape(1, -1),
        w_out=f(inp["w_out"])[0], w_up=f(inp["w_up"])[0], w_down=f(inp["w_down"])[0],
        lam=np.concatenate([f(inp[k]).reshape(-1) for k in ("lambda_q1", "lambda_k1", "lambda_q2", "lambda_k2")]).reshape(1, -1),
        g_subln=f(inp["g_subln"]).reshape(1, -1), conv_b=f(inp["conv_b"]).reshape(1, -1),
        d_skip=f(inp["d_skip"]).reshape(1, -1), g_ssm=f(inp["g_ssm_norm"]).reshape(1, -1),
        ident=np.eye(128, dtype=np.float32),
    )
    grows = np.stack([f(inp[k])[0] for k in ("g_mix_pre", "g_mix_post", "g_mlp_pre", "g_mlp_post")])
    shared["grows"] = np.ascontiguousarray(grows)
    shared["gcols"] = np.ascontiguousarray(grows.reshape(4, NKC, 128).transpose(2, 0, 1).reshape(128, 4 * NKC))
    conv_w = f(inp["conv_w"])[0]
    a_log, dt_bias = f(inp["a_log"])[0], f(inp["dt_bias"])[0]
    maps = []
    for i in range(8):
        b, hf = i // 2, i % 2
        rev = hf == 1
        m = dict(shared)
        xp = x_prompt[4 * i:4 * i + 4]
        xs = x_sample[b]
        pos = np.arange(NS_ALL)
        if rev:
            xp = xp[:, ::-1]
            xs = xs[::-1]
            pos = pos[::-1]
        m["xp"] = np.ascontiguousarray(xp.reshape(NP_TOK, D))
        m["xs"] = np.ascontiguousarray(xs)
        cc = np.stack([f(inp["c_ctx"]), f(inp["c"])[b]])
        m["cT"] = np.ascontiguousarray(cc.reshape(2, NKC, 128).transpose(2, 1, 0).reshape(128, NKC * 2))
        m["ck"] = np.ascontiguousarray(f(inp["cache_k"])[b, 0].reshape(PAST, 2048))
        m["cv"] = np.ascontiguousarray(f(inp["cache_v"])[b, 0].reshape(PAST, 2048))
        sfw = f(inp["state_ssm_fwd"])[b, 0].reshape(32 * 64, 128)
        sbw = f(inp["state_ssm_bwd"])[b, 0].reshape(32 * 64, 128)
        m["sf"], m["sb"] = (sbw, sfw) if rev else (sfw, sbw)
        m["w_in"] = w_in_odd if rev else w_in
        m["conv_w"] = np.ascontiguousarray(conv_w[::-1]) if rev else conv_w
        m["a_log"] = np.ascontiguousarray((a_log[::-1] if rev else a_log).reshape(1, 64))
        m["dt_bias"] = np.ascontiguousarray((dt_bias[::-1] if rev else dt_bias).reshape(1, 64))
        m["ropec"], m["ropes"] = _rope_tables(pos)
        maps.append(m)
    return maps


_NC_CACHE = {}


def kernel(**inputs):
    maps = prep_inputs(inputs)
    if "nc" not in _NC_CACHE:
        _NC_CACHE["nc"] = build()
    nc = _NC_CACHE["nc"]
    res = run_bass_kernel_spmd(nc, maps, core_ids=list(range(8)))
    R = res.results
    yp = np.zeros((32, 256, D), np.float32)
    ys = np.zeros((4, NS_ALL, D), np.float32)
    nk = np.zeros((32, 1, 256, 8, 2, 128), np.float32)
    nv = np.zeros((32, 1, 256, 8, 256), np.float32)
    nsf = np.zeros((32, 1, 32, 64, 128), np.float32)
    nsb = np.zeros((32, 1, 32, 64, 128), np.float32)
    for i in range(8):
        b, hf = i // 2, i % 2
        r = R[i]
        ypc = np.asarray(r["o_yp"]).reshape(4, 256, D)
        ysc = np.asarray(r["o_ys"])
        nkc = np.asarray(r["o_nk"]).reshape(4, 256, 8, 2, 128)
        nvc = np.asarray(r["o_nv"]).reshape(4, 256, 8, 256)
        f_ = np.asarray(r["o_nsf"]).reshape(4, 32, 64, 128)
        b_ = np.asarray(r["o_nsb"]).reshape(4, 32, 64, 128)
        if hf == 1:
            ypc, nkc, nvc, ysc = ypc[:, ::-1], nkc[:, ::-1], nvc[:, ::-1], ysc[::-1]
            f_, b_ = b_, f_
            ys[b, 2048:] = ysc
        else:
            ys[b, :2048] = ysc
        yp[4 * i:4 * i + 4] = ypc
        nk[4 * i:4 * i + 4, 0] = nkc
        nv[4 * i:4 * i + 4, 0] = nvc
        nsf[4 * i:4 * i + 4, 0] = f_
        nsb[4 * i:4 * i + 4, 0] = b_
    return yp, ys, nk, nv, nsf, nsb


def stage_ctxkv(g):
    nc, sc, I, S = g.nc, g.sc, g.I, g.S
    with ExitStack() as st:
        xf = st.enter_context(nc.sbuf_tensor("c_xf", [128, 2048], F32))
        xb = st.enter_context(nc.sbuf_tensor("c_xb", [128, 2048], BF16))
        kts = st.enter_context(nc.sbuf_tensor("c_kts", [128, 16, 128], BF16))
        b_xf, b_xb, b_kts = Buf("c_xf"), Buf("c_xb"), Buf("c_kts")
        for kt in range(PAST // 128):
            r0 = kt * 128
            sc.dma("sp", xf[:], I.ck[r0:r0 + 128, :], b_xf, writes=[b_xf])
            sc.op("dve", lambda e: e.tensor_copy(out=xb[:], in_=xf[:]), reads=[b_xf], writes=[b_xb])
            for q in range(2):
                tb = 6 + q
                TP = g.ps[tb][:, :].bitcast(BF16)

                def tr(e, q=q, TP=TP):
                    ins = None
                    for i in range(8):
                        bk = q * 8 + i
                        ins = e.transpose(TP[:, i * 128:(i + 1) * 128], xb[:, bk * 128:(bk + 1) * 128], g.ident_b[:])
                    return ins
                sc.op("pe", tr, reads=[b_xb, g.b_ident], writes=[g.psb[tb]])
                sc.op("act", lambda e, q=q, TP=TP: e.copy(out=kts[:, q * 8:(q + 1) * 8, :],
                                                          in_=TP[:, :].rearrange("p (b t) -> p b t", b=8)),
                      reads=[g.psb[tb]], writes=[b_kts])
            sc.dma("sp", S.KT[:, :, NT + r0:NT + r0 + 128].rearrange("b d t -> d b t"), kts[:], b_kts,
                   reads=[b_kts], writes=[g.dbuf["KT"]])
            sc.dma("sp", xf[:], I.cv[r0:r0 + 128, :], b_xf, writes=[b_xf])
            sc.op("dve", lambda e: e.tensor_copy(out=xb[:], in_=xf[:]), reads=[b_xf], writes=[b_xb])
            sc.dma("sp", S.V[NT + r0:NT + r0 + 128, :], xb[:], b_xb, reads=[b_xb], writes=[g.dbuf["V"]])


def stage_ssd(g):
    nc, sc, I, S, O = g.nc, g.sc, g.I, g.S, g.O
    seqs = g.cfg.get("ssd_seqs", [0, 1, 2, 3, 4])
    groups = g.cfg.get("ssd_groups", [0, 1, 2, 3])
    NCH = 16
    with ExitStack() as st:
        def sb(name, shape, dt=F32):
            return st.enter_context(nc.sbuf_tensor("s_" + name, shape, dt))
        Uf, Ub = sb("Uf", [128, 128]), sb("Ub", [128, 128])
        Mf, Mb = sb("Mf", [128, 8, 128]), sb("Mb", [128, 8, 128])
        ones = sb("ones", [128, 128])
        b_const = Buf("s_const")

        def mk_consts(e):
            e.memset(ones[:], 1.0)
            e.memset(Uf[:], 1.0)
            e.memset(Ub[:], 1.0)
            e.memset(Mf[:], 0.0)
            e.memset(Mb[:], 0.0)
            e.affine_select(out=Uf[:], in_=Uf[:], pattern=[[1, 128]], compare_op=ALU.is_ge, fill=0.0, base=0, channel_multiplier=-1)
            e.affine_select(out=Ub[:], in_=Ub[:], pattern=[[-1, 128]], compare_op=ALU.is_ge, fill=0.0, base=0, channel_multiplier=1)
            e.affine_select(out=Mf[:], in_=Mf[:], pattern=[[0, 8], [1, 128]], compare_op=ALU.is_ge, fill=-30000.0, base=0, channel_multiplier=-1)
            return e.affine_select(out=Mb[:], in_=Mb[:], pattern=[[0, 8], [-1, 128]], compare_op=ALU.is_ge, fill=-30000.0, base=0, channel_multiplier=1)
        sc.op("pool", mk_consts, writes=[b_const])
        dtb, Ab, dsk = sb("dtb", [128, 64]), sb("Ab", [128, 64]), sb("dsk", [128, 32])
        sc.dma("sp", dtb[:], I.dt_bias[0, :].partition_broadcast(128), b_const, writes=[b_const])
        sc.dma("sp", Ab[:], I.a_log[0, :].partition_broadcast(128), b_const, writes=[b_const])
        sc.dma("sp", dsk[:], I.d_skip[0, :].partition_broadcast(128), b_const, writes=[b_const])
        sc.op("act", lambda e: e.activation(out=Ab[:], in_=Ab[:], func=AF.Exp), reads=[b_const], writes=[b_const])
        sc.op("dve", lambda e: e.tensor_scalar(out=Ab[:], in0=Ab[:], scalar1=-1.0, scalar2=None, op0=ALU.mult),
              reads=[b_const], writes=[b_const])
        W5, cbias, gssm = sb("W5", [128, 5, 768]), sb("cbias", [128, 768]), sb("gssm", [128, 512])
        b_gc = Buf("s_gc")
        xs = [[sb("xs%d_%d" % (i, j), [128, 768]) for j in range(5)] for i in range(2)]
        b_xs = [[Buf("s_xs%d_%d" % (i, j)) for j in range(5)] for i in range(2)]
        acc, xc = sb("acc", [128, 768]), sb("xc", [128, 768])
        xcb = sb("xcb", [128, 768], BF16)
        b_acc, b_xc = Buf("s_acc"), Buf("s_xc")
        dtr = [sb("dtr%d" % i, [128, 64]) for i in range(2)]
        b_dtr = [Buf("s_dtr0"), Buf("s_dtr1")]
        sm = {n: sb(n, [128, 64]) for n in ("ab", "ex", "dt", "a", "tot", "cs", "ncs", "ecs", "dcs", "cdec", "w2")}
        b_sm = Buf("s_sm")
        BT, CT = sb("BT", [128, 128], BF16), sb("CT", [128, NCH, 128], BF16)
        b_BT, b_CT = Buf("s_BT"), [Buf("s_CT%d" % i) for i in range(NCH)]
        AU = [sb("AU%d" % d, [128, 8, 128]) for d in range(2)]
        b_AU = [Buf("s_AU0"), Buf("s_AU1")]
        Lm = [sb("Lm%d" % d, [128, 8, 128]) for d in range(2)]
        b_Lm = [Buf("s_Lm0"), Buf("s_Lm1")]
        wts = [sb("wts%d" % d, [128, 8, 128], BF16) for d in range(2)]
        b_wts = [Buf("s_wts0"), Buf("s_wts1")]
        cbT = sb("cbT", [128, 128])
        b_cbT = Buf("s_cbT")
        xdt = [sb("xdt%d" % d, [128, 512], BF16) for d in range(2)]
        xdd = [sb("xdd%d" % d, [128, 512], BF16) for d in range(2)]
        b_xdt, b_xdd = [Buf("s_xdt0"), Buf("s_xdt1")], [Buf("s_xdd0"), Buf("s_xdd1")]
        state = [sb("state%d" % d, [128, 512]) for d in range(2)]
        prevb = [sb("prevb%d" % d, [128, 512], BF16) for d in range(2)]
        b_state = [Buf("s_state0"), Buf("s_state1")]
        ysum = sb("ysum", [128, NCH, 512])
        Sb_st = sb("Sbst", [128, NCH, 512])
        ecs_st = sb("ecsst", [128, NCH, 8])
        cdec_st = sb("cdecst", [128, NCH, 8])
        b_ch = [Buf("s_ch%d" % i) for i in range(NCH)]
        tmp = sb("tmp", [128, 512])
        b_tmp = Buf("s_tmp")
        zt = sb("zt", [128, 512])
        b_zt = Buf("s_zt")
        yb = sb("yb", [128, 512], BF16)
        b_yb = Buf("s_yb")
        mxs = sb("mxs", [128, 4, 128], BF16)
        b_mxs = Buf("s_mxs")
        stT = sb("stT", [128, 4, 128])
        b_stT = Buf("s_stT")
        stin = sb("stin", [128, 4, 128])
        b_stin = Buf("s_stin")
        nrm = sb("nrm", [128, 2])
        b_nrm = Buf("s_nrm")
        junk = sb("junk", [128, 512])
        b_junk = Buf("s_junk")
        xsi = [0]

        def load_shift(t0, seq_lo, seq_hi, gi, ncols=768):
            i = xsi[0] % 2
            xsi[0] += 1
            for j in range(5):
                lo = t0 + j - 2
                hi = lo + 128
                p0 = max(0, seq_lo - lo)
                p1 = 128 - max(0, hi - seq_hi)
                if p0 > 0 or p1 < 128:
                    sc.op("pool", lambda e, i=i, j=j: e.memset(xs[i][j][:], 0.0), writes=[b_xs[i][j]])
                sc.dma("sp", xs[i][j][p0:p1, 0:ncols], S.XBC[lo + p0:lo + p1, gi * 768:gi * 768 + ncols], b_xs[i][j],
                       reads=[g.dbuf["XBC"]], writes=[b_xs[i][j]])
            return i

        def conv_silu(i, ncols=768):
            for j in range(5):
                ek = "pool" if j % 2 == 0 else "dve"
                sc.op(ek, lambda e, j=j: e.tensor_tensor(out=xs[i][j][:, 0:ncols], in0=xs[i][j][:, 0:ncols],
                                                          in1=W5[:, j, 0:ncols], op=ALU.mult),
                      reads=[b_xs[i][j], b_gc], writes=[b_xs[i][j]])
            sc.op("dve", lambda e: e.tensor_tensor(out=acc[:, 0:ncols], in0=xs[i][0][:, 0:ncols], in1=xs[i][1][:, 0:ncols], op=ALU.add),
                  reads=[b_xs[i][0], b_xs[i][1]], writes=[b_acc])
            for j in (2, 3, 4):
                sc.op("dve", lambda e, j=j: e.tensor_tensor(out=acc[:, 0:ncols], in0=acc[:, 0:ncols], in1=xs[i][j][:, 0:ncols], op=ALU.add),
                      reads=[b_xs[i][j], b_acc], writes=[b_acc])
            sc.op("dve", lambda e: e.tensor_tensor(out=acc[:, 0:ncols], in0=acc[:, 0:ncols], in1=cbias[:, 0:ncols], op=ALU.add),
                  reads=[b_acc, b_gc], writes=[b_acc])
            sc.op("act", lambda e: e.activation(out=xc[:, 0:ncols], in_=acc[:, 0:ncols], func=AF.Silu), reads=[b_acc], writes=[b_xc])
            sc.op("dve", lambda e: e.tensor_copy(out=xcb[:, 0:ncols], in_=xc[:, 0:ncols]), reads=[b_xc], writes=[b_xc])

        def dt_stuff(t0, gi, dirs):
            di = xsi[0] % 2
            sc.dma("sp", dtr[di][:], S.DT[t0:t0 + 128, :], b_dtr[di], reads=[g.dbuf["DT"]], writes=[b_dtr[di]])
            R = [b_dtr[di], b_const, b_sm]
            Wr = [b_sm]
            s_ = sm
            sc.op("dve", lambda e: e.tensor_tensor(out=s_["dt"][:], in0=dtr[di][:], in1=dtb[:], op=ALU.add), reads=R, writes=Wr)
            sc.op("act", lambda e: e.activation(out=s_["ab"][:], in_=s_["dt"][:], func=AF.Abs), reads=R, writes=Wr)
            sc.op("act", lambda e: e.activation(out=s_["ex"][:], in_=s_["ab"][:], func=AF.Exp, scale=-1.0), reads=R, writes=Wr)
            sc.op("act", lambda e: e.activation(out=s_["ex"][:], in_=s_["ex"][:], func=AF.Ln, bias=1.0), reads=R, writes=Wr)
            sc.op("dve", lambda e: e.scalar_tensor_tensor(out=s_["dt"][:], in0=s_["dt"][:], scalar=0.0, in1=s_["ex"][:],
                                                          op0=ALU.max, op1=ALU.add), reads=R, writes=Wr)
            sc.op("dve", lambda e: e.tensor_tensor(out=s_["a"][:], in0=s_["dt"][:], in1=Ab[:], op=ALU.mult), reads=R, writes=Wr)
            P6 = g.ps[6]
            c_f = slice(gi * 8, gi * 8 + 8)
            c_b = slice(32 + gi * 8, 32 + gi * 8 + 8)

            def mm(e):
                e.matmul(P6[:, 0:64], lhsT=ones[:], rhs=s_["a"][:], start=True, stop=True)
                e.matmul(P6[:, 64:72], lhsT=Uf[:], rhs=s_["a"][:, c_f], start=True, stop=True)
                return e.matmul(P6[:, 72:80], lhsT=Ub[:], rhs=s_["a"][:, c_b], start=True, stop=True)
            sc.op("pe", mm, reads=[b_sm, b_const], writes=[g.psb[6]])
            R2 = [g.psb[6], b_sm]
            sc.op("dve", lambda e: e.tensor_copy(out=s_["tot"][:, 0:8], in_=P6[:, c_f]), reads=R2, writes=Wr)
            sc.op("dve", lambda e: e.tensor_copy(out=s_["tot"][:, 8:16], in_=P6[:, c_b]), reads=R2, writes=Wr)
            sc.op("dve", lambda e: e.tensor_copy(out=s_["cs"][:, 0:16], in_=P6[:, 64:80]), reads=R2, writes=Wr)
            sc.op("dve", lambda e: e.tensor_scalar(out=s_["ncs"][:, 0:16], in0=s_["cs"][:, 0:16], scalar1=-1.0, scalar2=None, op0=ALU.mult),
                  reads=R2, writes=Wr)
            sc.op("act", lambda e: e.activation(out=s_["ecs"][:, 0:16], in_=s_["cs"][:, 0:16], func=AF.Exp), reads=R2, writes=Wr)
            sc.op("dve", lambda e: e.tensor_tensor(out=s_["dcs"][:, 0:16], in0=s_["tot"][:, 0:16], in1=s_["cs"][:, 0:16], op=ALU.subtract),
                  reads=R2, writes=Wr)
            sc.op("act", lambda e: e.activation(out=s_["dcs"][:, 0:16], in_=s_["dcs"][:, 0:16], func=AF.Exp), reads=R2, writes=Wr)
            sc.op("act", lambda e: e.activation(out=s_["cdec"][:, 0:16], in_=s_["tot"][:, 0:16], func=AF.Exp), reads=R2, writes=Wr)
            sc.op("dve", lambda e: e.tensor_copy(out=s_["w2"][:, 0:8], in_=s_["dt"][:, c_f]), reads=R2, writes=Wr)
            sc.op("dve", lambda e: e.tensor_copy(out=s_["w2"][:, 8:16], in_=s_["dt"][:, c_b]), reads=R2, writes=Wr)
            sc.op("dve", lambda e: e.tensor_tensor(out=s_["w2"][:, 16:32], in0=s_["w2"][:, 0:16], in1=s_["dcs"][:, 0:16], op=ALU.mult),
                  reads=R2, writes=Wr)
            sc.op("dve", lambda e: e.tensor_copy(out=s_["w2"][:, 32:40], in_=s_["a"][:, c_f]), reads=R2, writes=Wr)
            sc.op("dve", lambda e: e.tensor_copy(out=s_["w2"][:, 40:48], in_=s_["a"][:, c_b]), reads=R2, writes=Wr)

        def xmul(dst, b_dst, col0, d):
            sc.op("dve", lambda e: e.tensor_tensor(
                out=dst[:].rearrange("p (h q) -> p h q", h=8), in0=xc[:, 0:512].rearrange("p (h q) -> p h q", h=8),
                in1=sm["w2"][:, col0 + d * 8:col0 + d * 8 + 8].unsqueeze(2).to_broadcast([128, 8, 64]), op=ALU.mult),
                reads=[b_xc, b_sm], writes=[b_dst])

        def chunk_states(d):
            xmul(xdd[d], b_xdd[d], 16, d)
            sc.op("pe", lambda e: e.matmul(g.ps[5][:, :], lhsT=xcb[:, 512:640], rhs=xdd[d][:], start=True, stop=True),
                  reads=[b_xc, b_xdd[d]], writes=[g.psb[5]])

        def state_update(d, S_ap, S_bufs, cdec_ap, cdec_bufs):
            sc.op("dve", lambda e: e.tensor_tensor(
                out=state[d][:].rearrange("p (h q) -> p h q", h=8), in0=state[d][:].rearrange("p (h q) -> p h q", h=8),
                in1=cdec_ap.unsqueeze(2).to_broadcast([128, 8, 64]), op=ALU.mult),
                reads=[b_state[d]] + cdec_bufs, writes=[b_state[d]])
            sc.op("dve", lambda e: e.tensor_tensor(out=state[d][:], in0=state[d][:], in1=S_ap, op=ALU.add),
                  reads=[b_state[d]] + S_bufs, writes=[b_state[d]])
            sc.op("pool", lambda e: e.tensor_copy(out=prevb[d][:], in_=state[d][:]), reads=[b_state[d]], writes=[b_state[d]])

        def init_state(d, src):
            if src is None:
                sc.op("pool", lambda e: e.memset(state[d][:], 0.0), writes=[b_state[d]])
                sc.op("pool", lambda e: e.memset(prevb[d][:], 0.0), writes=[b_state[d]])
                return
            sc.dma("sp", stin[:], src.rearrange("(c p) n -> p c n", p=128), b_stin, writes=[b_stin])

            def tr(e):
                ins = None
                for c in range(4):
                    ins = e.transpose(g.ps[5][:, c * 128:(c + 1) * 128], stin[:, c, :], g.ident_f[:])
                return ins
            sc.op("pe", tr, reads=[b_stin, g.b_ident], writes=[g.psb[5]])
            sc.op("dve", lambda e: e.tensor_copy(out=state[d][:], in_=g.ps[5][:, :]), reads=[g.psb[5]], writes=[b_state[d]])
            sc.op("pool", lambda e: e.tensor_copy(out=prevb[d][:], in_=state[d][:]), reads=[b_state[d]], writes=[b_state[d]])

        def out_state(d, dst):
            def tr(e):
                ins = None
                for c in range(4):
                    ins = e.transpose(g.ps[5][:, c * 128:(c + 1) * 128], state[d][:, c * 128:(c + 1) * 128], g.ident_f[:])
                return ins
            sc.op("pe", tr, reads=[b_state[d], g.b_ident], writes=[g.psb[5]])
            sc.op("dve", lambda e: e.tensor_copy(out=stT[:].rearrange("p c n -> p (c n)"), in_=g.ps[5][:, :]),
                  reads=[g.psb[5]], writes=[b_stT])
            sc.dma("sp", dst.rearrange("(c p) n -> p c n", p=128), stT[:], b_stT, reads=[b_stT], writes=[g.dbuf["out"]])

        def diag_part(d, c):
            U = Uf if d == 0 else Ub
            M = Mf if d == 0 else Mb
            a_col = 32 + d * 8
            sc.op("dve" if d == 0 else "pool", lambda e: e.tensor_tensor(
                out=AU[d][:], in0=sm["w2"][:, a_col:a_col + 8].unsqueeze(2).to_broadcast([128, 8, 128]),
                in1=U[:].unsqueeze(1).to_broadcast([128, 8, 128]), op=ALU.mult),
                reads=[b_sm, b_const], writes=[b_AU[d]])
            pb0 = 0 + 2 * d

            def mm(e):
                ins = None
                for hf in range(2):
                    e.matmul(g.ps[pb0 + hf][:, :], lhsT=ones[:], rhs=AU[d][:, hf * 4:(hf + 1) * 4, :].rearrange("p h l -> p (h l)"),
                             start=True, stop=False)
                    ins = e.matmul(g.ps[pb0 + hf][:, :], lhsT=g.ident_f[:], rhs=M[:, hf * 4:(hf + 1) * 4, :].rearrange("p h l -> p (h l)"),
                                   start=False, stop=True)
                return ins
            sc.op("pe", mm, reads=[b_AU[d], b_const, g.b_ident], writes=[g.psb[pb0], g.psb[pb0 + 1]])
            for h in range(8):
                sc.op("act", lambda e, h=h: e.activation(
                    out=Lm[d][:, h, :], in_=g.ps[pb0 + h // 4][:, (h % 4) * 128:(h % 4 + 1) * 128], func=AF.Exp,
                    bias=sm["ncs"][:, d * 8 + h:d * 8 + h + 1]),
                    reads=[g.psb[pb0], g.psb[pb0 + 1], b_sm], writes=[b_Lm[d]])
            sc.op("dve" if d == 1 else "pool", lambda e: e.tensor_tensor(
                out=wts[d][:], in0=Lm[d][:], in1=cbT[:].unsqueeze(1).to_broadcast([128, 8, 128]), op=ALU.mult),
                reads=[b_Lm[d], b_cbT], writes=[b_wts[d]])
            xmul(xdt[d], b_xdt[d], 0, d)

            def mm2(e):
                ins = None
                for h in range(8):
                    ins = e.matmul(g.ps[4][:, h * 64:(h + 1) * 64], lhsT=wts[d][:, h, :], rhs=xdt[d][:, h * 64:(h + 1) * 64],
                                   start=True, stop=True)
                return ins
            sc.op("pe", mm2, reads=[b_wts[d], b_xdt[d]], writes=[g.psb[4]])

        def yoff_add(d, c, ecs_ap, ecs_bufs):
            sc.op("pe", lambda e: e.matmul(g.ps[5][:, :], lhsT=CT[:, c, :], rhs=prevb[d][:], start=True, stop=True),
                  reads=[b_CT[c], b_state[d]], writes=[g.psb[5]])
            sc.op("dve", lambda e: e.tensor_tensor(
                out=tmp[:].rearrange("p (h q) -> p h q", h=8), in0=g.ps[5][:, :].rearrange("p (h q) -> p h q", h=8),
                in1=ecs_ap.unsqueeze(2).to_broadcast([128, 8, 64]), op=ALU.mult),
                reads=[g.psb[5]] + ecs_bufs, writes=[b_tmp])
            sc.op("pool", lambda e: e.tensor_tensor(out=ysum[:, c, :], in0=ysum[:, c, :], in1=tmp[:], op=ALU.add),
                  reads=[b_tmp, b_ch[c]], writes=[b_ch[c]])

        for gi in groups:
            with nc.allow_non_contiguous_dma(reason="broadcast const loads"):
                for (o0, c0, n) in ((0, gi * 512, 512), (512, 2048 + gi * 128, 128), (640, 2560 + gi * 128, 128)):
                    sc.dma("sp", W5[:, :, o0:o0 + n], I.conv_w[:, c0:c0 + n].partition_broadcast(128), b_gc, writes=[b_gc])
                    sc.dma("sp", cbias[:, o0:o0 + n], I.conv_b[0, c0:c0 + n].partition_broadcast(128), b_gc, writes=[b_gc])
                sc.dma("sp", gssm[:], I.g_ssm[0, gi * 512:(gi + 1) * 512].partition_broadcast(128), b_gc, writes=[b_gc])
            for seq in seqs:
                if seq < 4:
                    base, nown, nrem = seq * 256, 2, 0
                    seq_lo, seq_hi = base, base + 256
                    init_state(0, None)
                    init_state(1, None)
                else:
                    base, nown, nrem = NP_TOK, 16, 16
                    seq_lo, seq_hi = base, base + NS_ALL
                    init_state(0, I.sf[gi * 512:(gi + 1) * 512, :])
                    init_state(1, I.sb[gi * 512:(gi + 1) * 512, :])
                    for c in range(nown + nrem - 1, nown - 1, -1):
                        t0 = base + c * 128
                        g.conv_step(2)
                        i = load_shift(t0, seq_lo, seq_hi, gi, 640)
                        conv_silu(i, 640)
                        dt_stuff(t0, gi, (1,))
                        chunk_states(1)
                        state_update(1, g.ps[5][:, :], [g.psb[5]], sm["cdec"][:, 8:16], [b_sm])
                for c in range(nown):
                    t0 = base + c * 128
                    g.conv_step(2)
                    i = load_shift(t0, seq_lo, seq_hi, gi)
                    conv_silu(i)
                    dt_stuff(t0, gi, (0, 1))
                    TP = g.ps[7][:, :].bitcast(BF16)

                    def tr(e, TP=TP):
                        e.transpose(TP[:, 0:128], xcb[:, 512:640], g.ident_b[:])
                        return e.transpose(TP[:, 128:256], xcb[:, 640:768], g.ident_b[:])
                    sc.op("pe", tr, reads=[b_xc, g.b_ident], writes=[g.psb[7]])
                    sc.op("act", lambda e, TP=TP: e.copy(out=BT[:], in_=TP[:, 0:128]), reads=[g.psb[7]], writes=[b_BT])
                    sc.op("act", lambda e, TP=TP, c=c: e.copy(out=CT[:, c, :], in_=TP[:, 128:256]), reads=[g.psb[7]], writes=[b_CT[c]])
                    sc.op("pe", lambda e, c=c: e.matmul(g.ps[6][:, 128:256], lhsT=BT[:], rhs=CT[:, c, :], start=True, stop=True),
                          reads=[b_BT, b_CT[c]], writes=[g.psb[6]])
                    sc.op("act", lambda e: e.copy(out=cbT[:], in_=g.ps[6][:, 128:256]), reads=[g.psb[6]], writes=[b_cbT])
                    sc.op("dve", lambda e, c=c: e.tensor_tensor(
                        out=ysum[:, c, :].rearrange("p (h q) -> p h q", h=8), in0=xc[:, 0:512].rearrange("p (h q) -> p h q", h=8),
                        in1=dsk[:, gi * 8:(gi + 1) * 8].unsqueeze(2).to_broadcast([128, 8, 64]), op=ALU.mult),
                        reads=[b_xc, b_const], writes=[b_ch[c]])
                    for d in (0, 1):
                        diag_part(d, c)
                        sc.op("dve", lambda e, c=c: e.tensor_tensor(out=ysum[:, c, :], in0=ysum[:, c, :], in1=g.ps[4][:, :], op=ALU.add),
                              reads=[g.psb[4], b_ch[c]], writes=[b_ch[c]])
                    yoff_add(0, c, sm["ecs"][:, 0:8], [b_sm])
                    chunk_states(0)
                    state_update(0, g.ps[5][:, :], [g.psb[5]], sm["cdec"][:, 0:8], [b_sm])
                    chunk_states(1)
                    sc.op("act", lambda e, c=c: e.copy(out=Sb_st[:, c, :], in_=g.ps[5][:, :]), reads=[g.psb[5]], writes=[b_ch[c]])
                    sc.op("pool", lambda e, c=c: e.tensor_copy(out=ecs_st[:, c, :], in_=sm["ecs"][:, 8:16]), reads=[b_sm], writes=[b_ch[c]])
                    sc.op("pool", lambda e, c=c: e.tensor_copy(out=cdec_st[:, c, :], in_=sm["cdec"][:, 8:16]), reads=[b_sm], writes=[b_ch[c]])
                if seq < 4:
                    out_state(0, O.nsf[(seq * 32 + gi * 8) * 64:(seq * 32 + gi * 8 + 8) * 64, :])
                for c in range(nown - 1, -1, -1):
                    t0 = base + c * 128
                    yoff_add(1, c, ecs_st[:, c, :], [b_ch[c]])
                    state_update(1, Sb_st[:, c, :], [b_ch[c]], cdec_st[:, c, :], [b_ch[c]])
                    sc.dma("sp", zt[:], S.Z[t0:t0 + 128, gi * 512:(gi + 1) * 512], b_zt, reads=[g.dbuf["Z"]], writes=[b_zt])
                    sc.op("act", lambda e: e.activation(out=zt[:], in_=zt[:], func=AF.Silu), reads=[b_zt], writes=[b_zt])
                    sc.op("dve", lambda e, c=c: e.tensor_tensor(out=tmp[:], in0=ysum[:, c, :], in1=zt[:], op=ALU.mult),
                          reads=[b_ch[c], b_zt], writes=[b_tmp])
                    sc.op("act", lambda e: e.activation(out=junk[:], in_=tmp[:], func=AF.Square, accum_out=nrm[:, 0:1]),
                          reads=[b_tmp], writes=[b_junk, b_nrm])
                    sc.op("act", lambda e: e.activation(out=nrm[:, 1:2], in_=nrm[:, 0:1], func=AF.Sqrt, scale=1.0 / 512, bias=EPS),
                          reads=[b_nrm], writes=[b_nrm])
                    sc.op("dve", lambda e: e.reciprocal(out=nrm[:, 1:2], in_=nrm[:, 1:2]), reads=[b_nrm], writes=[b_nrm])
                    sc.op("dve", lambda e: e.scalar_tensor_tensor(out=yb[:], in0=tmp[:], scalar=nrm[:, 1:2], in1=gssm[:],
                                                                  op0=ALU.mult, op1=ALU.mult),
                          reads=[b_tmp, b_nrm, b_gc], writes=[b_yb])
                    TP = g.ps[7][:, :].bitcast(BF16)

                    def tr2(e, TP=TP):
                        ins = None
                        for k in range(4):
                            ins = e.transpose(TP[:, k * 128:(k + 1) * 128], yb[:, k * 128:(k + 1) * 128], g.ident_b[:])
                        return ins
                    sc.op("pe", tr2, reads=[b_yb, g.b_ident], writes=[g.psb[7]])
                    sc.op("act", lambda e, TP=TP: e.copy(out=mxs[:].rearrange("p k t -> p (k t)"), in_=TP[:, 0:512]),
                          reads=[g.psb[7]], writes=[b_mxs])
                    kc0 = 16 + gi * 4
                    sc.dma("sp", S.MIXT[kc0:kc0 + 4, :, t0:t0 + 128].rearrange("k d t -> d k t"), mxs[:], b_mxs,
                           reads=[b_mxs], writes=[g.dbuf["MIXT"]])
                if seq < 4:
                    out_state(1, O.nsb[(seq * 32 + gi * 8) * 64:(seq * 32 + gi * 8 + 8) * 64, :])


def stage_attn(g):
    nc, sc, I, S = g.nc, g.sc, g.I, g.S
    seqs = g.cfg.get("attn_seqs", [0, 1, 2, 3, 4])
    heads = g.cfg.get("attn_heads", list(range(8)))
    scale = 128 ** -0.5
    with ExitStack() as st:
        def sb(name, shape, dt=F32):
            return st.enter_context(nc.sbuf_tensor("t_" + name, shape, dt))
        NKMAX = (NS_ALL + PAST) // 128
        K2 = [sb("K2_%d" % i, [128, 2, NKMAX * 128], BF16) for i in range(2)]
        Vt = [sb("Vt_%d" % i, [128, NKMAX, 257], BF16) for i in range(2)]
        Q2 = [sb("Q2_%d" % i, [128, 2, 512], BF16) for i in range(2)]
        PT = [sb("PT_%d" % i, [128, 512], BF16) for i in range(3)]
        osb = sb("osb", [128, 4, 256])
        ob = sb("ob", [128, 256], BF16)
        mx = [sb("mx%d" % i, [128, 2, 512], BF16) for i in range(2)]
        lamt = sb("lamt", [128, 512])
        sm = sb("sm", [128, 8])
        rr = sb("rr", [128, 4])
        gsub = sb("gsub", [128, 256])
        junk = sb("junk", [128, 256])
        b_K2, b_Vt, b_Q2 = [Buf("t_K0"), Buf("t_K1")], [Buf("t_V0"), Buf("t_V1")], [Buf("t_Q0"), Buf("t_Q1")]
        b_PT = [Buf("t_PT%d" % i) for i in range(3)]
        b_osb, b_ob, b_mx = Buf("t_osb"), Buf("t_ob"), [Buf("t_mx0"), Buf("t_mx1")]
        b_c, b_rr, b_junk = Buf("t_c"), Buf("t_rr"), Buf("t_junk")
        sc.dma("sp", lamt[:], I.lam[0, :].partition_broadcast(128), b_c, writes=[b_c])
        sc.dma("sp", gsub[:], I.g_subln[0, :].partition_broadcast(128), b_c, writes=[b_c])
        for i in range(2):
            sc.op("dve", lambda e, i=i: e.tensor_tensor(out=lamt[:, i * 256:i * 256 + 128], in0=lamt[:, i * 256:i * 256 + 128],
                                                        in1=lamt[:, i * 256 + 128:i * 256 + 256], op=ALU.mult), reads=[b_c], writes=[b_c])
            sc.op("act", lambda e, i=i: e.activation(out=junk[:, 0:128], in_=lamt[:, i * 256:i * 256 + 128], func=AF.Identity,
                                                     accum_out=sm[:, i:i + 1]), reads=[b_c], writes=[b_c, b_junk])
        sc.op("act", lambda e: e.activation(out=sm[:, 2:4], in_=sm[:, 0:2], func=AF.Exp), reads=[b_c], writes=[b_c])
        sc.op("dve", lambda e: e.tensor_tensor(out=sm[:, 4:5], in0=sm[:, 3:4], in1=sm[:, 2:3], op=ALU.subtract), reads=[b_c], writes=[b_c])
        sc.op("dve", lambda e: e.tensor_scalar(out=sm[:, 4:5], in0=sm[:, 4:5], scalar1=-LAM_INIT, scalar2=None, op0=ALU.add),
              reads=[b_c], writes=[b_c])
        sc.op("dve", lambda e: e.tensor_scalar(out=gsub[:], in0=gsub[:], scalar1=1.0 - LAM_INIT, scalar2=None, op0=ALU.mult),
              reads=[b_c], writes=[b_c])
        for i in range(2):
            sc.op("pool", lambda e, i=i: e.memset(Vt[i][:, :, 256:257], 1.0), writes=[b_Vt[i]])
        cnt = dict(kv=0, q=0, pt=0, st=0, mx=0)
        for seq in seqs:
            if seq < 4:
                k0, nk, q0, nq = seq * 256, 256, seq * 256, 256
            else:
                k0, nk, q0, nq = NP_TOK, NS_ALL + PAST, NP_TOK, NS_OWN
            nkc = nk // 128
            for h in heads:
                ki = cnt["kv"] % 2
                cnt["kv"] += 1
                for j in range(2):
                    sc.dma("sp", K2[ki][:, j, 0:nk], S.KT[h * 2 + j, :, k0:k0 + nk], b_K2[ki], reads=[g.dbuf["KT"]], writes=[b_K2[ki]])
                sc.dma("sp", Vt[ki][:, 0:nkc, 0:256], S.V[k0:k0 + nk, h * 256:(h + 1) * 256].rearrange("(c p) v -> p c v", p=128),
                       b_Vt[ki], reads=[g.dbuf["V"]], writes=[b_Vt[ki]])
                for qt0 in range(0, nq, 512):
                    nqt = min(512, nq - qt0)
                    nqb = nqt // 128
                    qi = cnt["q"] % 2
                    cnt["q"] += 1
                    for j in range(2):
                        sc.dma("sp", Q2[qi][:, j, 0:nqt], S.QT[h * 2 + j, :, q0 + qt0:q0 + qt0 + nqt], b_Q2[qi],
                               reads=[g.dbuf["QT"]], writes=[b_Q2[qi]])
                    items = [(j, kc) for j in range(2) for kc in range(nkc)]
                    slots = {}

                    def issue_st(idx):
                        j, kc = items[idx]
                        pb = 4 + cnt["st"] % 3
                        cnt["st"] += 1
                        pi = cnt["pt"] % 3
                        cnt["pt"] += 1
                        slots[idx] = pi
                        sc.op("pe", lambda e, kc=kc, j=j, pb=pb: e.matmul(
                            g.ps[pb][:, 0:nqt], lhsT=K2[ki][:, j, kc * 128:(kc + 1) * 128], rhs=Q2[qi][:, j, 0:nqt],
                            start=True, stop=True), reads=[b_K2[ki], b_Q2[qi]], writes=[g.psb[pb]])
                        sc.op("act", lambda e, pb=pb, pi=pi: e.activation(out=PT[pi][:, 0:nqt], in_=g.ps[pb][:, 0:nqt],
                                                                          func=AF.Exp, scale=scale),
                              reads=[g.psb[pb]], writes=[b_PT[pi]])
                    PRE = 2
                    for idx in range(min(PRE, len(items))):
                        issue_st(idx)
                    for idx, (j, kc) in enumerate(items):
                        if idx + PRE < len(items):
                            issue_st(idx + PRE)
                        pi = slots.pop(idx)

                        def av(e, kc=kc, pi=pi):
                            ins = None
                            for qb in range(nqb):
                                ins = e.matmul(g.ps[qb][:, 0:257], lhsT=PT[pi][:, qb * 128:(qb + 1) * 128], rhs=Vt[ki][:, kc, :],
                                               start=(kc == 0), stop=(kc == nkc - 1))
                            return ins
                        sc.op("pe", av, reads=[b_PT[pi], b_Vt[ki]], writes=[g.psb[qb] for qb in range(nqb)])
                        if kc != nkc - 1:
                            continue
                        for qb in range(nqb):
                            A = g.ps[qb]
                            sc.op("dve", lambda e, A=A, j=j: e.reciprocal(out=rr[:, j:j + 1], in_=A[:, 256:257]),
                                  reads=[g.psb[qb]], writes=[b_rr])
                            if j == 0:
                                sc.op("dve", lambda e, A=A, qb=qb: e.tensor_scalar(out=osb[:, qb, :], in0=A[:, 0:256], scalar1=rr[:, 0:1],
                                                                                   scalar2=None, op0=ALU.mult),
                                      reads=[g.psb[qb], b_rr], writes=[b_osb])
                            else:
                                sc.op("dve", lambda e: e.tensor_tensor(out=rr[:, 2:3], in0=rr[:, 1:2], in1=sm[:, 4:5], op=ALU.mult),
                                      reads=[b_rr, b_c], writes=[b_rr])
                                sc.op("dve", lambda e, A=A, qb=qb: e.scalar_tensor_tensor(
                                    out=osb[:, qb, :], in0=A[:, 0:256], scalar=rr[:, 2:3], in1=osb[:, qb, :], op0=ALU.mult, op1=ALU.add),
                                    reads=[g.psb[qb], b_rr, b_osb], writes=[b_osb])
                    mi = cnt["mx"] % 2
                    cnt["mx"] += 1
                    for qb in range(nqb):
                        sc.op("act", lambda e, qb=qb: e.activation(out=junk[:], in_=osb[:, qb, :], func=AF.Square, accum_out=rr[:, 3:4]),
                              reads=[b_osb], writes=[b_rr, b_junk])
                        sc.op("act", lambda e: e.activation(out=rr[:, 3:4], in_=rr[:, 3:4], func=AF.Sqrt, scale=1.0 / 256, bias=EPS),
                              reads=[b_rr], writes=[b_rr])
                        sc.op("dve", lambda e: e.reciprocal(out=rr[:, 3:4], in_=rr[:, 3:4]), reads=[b_rr], writes=[b_rr])
                        sc.op("dve", lambda e, qb=qb: e.scalar_tensor_tensor(out=ob[:], in0=osb[:, qb, :], scalar=rr[:, 3:4], in1=gsub[:],
                                                                             op0=ALU.mult, op1=ALU.mult),
                              reads=[b_osb, b_rr, b_c], writes=[b_ob])
                        TP = g.ps[7][:, :].bitcast(BF16)

                        def tr(e, TP=TP):
                            e.transpose(TP[:, 0:128], ob[:, 0:128], g.ident_b[:])
                            return e.transpose(TP[:, 128:256], ob[:, 128:256], g.ident_b[:])
                        sc.op("pe", tr, reads=[b_ob, g.b_ident], writes=[g.psb[7]])
                        sc.op("act", lambda e, TP=TP, qb=qb, mi=mi: e.copy(out=mx[mi][:, :, qb * 128:(qb + 1) * 128],
                                                                          in_=TP[:, 0:256].rearrange("p (k t) -> p k t", k=2)),
                              reads=[g.psb[7]], writes=[b_mx[mi]])
                    sc.dma("sp", S.MIXT[h * 2:h * 2 + 2, :, q0 + qt0:q0 + qt0 + nqt].rearrange("k d t -> d k t"), mx[mi][:, :, 0:nqt],
                           b_mx[mi], reads=[b_mx[mi]], writes=[g.dbuf["MIXT"]])


def stage_mlp(g):
    nc, sc, I, S, O = g.nc, g.sc, g.I, g.S, g.O
    groups = g.cfg.get("mlp_groups", list(range(NOWN // 512)))
    nfb = g.cfg.get("mlp_fblocks", 16)
    if "MIXO" not in g.dbuf:
        g.dbuf["MIXO"] = Buf("d_MIXO", acc=True)
    MIXO = nc.dram_tensor("MIXO", [NOWN, D], F32, kind="ExternalOutput" if "MIXO" in g.debug else "Internal").ap()

    def load_rowmod(st, name, mode, idx, which):
        t = st.enter_context(nc.sbuf_tensor(name, [128, D], F32))
        b = Buf(name)
        return t, b

    def fill_rowmod(t, b, tmp, b_tmp, mode, idx, which):
        sc.dma("sp", t[:], S.modraw[mode, idx * D:(idx + 1) * D].partition_broadcast(128), b, reads=[g.dbuf["modraw"]], writes=[b])
        sc.dma("sp", tmp[:], I.grows[which, :].partition_broadcast(128), b_tmp, writes=[b_tmp])
        sc.op("dve", lambda e: e.tensor_tensor(out=t[:], in0=t[:], in1=tmp[:], op=ALU.mult), reads=[b, b_tmp], writes=[b])

    with ExitStack() as st0:
        h2T = st0.enter_context(nc.sbuf_tensor("m_h2T", [128, NKC, 512], BF16))
        b_h2T = Buf("m_h2T")
        ss = st0.enter_context(nc.sbuf_tensor("m_ss", [128, 1], F32))
        rstd = st0.enter_context(nc.sbuf_tensor("m_rstd", [128, 1], F32))
        b_scr = Buf("m_scr")
        modc = {}
        modc[0] = load_modcols(g, st0, 0, 2, 4, 3, "m_modp")
        modc[1] = load_modcols(g, st0, 1, 2, 4, 3, "m_mods")
        for grp in groups:
            t0g = grp * 512
            mode = 0 if t0g < NP_TOK else 1
            with ExitStack() as st:
                mixT = st.enter_context(nc.sbuf_tensor("m_mixT_%d" % grp, [128, NKC, 512], BF16))
                W = [st.enter_context(nc.sbuf_tensor("m_Wo%d_%d" % (i, grp), [128, NKC, 512], BF16)) for i in range(2)]
                stg = [st.enter_context(nc.sbuf_tensor("m_stg%d_%d" % (i, grp), [128, 512], F32)) for i in range(4)]
                GA = st.enter_context(nc.sbuf_tensor("m_GA_%d" % grp, [128, D], F32))
                mixt = st.enter_context(nc.sbuf_tensor("m_mix_%d" % grp, [128, D], F32))
                xt = st.enter_context(nc.sbuf_tensor("m_x_%d" % grp, [128, D], F32))
                junk = st.enter_context(nc.sbuf_tensor("m_junk_%d" % grp, [128, D], BF16))
                b_mixT, b_W, b_stg = Buf("m_mixT"), [Buf("m_Wo0"), Buf("m_Wo1")], [Buf("m_stg%d" % i) for i in range(4)]
                b_GA, b_mix, b_x, b_junk = Buf("m_GA"), Buf("m_mix"), Buf("m_x"), Buf("m_junk")
                fill_rowmod(GA, b_GA, mixt, b_mix, mode, 2, 1)
                sc.dma("sp", mixT[:], S.MIXT[:, :, t0g:t0g + 512].rearrange("k d t -> d k t"), b_mixT,
                       reads=[g.dbuf["MIXT"]], writes=[b_mixT])
                k = 0
                for c in range(8):
                    wi = c % 2
                    sc.dma("pool", W[wi][:], S.WOB[c].rearrange("p (kc n) -> p kc n", kc=NKC), b_W[wi], reads=[g.dbuf["WB"]], writes=[b_W[wi]])
                    for tt in range(4):
                        pb = k % 4
                        si = k % 4
                        k += 1

                        def mm(e, tt=tt, wi=wi, pb=pb):
                            ins = None
                            for kc in range(NKC):
                                ins = e.matmul(g.ps[pb][:, :], lhsT=mixT[:, kc, tt * 128:(tt + 1) * 128], rhs=W[wi][:, kc, :],
                                               start=(kc == 0), stop=(kc == NKC - 1))
                            return ins
                        sc.op("pe", mm, reads=[b_mixT, b_W[wi]], writes=[g.psb[pb]])
                        if k % 2 == 0:
                            sc.op("act", lambda e, si=si, pb=pb: e.copy(out=stg[si][:], in_=g.ps[pb][:, :]), reads=[g.psb[pb]], writes=[b_stg[si]])
                        else:
                            sc.op("dve", lambda e, si=si, pb=pb: e.tensor_copy(out=stg[si][:], in_=g.ps[pb][:, :]), reads=[g.psb[pb]], writes=[b_stg[si]])
                        sc.dma("sp", MIXO[t0g + tt * 128:t0g + (tt + 1) * 128, c * 512:(c + 1) * 512], stg[si][:], b_stg[si],
                               reads=[b_stg[si]], writes=[g.dbuf["MIXO"]])
                for tt in range(4):
                    t0 = t0g + tt * 128
                    sc.dma("sp", mixt[:], MIXO[t0:t0 + 128, :], b_mix, reads=[g.dbuf["MIXO"]], writes=[b_mix])
                    sc.dma("sp", xt[:], tok_src(g, t0), b_x, writes=[b_x])
                    sc.op("act", lambda e: e.activation(out=junk[:], in_=mixt[:], func=AF.Square, accum_out=ss[:]),
                          reads=[b_mix], writes=[b_scr, b_junk])
                    sc.op("act", lambda e: e.activation(out=rstd[:], in_=ss[:], func=AF.Sqrt, scale=1.0 / D, bias=EPS),
                          reads=[b_scr], writes=[b_scr])
                    sc.op("dve", lambda e: e.reciprocal(out=rstd[:], in_=rstd[:]), reads=[b_scr], writes=[b_scr])
                    sc.op("dve", lambda e: e.scalar_tensor_tensor(out=mixt[:], in0=mixt[:], scalar=rstd[:, 0:1], in1=GA[:],
                                                                  op0=ALU.mult, op1=ALU.mult), reads=[b_mix, b_scr, b_GA], writes=[b_mix])
                    sc.op("pool", lambda e: e.tensor_tensor(out=xt[:], in0=xt[:], in1=mixt[:], op=ALU.add), reads=[b_mix, b_x], writes=[b_x])
                    sc.dma("sp", S.X1[t0:t0 + 128, :], xt[:], b_x, reads=[b_x], writes=[g.dbuf["X1"]])
                    norm_transpose(g, xt, b_x, modc[mode][0], modc[mode][1], h2T, b_h2T, tt, (ss, rstd), b_scr, junk, b_junk)
            sc.barrier()
            with ExitStack() as stB:
                macc = stB.enter_context(nc.sbuf_tensor("m_acc_%d" % grp, [128, 4, D], F32))
                b_macc = [[Buf("m_acc%d_%d" % (a, c)) for c in range(8)] for a in range(4)]
                with ExitStack() as st:
                    Wu = [st.enter_context(nc.sbuf_tensor("m_Wu%d_%d" % (i, grp), [128, NKC, 512], BF16)) for i in range(2)]
                    Wd = [st.enter_context(nc.sbuf_tensor("m_Wd%d_%d" % (i, grp), [128, 8, 512], BF16)) for i in range(2)]
                    uT = [st.enter_context(nc.sbuf_tensor("m_uT%d_%d" % (i, grp), [128, 8, 512], BF16)) for i in range(2)]
                    r32 = [st.enter_context(nc.sbuf_tensor("m_r%d_%d" % (i, grp), [128, 512], F32)) for i in range(2)]
                    b_Wu, b_Wd, b_uT = [Buf("m_Wu%d" % i) for i in range(2)], [Buf("m_Wd0"), Buf("m_Wd1")], [Buf("m_uT0"), Buf("m_uT1")]
                    b_r = [Buf("m_r0"), Buf("m_r1")]
                    cnt = dict(wu=0, wd=0, ps=0, r=0)
                    for fb in range(nfb):
                        ui = fb % 2
                        for hf in range(2):
                            wi = cnt["wu"] % 2
                            cnt["wu"] += 1
                            c0 = fb * 1024 + hf * 512
                            sc.dma("pool", Wu[wi][:], S.WUB[fb * 2 + hf].rearrange("p (kc n) -> p kc n", kc=NKC), b_Wu[wi],
                                   reads=[g.dbuf["WB"]], writes=[b_Wu[wi]])
                            for c4 in range(4):
                                ch = hf * 4 + c4
                                pb = 4 + cnt["ps"] % 2

                                def mm(e, wi=wi, pb=pb, c4=c4):
                                    ins = None
                                    for kc in range(NKC):
                                        ins = e.matmul(g.ps[pb][:, :], lhsT=Wu[wi][:, kc, c4 * 128:(c4 + 1) * 128], rhs=h2T[:, kc, :],
                                                       start=(kc == 0), stop=(kc == NKC - 1))
                                    return ins
                                sc.op("pe", mm, reads=[b_Wu[wi], b_h2T], writes=[g.psb[pb]])
                                ri = cnt["r"] % 2
                                cnt["r"] += 1
                                cnt["ps"] += 1
                                sc.op("act", lambda e, ri=ri, pb=pb: e.activation(out=r32[ri][:], in_=g.ps[pb][:, :], func=AF.Relu),
                                      reads=[g.psb[pb]], writes=[b_r[ri]])
                                sc.op("pool", lambda e, ri=ri, ui=ui, ch=ch: e.tensor_tensor(out=uT[ui][:, ch, :], in0=r32[ri][:], in1=r32[ri][:], op=ALU.mult),
                                      reads=[b_r[ri]], writes=[b_uT[ui]])
                        for c in range(8):
                            di = cnt["wd"] % 2
                            cnt["wd"] += 1
                            sc.dma("pool", Wd[di][:], S.WDB[fb, c].rearrange("p (ch n) -> p ch n", ch=8), b_Wd[di],
                                   reads=[g.dbuf["WB"]], writes=[b_Wd[di]])
                            for tt in range(4):
                                pb = cnt["ps"] % 4
                                cnt["ps"] += 1

                                def mm2(e, tt=tt, di=di, pb=pb, ui=ui):
                                    ins = None
                                    for ch in range(8):
                                        ins = e.matmul(g.ps[pb][:, :], lhsT=uT[ui][:, ch, tt * 128:(tt + 1) * 128], rhs=Wd[di][:, ch, :],
                                                       start=(ch == 0), stop=(ch == 7))
                                    return ins
                                sc.op("pe", mm2, reads=[b_uT[ui], b_Wd[di]], writes=[g.psb[pb]])
                                dst = macc[:, tt, c * 512:(c + 1) * 512]
                                if fb == 0:
                                    sc.op("act", lambda e, dst=dst, pb=pb: e.copy(out=dst, in_=g.ps[pb][:, :]), reads=[g.psb[pb]], writes=[b_macc[tt][c]])
                                else:
                                    sc.op("dve", lambda e, dst=dst, pb=pb: e.tensor_tensor(out=dst, in0=dst, in1=g.ps[pb][:, :], op=ALU.add),
                                          reads=[g.psb[pb], b_macc[tt][c]], writes=[b_macc[tt][c]])
                sc.barrier()
                with ExitStack() as st:
                    GM = st.enter_context(nc.sbuf_tensor("m_GM_%d" % grp, [128, D], F32))
                    xt = st.enter_context(nc.sbuf_tensor("m_x1_%d" % grp, [128, D], F32))
                    junk = st.enter_context(nc.sbuf_tensor("m_junk2_%d" % grp, [128, D], BF16))
                    b_GM, b_x, b_junk = Buf("m_GM"), Buf("m_x1"), Buf("m_junk2")
                    fill_rowmod(GM, b_GM, xt, b_x, mode, 5, 3)
                    for tt in range(4):
                        t0 = t0g + tt * 128
                        mt = macc[:, tt, :]
                        sc.dma("sp", xt[:], S.X1[t0:t0 + 128, :], b_x, reads=[g.dbuf["X1"]], writes=[b_x])
                        sc.op("act", lambda e, mt=mt: e.activation(out=junk[:], in_=mt, func=AF.Square, accum_out=ss[:]),
                              reads=b_macc[tt], writes=[b_scr, b_junk])
                        sc.op("act", lambda e: e.activation(out=rstd[:], in_=ss[:], func=AF.Sqrt, scale=1.0 / D, bias=EPS),
                              reads=[b_scr], writes=[b_scr])
                        sc.op("dve", lambda e: e.reciprocal(out=rstd[:], in_=rstd[:]), reads=[b_scr], writes=[b_scr])
                        sc.op("dve", lambda e, mt=mt: e.scalar_tensor_tensor(out=mt, in0=mt, scalar=rstd[:, 0:1], in1=GM[:],
                                                                             op0=ALU.mult, op1=ALU.mult),
                              reads=b_macc[tt] + [b_scr, b_GM], writes=b_macc[tt])
                        sc.op("pool", lambda e, mt=mt: e.tensor_tensor(out=xt[:], in0=xt[:], in1=mt, op=ALU.add), reads=b_macc[tt] + [b_x], writes=[b_x])
                        dst = O.yp[t0:t0 + 128, :] if t0 < NP_TOK else O.ys[t0 - NP_TOK:t0 - NP_TOK + 128, :]
                        sc.dma("sp", dst, xt[:], b_x, reads=[b_x], writes=[g.dbuf["out"]])
                sc.barrier()
```

```python
import math
from contextlib import ExitStack
import numpy as np
import concourse.bass as bass
import concourse.mybir as mybir
from concourse.bass_utils import run_bass_kernel_spmd

F32 = mybir.dt.float32
BF16 = mybir.dt.bfloat16
AF = mybir.ActivationFunctionType
ALU = mybir.AluOpType
AX = mybir.AxisListType

D = 4096
NKC = 32
IN_COLS = 11328
NP_TOK = 1024
NS_OWN = 2048
NS_ALL = 4096
PAST = 512
NT = NP_TOK + NS_ALL
NOWN = NP_TOK + NS_OWN
DFF = 16384
EPS = 1e-6
LAM_INIT = 0.8 - 0.6 * math.exp(-0.3 * 0)


class Buf:
    __slots__ = ("name", "acc", "w", "r", "dsem", "dkey", "dcnt")

    def __init__(self, name, acc=False):
        self.name = name
        self.acc = acc
        self.w = {}
        self.r = {}
        self.dsem = None
        self.dkey = None
        self.dcnt = 0


class Sched:
    def __init__(self, nc, stack):
        self.nc = nc
        self.stack = stack
        self.engs = dict(pe=nc.tensor, act=nc.scalar, dve=nc.vector, pool=nc.gpsimd, sp=nc.sync)
        self.esem = {}
        self.ecnt = {}
        self.waited = {}
        for k in self.engs:
            self.esem[k] = stack.enter_context(nc.semaphore("es_" + k))
            self.ecnt[k] = 0
            self.waited[k] = {}
        self.dsems = []
        self.pinned = []
        self.free_sems = []
        self.nsem = 0
        self.nops = 0

    def _deps(self, ek, reads, writes, is_dma):
        deps = {}
        own = None if is_dma else "es_" + ek
        for b in reads:
            for k, sv in b.w.items():
                if k not in deps or deps[k][1] < sv[1]:
                    deps[k] = sv
        for b in writes:
            if b.acc:
                continue
            for dd in (b.w, b.r):
                for k, sv in dd.items():
                    if k == own:
                        continue
                    if k not in deps or deps[k][1] < sv[1]:
                        deps[k] = sv
        return deps

    def _wait(self, ek, deps):
        e = self.engs[ek]
        wd = self.waited[ek]
        for k, (sem, val) in deps.items():
            if wd.get(k, 0) < val:
                e.wait_ge(sem, val)
                wd[k] = val

    def _record(self, k, sv, reads, writes):
        for b in writes:
            if b.acc:
                if k not in b.w or b.w[k][1] < sv[1]:
                    b.w[k] = sv
            else:
                b.w = {k: sv}
                b.r = {}
        for b in reads:
            if k not in b.r or b.r[k][1] < sv[1]:
                b.r[k] = sv

    def op(self, ek, fn, reads=(), writes=()):
        deps = self._deps(ek, reads, writes, False)
        self._wait(ek, deps)
        ins = fn(self.engs[ek])
        self.ecnt[ek] += 1
        ins.then_inc(self.esem[ek], 1)
        self._record("es_" + ek, (self.esem[ek], self.ecnt[ek]), reads, writes)
        self.nops += 1

    def dma(self, qk, out, in_, sb, reads=(), writes=()):
        deps = self._deps(qk, reads, writes, True)
        if sb.dsem is None:
            if self.free_sems:
                sb.dkey, sb.dsem, sb.dcnt = self.free_sems.pop()
            else:
                sb.dkey = "ds%d" % self.nsem
                self.nsem += 1
                sb.dsem = self.stack.enter_context(self.nc.semaphore(sb.dkey))
                sb.dcnt = 0
            self.dsems.append(sb)
        if sb.dcnt > 0:
            k = sb.dkey
            if k not in deps or deps[k][1] < sb.dcnt:
                deps[k] = (sb.dsem, sb.dcnt)
        self._wait(qk, deps)
        self.engs[qk].dma_start(out=out, in_=in_).then_inc(sb.dsem, 16)
        sb.dcnt += 16
        self._record(sb.dkey, (sb.dsem, sb.dcnt), reads, writes)
        self.nops += 1

    def pin(self, sb):
        sb.dkey = "dp%d" % len(self.pinned)
        sb.dsem = self.stack.enter_context(self.nc.semaphore(sb.dkey))
        sb.dcnt = 0
        self.pinned.append(sb)

    def barrier(self):
        deps = {}
        for k in self.engs:
            if self.ecnt[k] > 0:
                deps["es_" + k] = (self.esem[k], self.ecnt[k])
        for sb in self.dsems + self.pinned:
            if sb.dcnt > 0:
                deps[sb.dkey] = (sb.dsem, sb.dcnt)
        for k in self.engs:
            d2 = {kk: v for kk, v in deps.items() if kk != "es_" + k}
            self._wait(k, d2)
        for sb in self.dsems:
            self.free_sems.append((sb.dkey, sb.dsem, sb.dcnt))
            sb.dsem = None
        self.dsems = []

    def finish(self):
        deps = {}
        for k in self.engs:
            if self.ecnt[k] > 0 and k != "sp":
                deps["es_" + k] = (self.esem[k], self.ecnt[k])
        for sb in self.dsems + self.pinned:
            if sb.dcnt > 0:
                deps[sb.dkey] = (sb.dsem, sb.dcnt)
        self._wait("sp", deps)


class Ctx:
    pass


def build(debug=None, cfg=None):
    cfg = cfg or {}
    nc = bass.Bass("TRN2", target_bir_lowering=False)
    g = Ctx()
    g.nc = nc
    g.cfg = cfg
    g.debug = debug or ()

    def din(name, shape, dt=F32):
        return nc.dram_tensor(name, list(shape), dt, kind="ExternalInput").ap()

    def dout(name, shape, dt=F32):
        return nc.dram_tensor(name, list(shape), dt, kind="ExternalOutput").ap()

    def dscr(name, shape, dt=F32):
        kind = "ExternalOutput" if name in g.debug else "Internal"
        return nc.dram_tensor(name, list(shape), dt, kind=kind).ap()

    IN_SHAPES = dict(xp=[NP_TOK, D], xs=[NS_ALL, D], cT=[128, NKC * 2], ck=[PAST, 2048], cv=[PAST, 2048],
                     sf=[32 * 64, 128], sb=[32 * 64, 128], w_ada=[D, 6 * D], b_ada=[1, 6 * D], gcols=[128, 4 * NKC],
                     grows=[4, D], w_in=[D, IN_COLS], lam=[1, 4 * 128], g_subln=[1, 256], conv_w=[5, 3072],
                     conv_b=[1, 3072], a_log=[1, 64], dt_bias=[1, 64], d_skip=[1, 32], g_ssm=[1, 2048],
                     w_out=[D, D], w_up=[D, DFF], w_down=[DFF, D], ropec=[NS_ALL, 128], ropes=[NS_ALL, 128],
                     ident=[128, 128])

    class LazyIn:
        def __getattr__(self, name):
            ap = din(name, IN_SHAPES[name])
            object.__setattr__(self, name, ap)
            g.used_inputs.append(name)
            return ap
    g.used_inputs = []
    I = LazyIn()
    g.I = I
    if not cfg.get("lazy_inputs", False):
        for n_ in IN_SHAPES:
            getattr(I, n_)

    O = Ctx()
    g.O = O
    O.yp = dout("o_yp", [NP_TOK, D])
    O.ys = dout("o_ys", [NS_OWN, D])
    O.nk = dout("o_nk", [NP_TOK, 2048])
    O.nv = dout("o_nv", [NP_TOK, 2048])
    O.nsf = dout("o_nsf", [4 * 32 * 64, 128])
    O.nsb = dout("o_nsb", [4 * 32 * 64, 128])

    S = Ctx()
    g.S = S
    S.modraw = dscr("modraw", [2, 6 * D])
    S.QT = dscr("QT", [16, 128, NOWN], BF16)
    S.KT = dscr("KT", [16, 128, NT + PAST], BF16)
    S.V = dscr("V", [NT + PAST, 2048], BF16)
    S.Z = dscr("Zs", [NOWN, 2048 + g.cfg.get("zpad", 0)])
    S.XBC = dscr("XBC", [NT, 4 * 768])
    S.DT = dscr("DT", [NT, 64])
    S.MIXT = dscr("MIXT", [NKC, 128, NOWN], BF16)
    S.X1 = dscr("X1", [NOWN, D])

    with ExitStack() as stack:
        sc = Sched(nc, stack)
        g.sc = sc
        g.stack = stack
        g.ps = []
        g.psb = []
        for i in range(8):
            t = stack.enter_context(nc.psum_tensor("ps%d" % i, [128, 512], F32))
            g.ps.append(t)
            g.psb.append(Buf("ps%d" % i))
        g.ident_f = stack.enter_context(nc.sbuf_tensor("ident_f", [128, 128], F32))
        g.ident_b = stack.enter_context(nc.sbuf_tensor("ident_b", [128, 128], BF16))
        g.b_ident = Buf("ident")
        sc.dma("sp", g.ident_f[:], I.ident[:, :], g.b_ident, writes=[g.b_ident])
        sc.op("dve", lambda e: e.tensor_copy(g.ident_b[:], g.ident_f[:]), reads=[g.b_ident], writes=[g.b_ident])
        g.dbuf = {n: Buf("d_" + n, acc=True) for n in
                  ("modraw", "QT", "KT", "V", "Z", "XBC", "DT", "MIXT", "X1", "out")}

        S.WOB = dscr("WOB", [8, 128, NKC * 512], BF16)
        S.WUB = dscr("WUB", [32, 128, NKC * 512], BF16)
        S.WDB = dscr("WDB", [16, 8, 128, 8 * 512], BF16)
        g.dbuf["WB"] = Buf("d_WB", acc=True)
        g.conv_list = []
        g.conv_bufs = [Buf("cv%d" % i) for i in range(6)]
        for cb_ in g.conv_bufs:
            sc.pin(cb_)
        g.conv_i = [0]

        def conv_step(n=1):
            if not g.conv_list:
                for kc in range(NKC):
                    g.conv_list.append((S.WOB[:, :, kc * 512:(kc + 1) * 512].rearrange("c p n -> p c n"),
                                        I.w_out[kc * 128:(kc + 1) * 128, :].rearrange("p (c n) -> p c n", c=8)))
                for kc in range(NKC):
                    g.conv_list.append((S.WUB[:, :, kc * 512:(kc + 1) * 512].rearrange("c p n -> p c n"),
                                        I.w_up[kc * 128:(kc + 1) * 128, :].rearrange("p (c n) -> p c n", c=32)))
                for fb in range(16):
                    for ch in range(8):
                        r0 = (fb * 8 + ch) * 128
                        g.conv_list.append((S.WDB[fb, :, :, ch * 512:(ch + 1) * 512].rearrange("c p n -> p c n"),
                                            I.w_down[r0:r0 + 128, :].rearrange("p (c n) -> p c n", c=8)))
            for _ in range(n):
                i = g.conv_i[0]
                if i >= len(g.conv_list):
                    return
                g.conv_i[0] += 1
                o_, i_ = g.conv_list[i]
                sc.dma("pool", o_, i_, g.conv_bufs[i % len(g.conv_bufs)], writes=[g.dbuf["WB"]])
        g.conv_step = conv_step

        if cfg.get("conv_only"):
            g.conv_step(cfg["conv_only"])
            sc.barrier()
        stages = cfg.get("stages", "0123456")
        if "0" in stages:
            stage_adaln(g)
            sc.barrier()
        if "1" in stages:
            stage_inproj(g)
            sc.barrier()
        if "2" in stages:
            stage_ctxkv(g)
            sc.barrier()
        if "3" in stages:
            stage_ssd(g)
            sc.barrier()
        if "4" in stages:
            stage_attn(g)
            sc.barrier()
        if "5" in stages:
            g.conv_step(1000)
            sc.barrier()
            stage_mlp(g)
        sc.finish()
    nc.used_inputs = g.used_inputs
    return nc


def stage_adaln(g):
    nc, sc, I, S = g.nc, g.sc, g.I, g.S
    ncol = g.cfg.get("ada_tiles", 48)
    with ExitStack() as st:
        cT = st.enter_context(nc.sbuf_tensor("a_cT", [128, NKC * 2], F32))
        sg = st.enter_context(nc.sbuf_tensor("a_sg", [128, NKC * 2], F32))
        L = st.enter_context(nc.sbuf_tensor("a_L", [128, NKC * 2, 128], BF16))
        brow = st.enter_context(nc.sbuf_tensor("a_brow", [1, 6 * D], F32))
        W = [st.enter_context(nc.sbuf_tensor("a_W%d" % i, [128, NKC, 512], BF16)) for i in range(2)]
        ot = [st.enter_context(nc.sbuf_tensor("a_ot%d" % i, [1, 512], F32)) for i in range(4)]
        b_c, b_L, b_brow = Buf("a_c"), Buf("a_L"), Buf("a_brow")
        b_W = [Buf("a_W0"), Buf("a_W1")]
        b_ot = [Buf("a_ot%d" % i) for i in range(4)]
        sc.dma("sp", cT[:], I.cT[:, :], b_c, writes=[b_c])
        sc.dma("sp", brow[:], I.b_ada[:, :], b_brow, writes=[b_brow])
        sc.op("act", lambda e: e.activation(out=sg[:], in_=cT[:], func=AF.Sigmoid), reads=[b_c], writes=[b_L])
        sc.op("dve", lambda e: e.tensor_tensor(out=sg[:], in0=sg[:], in1=cT[:], op=ALU.mult), reads=[b_c, b_L], writes=[b_L])
        sc.op("dve", lambda e: e.tensor_copy(out=L[:], in_=sg[:].unsqueeze(2).to_broadcast([128, NKC * 2, 128])),
              reads=[b_L], writes=[b_L])
        wv = I.w_ada.rearrange("(kc p) n -> p kc n", p=128)
        k = 0
        for n in range(ncol):
            wi = n % 2
            sc.dma("pool", W[wi][:], wv[:, :, n * 512:(n + 1) * 512], b_W[wi], writes=[b_W[wi]])
            for m in range(2):
                pb = 4 + (k % 2)
                oi = k % 4
                k += 1

                def mm(e, m=m, wi=wi, pb=pb):
                    ins = None
                    for kc in range(NKC):
                        ins = e.matmul(g.ps[pb][:, :], lhsT=L[:, kc * 2 + m, :], rhs=W[wi][:, kc, :],
                                       start=(kc == 0), stop=(kc == NKC - 1))
                    return ins
                sc.op("pe", mm, reads=[b_L, b_W[wi]], writes=[g.psb[pb]])
                sc.op("dve", lambda e, pb=pb, oi=oi, n=n: e.tensor_tensor(
                    out=ot[oi][:], in0=g.ps[pb][0:1, :], in1=brow[0:1, n * 512:(n + 1) * 512], op=ALU.add),
                    reads=[g.psb[pb], b_brow], writes=[b_ot[oi]])
                sc.dma("sp", S.modraw[m:m + 1, n * 512:(n + 1) * 512], ot[oi][:], b_ot[oi],
                       reads=[b_ot[oi]], writes=[g.dbuf["modraw"]])


def load_modcols(g, st, mode, which_g, i_sc, i_sh, name):
    nc, sc, I, S = g.nc, g.sc, g.I, g.S
    t = st.enter_context(nc.sbuf_tensor(name, [128, 3, NKC], F32))
    rr = st.enter_context(nc.sbuf_tensor(name + "_r", [32, 2, 128], F32))
    b = Buf(name)
    b_rr = Buf(name + "_r")
    sc.dma("sp", rr[:, 0, :], S.modraw[mode, i_sc * D:(i_sc + 1) * D].rearrange("(kc p) -> kc p", p=128), b_rr,
           reads=[g.dbuf["modraw"]], writes=[b_rr])
    sc.dma("sp", rr[:, 1, :], S.modraw[mode, i_sh * D:(i_sh + 1) * D].rearrange("(kc p) -> kc p", p=128), b_rr,
           reads=[g.dbuf["modraw"]], writes=[b_rr])

    def tr(e):
        e.transpose(g.ps[7][:, 0:32], rr[:, 0, :], g.ident_f[0:32, 0:32])
        return e.transpose(g.ps[7][:, 32:64], rr[:, 1, :], g.ident_f[0:32, 0:32])
    sc.op("pe", tr, reads=[b_rr, g.b_ident], writes=[g.psb[7]])
    sc.op("dve", lambda e: e.tensor_copy(out=t[:, 0:2, :].rearrange("p a k -> p (a k)"), in_=g.ps[7][:, 0:64]),
          reads=[g.psb[7]], writes=[b])
    sc.dma("sp", t[:, 2, :], I.gcols[:, which_g * NKC:(which_g + 1) * NKC], b, writes=[b])
    sc.op("dve", lambda e: e.scalar_tensor_tensor(out=t[:, 0, :], in0=t[:, 0, :], scalar=1.0, in1=t[:, 2, :],
                                                  op0=ALU.add, op1=ALU.mult), reads=[b], writes=[b])
    return t, b


def norm_transpose(g, xt, b_x, modc, b_modc, hT, b_hT, tt, scratch, b_scr, junk, b_junk):
    sc = g.sc
    ss, rstd = scratch
    sc.op("act", lambda e: e.activation(out=junk[:], in_=xt[:], func=AF.Square, accum_out=ss[:]),
          reads=[b_x], writes=[b_scr, b_junk])
    sc.op("act", lambda e: e.activation(out=rstd[:], in_=ss[:], func=AF.Sqrt, scale=1.0 / D, bias=EPS),
          reads=[b_scr], writes=[b_scr])
    sc.op("dve", lambda e: e.reciprocal(out=rstd[:], in_=rstd[:]), reads=[b_scr], writes=[b_scr])
    sc.op("dve", lambda e: e.tensor_scalar(out=xt[:], in0=xt[:], scalar1=rstd[:, 0:1], scalar2=None, op0=ALU.mult),
          reads=[b_x, b_scr], writes=[b_x])
    for q in range(8):
        pb = 4 + (q % 2)

        def tr(e, q=q, pb=pb):
            ins = None
            for i in range(4):
                kc = q * 4 + i
                ins = e.transpose(g.ps[pb][:, i * 128:(i + 1) * 128], xt[:, kc * 128:(kc + 1) * 128], g.ident_f[:])
            return ins
        sc.op("pe", tr, reads=[b_x, g.b_ident], writes=[g.psb[pb]])
        for i in range(4):
            kc = q * 4 + i
            ek = "act" if (i % 2 == 0) else "dve"
            if ek == "act":
                sc.op("act", lambda e, kc=kc, i=i, pb=pb: e.activation(
                    out=hT[:, kc, tt * 128:(tt + 1) * 128], in_=g.ps[pb][:, i * 128:(i + 1) * 128],
                    func=AF.Identity, scale=modc[:, 0, kc:kc + 1], bias=modc[:, 1, kc:kc + 1]),
                    reads=[g.psb[pb], b_modc], writes=[b_hT])
            else:
                sc.op("dve", lambda e, kc=kc, i=i, pb=pb: e.tensor_scalar(
                    out=hT[:, kc, tt * 128:(tt + 1) * 128], in0=g.ps[pb][:, i * 128:(i + 1) * 128],
                    scalar1=modc[:, 0, kc:kc + 1], scalar2=modc[:, 1, kc:kc + 1], op0=ALU.mult, op1=ALU.add),
                    reads=[g.psb[pb], b_modc], writes=[b_hT])


def tok_src(g, t0):
    if t0 < NP_TOK:
        return g.I.xp[t0:t0 + 128, :]
    return g.I.xs[t0 - NP_TOK:t0 - NP_TOK + 128, :]


def stage_inproj(g):
    nc, sc, I, S, O = g.nc, g.sc, g.I, g.S, g.O
    GT = 1024
    NTT = GT // 128
    ngroups = g.cfg.get("in_groups", list(range(NT // GT)))
    with ExitStack() as st:
        xts = [st.enter_context(nc.sbuf_tensor("i_x%d" % i, [128, D], F32)) for i in range(2)]
        junk = st.enter_context(nc.sbuf_tensor("i_junk", [128, D], BF16))
        hT = st.enter_context(nc.sbuf_tensor("i_hT", [128, NKC, GT], BF16))
        W = [st.enter_context(nc.sbuf_tensor("i_W%d" % i, [128, NKC, 512], BF16)) for i in range(2)]
        NF = 4
        sf32 = [st.enter_context(nc.sbuf_tensor("i_sf%d" % i, [128, 512], F32)) for i in range(NF)]
        sb16 = [st.enter_context(nc.sbuf_tensor("i_sb%d" % i, [128, 512], BF16)) for i in range(NF)]
        rt1 = st.enter_context(nc.sbuf_tensor("i_rt1", [128, 512], F32))
        rt2 = st.enter_context(nc.sbuf_tensor("i_rt2", [128, 512], F32))
        qts = [st.enter_context(nc.sbuf_tensor("i_qts%d" % i, [128, 4, GT], BF16)) for i in range(1)]
        rc = st.enter_context(nc.sbuf_tensor("i_rc", [128, NTT, 128], F32))
        rs = st.enter_context(nc.sbuf_tensor("i_rs", [128, NTT, 128], F32))
        b_xs, b_junk, b_hT = [Buf("i_x0"), Buf("i_x1")], Buf("i_junk"), Buf("i_hT")
        sss = [st.enter_context(nc.sbuf_tensor("i_ss%d" % i, [128, 1], F32)) for i in range(2)]
        rstds = [st.enter_context(nc.sbuf_tensor("i_rstd%d" % i, [128, 1], F32)) for i in range(2)]
        b_scrs = [Buf("i_scr0"), Buf("i_scr1")]
        b_W = [Buf("i_W0"), Buf("i_W1")]
        b_sf = [Buf("i_sf%d" % i) for i in range(NF)]
        b_sb = [Buf("i_sb%d" % i) for i in range(NF)]
        b_rt, b_rope = Buf("i_rt"), Buf("i_rope")
        b_qts = [Buf("i_qts0")]
        modc = {}
        modc[0] = load_modcols(g, st, 0, 0, 1, 0, "i_modp")
        modc[1] = load_modcols(g, st, 1, 0, 1, 0, "i_mods")
        wv = I.w_in.rearrange("(kc p) n -> p kc n", p=128)
        cnt = dict(w=0, ps=0, sf=0, sb=0, q=0)
        pending = []

        def qk_post(xb, bi, tb, TP, qi, tt, kind, ci, t0g):
            def tr(e):
                ins = None
                for bk in range(4):
                    ins = e.transpose(TP[:, bk * 128:(bk + 1) * 128], xb[:, bk * 128:(bk + 1) * 128], g.ident_b[:])
                return ins
            sc.op("pe", tr, reads=[b_sb[bi], g.b_ident], writes=[g.psb[tb]])
            sc.op("act", lambda e: e.copy(out=qts[qi][:, :, tt * 128:(tt + 1) * 128],
                                          in_=TP[:, 0:512].rearrange("p (b t) -> p b t", b=4)),
                  reads=[g.psb[tb]], writes=[b_qts[qi]])
            if tt == NTT - 1:
                dst = S.QT if kind == "q" else S.KT
                sc.dma("sp", dst[ci * 4:(ci + 1) * 4, :, t0g:t0g + GT].rearrange("b d t -> d b t"),
                       qts[qi][:], b_qts[qi], reads=[b_qts[qi]], writes=[g.dbuf["QT" if kind == "q" else "KT"]])
        for grp in ngroups:
            t0g = grp * GT
            prompt = t0g < NP_TOK
            remote = t0g >= NOWN
            mode = 0 if prompt else 1
            for tt in range(NTT):
                xi = tt % 2
                sc.dma("sp", xts[xi][:], tok_src(g, t0g + tt * 128), b_xs[xi], writes=[b_xs[xi]])
                norm_transpose(g, xts[xi], b_xs[xi], modc[mode][0], modc[mode][1], hT, b_hT, tt, (sss[xi], rstds[xi]), b_scrs[xi],
                               junk, b_junk)
            if not prompt:
                ls = t0g - NP_TOK
                sc.dma("sp", rc[:], I.ropec[ls:ls + GT, :].rearrange("(t p) d -> p t d", p=128), b_rope, writes=[b_rope])
                sc.dma("sp", rs[:], I.ropes[ls:ls + GT, :].rearrange("(t p) d -> p t d", p=128), b_rope, writes=[b_rope])
            tiles = []
            if not remote:
                tiles += [("q", c * 512, 512, c) for c in range(4)]
            tiles += [("k", 2048 + c * 512, 512, c) for c in range(4)]
            tiles += [("v", 4096 + c * 512, 512, c) for c in range(4)]
            if not remote:
                tiles += [("z", 6144 + c * 512, 512, c) for c in range(4)]
            tiles += [("x", 8192 + c * 512, 512, c) for c in range(4)]
            tiles += [("B", 8192 + 2048, 512, 0)]
            tiles += [("C", 8192 + 2560, 512, 0)]
            tiles += [("dt", 11264, 64, 0)]
            kinds_ok = g.cfg.get('in_kinds')
            if kinds_ok is not None:
                tiles = [t_ for t_ in tiles if t_[0] in kinds_ok]
            wslot = {}

            def load_w(ti):
                kind_, c0_, ncols_, ci_ = tiles[ti]
                wi_ = cnt["w"] % 2
                cnt["w"] += 1
                wslot[ti] = wi_
                sc.dma("pool", W[wi_][:, :, 0:ncols_], wv[:, :, c0_:c0_ + ncols_], b_W[wi_], writes=[b_W[wi_]])
            if tiles:
                load_w(0)
            for ti, (kind, c0, ncols, ci) in enumerate(tiles):
                if ti + 1 < len(tiles):
                    load_w(ti + 1)
                wi = wslot[ti]
                qi = None
                if kind in ("q", "k"):
                    qi = 0
                    cnt["q"] += 1
                for tt in range(NTT):
                    t0 = t0g + tt * 128
                    pb = cnt["ps"] % 4
                    cnt["ps"] += 1

                    def mm(e, tt=tt, wi=wi, pb=pb, ncols=ncols):
                        ins = None
                        for kc in range(NKC):
                            ins = e.matmul(g.ps[pb][:, 0:ncols], lhsT=hT[:, kc, tt * 128:(tt + 1) * 128],
                                           rhs=W[wi][:, kc, 0:ncols], start=(kc == 0), stop=(kc == NKC - 1))
                        return ins
                    sc.op("pe", mm, reads=[b_hT, b_W[wi]], writes=[g.psb[pb]])
                    while pending:
                        pending.pop(0)()
                    P = g.ps[pb]
                    bP = g.psb[pb]
                    if kind in ("q", "k"):
                        bi = cnt["sb"] % NF
                        cnt["sb"] += 1
                        xb = sb16[bi]
                        if prompt:
                            if kind == "k":
                                fi = cnt["sf"] % NF
                                cnt["sf"] += 1
                                sc.op("act", lambda e, fi=fi, P=P: e.copy(out=sf32[fi][:], in_=P[:, :]),
                                      reads=[bP], writes=[b_sf[fi]])
                                sc.dma("sp", O.nk[t0:t0 + 128, ci * 512:(ci + 1) * 512], sf32[fi][:], b_sf[fi],
                                       reads=[b_sf[fi]], writes=[g.dbuf["out"]])
                                sc.op("dve", lambda e, xb=xb, fi=fi: e.tensor_copy(out=xb[:], in_=sf32[fi][:]),
                                      reads=[b_sf[fi]], writes=[b_sb[bi]])
                            else:
                                sc.op("dve", lambda e, xb=xb, P=P: e.tensor_copy(out=xb[:], in_=P[:, :]),
                                      reads=[bP], writes=[b_sb[bi]])
                        else:
                            Cb = rc[:, tt, :].unsqueeze(1).to_broadcast([128, 4, 128])
                            sc.op("dve", lambda e, P=P, Cb=Cb: e.tensor_tensor(
                                out=rt1[:].rearrange("p (b d) -> p b d", b=4), in0=P[:, :].rearrange("p (b d) -> p b d", b=4),
                                in1=Cb, op=ALU.mult), reads=[bP, b_rope], writes=[b_rt])
                            Pv = P[:, :].rearrange("p (b a h d) -> p b a h d", b=4, a=2, h=2)
                            t2v = rt2[:].rearrange("p (b a h d) -> p b a h d", b=4, a=2, h=2)
                            Sv = rs[:, tt, :].rearrange("p (a h d) -> p a h d", a=2, h=2)
                            for hh in range(2):
                                sc.op("dve", lambda e, hh=hh, Pv=Pv, t2v=t2v, Sv=Sv: e.tensor_tensor(
                                    out=t2v[:, :, :, hh, :], in0=Pv[:, :, :, 1 - hh, :],
                                    in1=Sv[:, :, hh, :].unsqueeze(1).to_broadcast([128, 4, 2, 32]), op=ALU.mult),
                                    reads=[bP, b_rope, b_rt], writes=[b_rt])
                            sc.op("pool", lambda e, xb=xb: e.tensor_tensor(out=xb[:], in0=rt1[:], in1=rt2[:], op=ALU.add),
                                  reads=[b_rt], writes=[b_sb[bi]])
                        tb = 6 + (cnt["ps"] % 2)
                        TP = g.ps[tb][:, :].bitcast(BF16)
                        pending.append(lambda xb=xb, bi=bi, tb=tb, TP=TP, qi=qi, tt=tt, kind=kind, ci=ci, t0g=t0g:
                                       qk_post(xb, bi, tb, TP, qi, tt, kind, ci, t0g))
                        continue

                        def tr(e, xb=xb, TP=TP):
                            ins = None
                            for bk in range(4):
                                ins = e.transpose(TP[:, bk * 128:(bk + 1) * 128], xb[:, bk * 128:(bk + 1) * 128], g.ident_b[:])
                            return ins
                        sc.op("pe", tr, reads=[b_sb[bi], g.b_ident], writes=[g.psb[tb]])
                        sc.op("act", lambda e, TP=TP, qi=qi, tt=tt: e.copy(
                            out=qts[qi][:, :, tt * 128:(tt + 1) * 128],
                            in_=TP[:, 0:512].rearrange("p (b t) -> p b t", b=4)),
                            reads=[g.psb[tb]], writes=[b_qts[qi]])
                        if tt == NTT - 1:
                            dst = S.QT if kind == "q" else S.KT
                            sc.dma("sp", dst[ci * 4:(ci + 1) * 4, :, t0g:t0g + GT].rearrange("b d t -> d b t"),
                                   qts[qi][:], b_qts[qi], reads=[b_qts[qi]], writes=[g.dbuf["QT" if kind == "q" else "KT"]])
                    elif kind == "v":
                        if prompt and not g.cfg.get("skip_nv"):
                            fi = cnt["sf"] % NF
                            cnt["sf"] += 1
                            sc.op("act", lambda e, fi=fi, P=P: e.copy(out=sf32[fi][:], in_=P[:, :]),
                                  reads=[bP], writes=[b_sf[fi]])
                            sc.dma("sp", O.nv[t0:t0 + 128, ci * 512:(ci + 1) * 512], sf32[fi][:], b_sf[fi],
                                   reads=[b_sf[fi]], writes=[g.dbuf["out"]])
                        bi = cnt["sb"] % NF
                        cnt["sb"] += 1
                        if prompt and not g.cfg.get("skip_nv"):
                            sc.op("dve", lambda e, bi=bi, fi=fi: e.tensor_copy(out=sb16[bi][:], in_=sf32[fi][:]),
                                  reads=[b_sf[fi]], writes=[b_sb[bi]])
                        else:
                            sc.op("dve", lambda e, bi=bi, P=P: e.tensor_copy(out=sb16[bi][:], in_=P[:, :]),
                                  reads=[bP], writes=[b_sb[bi]])
                        if not g.cfg.get("skip_V"):
                            sc.dma("sp", S.V[t0:t0 + 128, ci * 512:(ci + 1) * 512], sb16[bi][:], b_sb[bi],
                                   reads=[b_sb[bi]], writes=[g.dbuf["V"]])
                    else:
                        fi = cnt["sf"] % NF
                        cnt["sf"] += 1
                        ek = "act" if (cnt["sf"] % 2 == 0) else "dve"
                        if ek == "act":
                            sc.op("act", lambda e, fi=fi, P=P, ncols=ncols: e.copy(out=sf32[fi][:, 0:ncols], in_=P[:, 0:ncols]),
                                  reads=[bP], writes=[b_sf[fi]])
                        else:
                            sc.op("dve", lambda e, fi=fi, P=P, ncols=ncols: e.tensor_copy(out=sf32[fi][:, 0:ncols], in_=P[:, 0:ncols]),
                                  reads=[bP], writes=[b_sf[fi]])
                        if kind == "z":
                            dst = S.Z[t0:t0 + 128, ci * 512:(ci + 1) * 512]
                            src = sf32[fi][:]
                            dn = "Z"
                        elif kind == "x":
                            dst = S.XBC[t0:t0 + 128, ci * 768:ci * 768 + 512]
                            src = sf32[fi][:]
                            dn = "XBC"
                        elif kind in ("B", "C"):
                            off = 512 if kind == "B" else 640
                            dst = S.XBC[t0:t0 + 128, :].rearrange("t (g c) -> t g c", g=4)[:, :, off:off + 128]
                            src = sf32[fi][:].rearrange("p (g c) -> p g c", g=4)
                            dn = "XBC"
                        else:
                            dst = S.DT[t0:t0 + 128, :]
                            src = sf32[fi][:, 0:64]
                            dn = "DT"
                        sc.dma(g.cfg.get("zq", "sp"), dst, src, b_sf[fi], reads=[b_sf[fi]], writes=[g.dbuf[dn]])
            while pending:
                pending.pop(0)()


def _rope_tables(pos):
    pos = np.asarray(pos)
    row = (pos // 64).astype(np.float32)
    col = (pos % 64).astype(np.float32)
    inv = (1.0 / (np.float32(10000.0) ** (np.arange(0, 64, 2, dtype=np.float32) / np.float32(64)))).astype(np.float32)
    ar = row[:, None] * inv[None, :]
    ac = col[:, None] * inv[None, :]
    cr, sr, cc, s_c = np.cos(ar), np.sin(ar), np.cos(ac), np.sin(ac)
    C = np.concatenate([cr, cr, cc, cc], axis=1).astype(np.float32)
    Ssg = np.concatenate([-sr, sr, -s_c, s_c], axis=1).astype(np.float32)
    return np.ascontiguousarray(C), np.ascontiguousarray(Ssg)


def prep_inputs(inp):
    f = lambda a: np.ascontiguousarray(np.asarray(a, dtype=np.float32))
    x_prompt, x_sample = f(inp["x_prompt"]), f(inp["x_sample"])
    w_in = f(inp["w_in"])[0]
    w_in_odd = w_in.copy()
    w_in_odd[:, 11264:11296] = w_in[:, 11296:11328]
    w_in_odd[:, 11296:11328] = w_in[:, 11264:11296]
    shared = dict(
        w_ada=f(inp["w_ada"])[0], b_ada=f(inp["b_ada"]).reshape(1, -1),
        w_out=f(inp["w_out"])[0], w_up=f(inp["w_up"])[0], w_down=f(inp["w_down"])[0],
        lam=np.concatenate([f(inp[k]).reshape(-1) for k in ("lambda_q1", "lambda_k1", "lambda_q2", "lambda_k2")]).reshape(1, -1),
        g_subln=f(inp["g_subln"]).reshape(1, -1), conv_b=f(inp["conv_b"]).reshape(1, -1),
        d_skip=f(inp["d_skip"]).reshape(1, -1), g_ssm=f(inp["g_ssm_norm"]).reshape(1, -1),
        ident=np.eye(128, dtype=np.float32),
    )
    grows = np.stack([f(inp[k])[0] for k in ("g_mix_pre", "g_mix_post", "g_mlp_pre", "g_mlp_post")])
    shared["grows"] = np.ascontiguousarray(grows)
    shared["gcols"] = np.ascontiguousarray(grows.reshape(4, NKC, 128).transpose(2, 0, 1).reshape(128, 4 * NKC))
    conv_w = f(inp["conv_w"])[0]
    a_log, dt_bias = f(inp["a_log"])[0], f(inp["dt_bias"])[0]
    maps = []
    for i in range(8):
        b, hf = i // 2, i % 2
        rev = hf == 1
        m = dict(shared)
        xp = x_prompt[4 * i:4 * i + 4]
        xs = x_sample[b]
        pos = np.arange(NS_ALL)
        if rev:
            xp = xp[:, ::-1]
            xs = xs[::-1]
            pos = pos[::-1]
        m["xp"] = np.ascontiguousarray(xp.reshape(NP_TOK, D))
        m["xs"] = np.ascontiguousarray(xs)
        cc = np.stack([f(inp["c_ctx"]), f(inp["c"])[b]])
        m["cT"] = np.ascontiguousarray(cc.reshape(2, NKC, 128).transpose(2, 1, 0).reshape(128, NKC * 2))
        m["ck"] = np.ascontiguousarray(f(inp["cache_k"])[b, 0].reshape(PAST, 2048))
        m["cv"] = np.ascontiguousarray(f(inp["cache_v"])[b, 0].reshape(PAST, 2048))
        sfw = f(inp["state_ssm_fwd"])[b, 0].reshape(32 * 64, 128)
        sbw = f(inp["state_ssm_bwd"])[b, 0].reshape(32 * 64, 128)
        m["sf"], m["sb"] = (sbw, sfw) if rev else (sfw, sbw)
        m["w_in"] = w_in_odd if rev else w_in
        m["conv_w"] = np.ascontiguousarray(conv_w[::-1]) if rev else conv_w
        m["a_log"] = np.ascontiguousarray((a_log[::-1] if rev else a_log).reshape(1, 64))
        m["dt_bias"] = np.ascontiguousarray((dt_bias[::-1] if rev else dt_bias).reshape(1, 64))
        m["ropec"], m["ropes"] = _rope_tables(pos)
        maps.append(m)
    return maps


_NC_CACHE = {}


def kernel(**inputs):
    maps = prep_inputs(inputs)
    if "nc" not in _NC_CACHE:
        _NC_CACHE["nc"] = build()
    nc = _NC_CACHE["nc"]
    res = run_bass_kernel_spmd(nc, maps, core_ids=list(range(8)))
    R = res.results
    yp = np.zeros((32, 256, D), np.float32)
    ys = np.zeros((4, NS_ALL, D), np.float32)
    nk = np.zeros((32, 1, 256, 8, 2, 128), np.float32)
    nv = np.zeros((32, 1, 256, 8, 256), np.float32)
    nsf = np.zeros((32, 1, 32, 64, 128), np.float32)
    nsb = np.zeros((32, 1, 32, 64, 128), np.float32)
    for i in range(8):
        b, hf = i // 2, i % 2
        r = R[i]
        ypc = np.asarray(r["o_yp"]).reshape(4, 256, D)
        ysc = np.asarray(r["o_ys"])
        nkc = np.asarray(r["o_nk"]).reshape(4, 256, 8, 2, 128)
        nvc = np.asarray(r["o_nv"]).reshape(4, 256, 8, 256)
        f_ = np.asarray(r["o_nsf"]).reshape(4, 32, 64, 128)
        b_ = np.asarray(r["o_nsb"]).reshape(4, 32, 64, 128)
        if hf == 1:
            ypc, nkc, nvc, ysc = ypc[:, ::-1], nkc[:, ::-1], nvc[:, ::-1], ysc[::-1]
            f_, b_ = b_, f_
            ys[b, 2048:] = ysc
        else:
            ys[b, :2048] = ysc
        yp[4 * i:4 * i + 4] = ypc
        nk[4 * i:4 * i + 4, 0] = nkc
        nv[4 * i:4 * i + 4, 0] = nvc
        nsf[4 * i:4 * i + 4, 0] = f_
        nsb[4 * i:4 * i + 4, 0] = b_
    return yp, ys, nk, nv, nsf, nsb


def stage_ctxkv(g):
    nc, sc, I, S = g.nc, g.sc, g.I, g.S
    with ExitStack() as st:
        xf = st.enter_context(nc.sbuf_tensor("c_xf", [128, 2048], F32))
        xb = st.enter_context(nc.sbuf_tensor("c_xb", [128, 2048], BF16))
        kts = st.enter_context(nc.sbuf_tensor("c_kts", [128, 16, 128], BF16))
        b_xf, b_xb, b_kts = Buf("c_xf"), Buf("c_xb"), Buf("c_kts")
        for kt in range(PAST // 128):
            r0 = kt * 128
            sc.dma("sp", xf[:], I.ck[r0:r0 + 128, :], b_xf, writes=[b_xf])
            sc.op("dve", lambda e: e.tensor_copy(out=xb[:], in_=xf[:]), reads=[b_xf], writes=[b_xb])
            for q in range(2):
                tb = 6 + q
                TP = g.ps[tb][:, :].bitcast(BF16)

                def tr(e, q=q, TP=TP):
                    ins = None
                    for i in range(8):
                        bk = q * 8 + i
                        ins = e.transpose(TP[:, i * 128:(i + 1) * 128], xb[:, bk * 128:(bk + 1) * 128], g.ident_b[:])
                    return ins
                sc.op("pe", tr, reads=[b_xb, g.b_ident], writes=[g.psb[tb]])
                sc.op("act", lambda e, q=q, TP=TP: e.copy(out=kts[:, q * 8:(q + 1) * 8, :],
                                                          in_=TP[:, :].rearrange("p (b t) -> p b t", b=8)),
                      reads=[g.psb[tb]], writes=[b_kts])
            sc.dma("sp", S.KT[:, :, NT + r0:NT + r0 + 128].rearrange("b d t -> d b t"), kts[:], b_kts,
                   reads=[b_kts], writes=[g.dbuf["KT"]])
            sc.dma("sp", xf[:], I.cv[r0:r0 + 128, :], b_xf, writes=[b_xf])
            sc.op("dve", lambda e: e.tensor_copy(out=xb[:], in_=xf[:]), reads=[b_xf], writes=[b_xb])
            sc.dma("sp", S.V[NT + r0:NT + r0 + 128, :], xb[:], b_xb, reads=[b_xb], writes=[g.dbuf["V"]])


def stage_ssd(g):
    nc, sc, I, S, O = g.nc, g.sc, g.I, g.S, g.O
    seqs = g.cfg.get("ssd_seqs", [0, 1, 2, 3, 4])
    groups = g.cfg.get("ssd_groups", [0, 1, 2, 3])
    NCH = 16
    with ExitStack() as st:
        def sb(name, shape, dt=F32):
            return st.enter_context(nc.sbuf_tensor("s_" + name, shape, dt))
        Uf, Ub = sb("Uf", [128, 128]), sb("Ub", [128, 128])
        Mf, Mb = sb("Mf", [128, 8, 128]), sb("Mb", [128, 8, 128])
        ones = sb("ones", [128, 128])
        b_const = Buf("s_const")

        def mk_consts(e):
            e.memset(ones[:], 1.0)
            e.memset(Uf[:], 1.0)
            e.memset(Ub[:], 1.0)
            e.memset(Mf[:], 0.0)
            e.memset(Mb[:], 0.0)
            e.affine_select(out=Uf[:], in_=Uf[:], pattern=[[1, 128]], compare_op=ALU.is_ge, fill=0.0, base=0, channel_multiplier=-1)
            e.affine_select(out=Ub[:], in_=Ub[:], pattern=[[-1, 128]], compare_op=ALU.is_ge, fill=0.0, base=0, channel_multiplier=1)
            e.affine_select(out=Mf[:], in_=Mf[:], pattern=[[0, 8], [1, 128]], compare_op=ALU.is_ge, fill=-30000.0, base=0, channel_multiplier=-1)
            return e.affine_select(out=Mb[:], in_=Mb[:], pattern=[[0, 8], [-1, 128]], compare_op=ALU.is_ge, fill=-30000.0, base=0, channel_multiplier=1)
        sc.op("pool", mk_consts, writes=[b_const])
        dtb, Ab, dsk = sb("dtb", [128, 64]), sb("Ab", [128, 64]), sb("dsk", [128, 32])
        sc.dma("sp", dtb[:], I.dt_bias[0, :].partition_broadcast(128), b_const, writes=[b_const])
        sc.dma("sp", Ab[:], I.a_log[0, :].partition_broadcast(128), b_const, writes=[b_const])
        sc.dma("sp", dsk[:], I.d_skip[0, :].partition_broadcast(128), b_const, writes=[b_const])
        sc.op("act", lambda e: e.activation(out=Ab[:], in_=Ab[:], func=AF.Exp), reads=[b_const], writes=[b_const])
        sc.op("dve", lambda e: e.tensor_scalar(out=Ab[:], in0=Ab[:], scalar1=-1.0, scalar2=None, op0=ALU.mult),
              reads=[b_const], writes=[b_const])
        W5, cbias, gssm = sb("W5", [128, 5, 768]), sb("cbias", [128, 768]), sb("gssm", [128, 512])
        b_gc = Buf("s_gc")
        xs = [[sb("xs%d_%d" % (i, j), [128, 768]) for j in range(5)] for i in range(2)]
        b_xs = [[Buf("s_xs%d_%d" % (i, j)) for j in range(5)] for i in range(2)]
        acc, xc = sb("acc", [128, 768]), sb("xc", [128, 768])
        xcb = sb("xcb", [128, 768], BF16)
        b_acc, b_xc = Buf("s_acc"), Buf("s_xc")
        dtr = [sb("dtr%d" % i, [128, 64]) for i in range(2)]
        b_dtr = [Buf("s_dtr0"), Buf("s_dtr1")]
        sm = {n: sb(n, [128, 64]) for n in ("ab", "ex", "dt", "a", "tot", "cs", "ncs", "ecs", "dcs", "cdec", "w2")}
        b_sm = Buf("s_sm")
        BT, CT = sb("BT", [128, 128], BF16), sb("CT", [128, NCH, 128], BF16)
        b_BT, b_CT = Buf("s_BT"), [Buf("s_CT%d" % i) for i in range(NCH)]
        AU = [sb("AU%d" % d, [128, 8, 128]) for d in range(2)]
        b_AU = [Buf("s_AU0"), Buf("s_AU1")]
        Lm = [sb("Lm%d" % d, [128, 8, 128]) for d in range(2)]
        b_Lm = [Buf("s_Lm0"), Buf("s_Lm1")]
        wts = [sb("wts%d" % d, [128, 8, 128], BF16) for d in range(2)]
        b_wts = [Buf("s_wts0"), Buf("s_wts1")]
        cbT = sb("cbT", [128, 128])
        b_cbT = Buf("s_cbT")
        xdt = [sb("xdt%d" % d, [128, 512], BF16) for d in range(2)]
        xdd = [sb("xdd%d" % d, [128, 512], BF16) for d in range(2)]
        b_xdt, b_xdd = [Buf("s_xdt0"), Buf("s_xdt1")], [Buf("s_xdd0"), Buf("s_xdd1")]
        state = [sb("state%d" % d, [128, 512]) for d in range(2)]
        prevb = [sb("prevb%d" % d, [128, 512], BF16) for d in range(2)]
        b_state = [Buf("s_state0"), Buf("s_state1")]
        ysum = sb("ysum", [128, NCH, 512])
        Sb_st = sb("Sbst", [128, NCH, 512])
        ecs_st = sb("ecsst", [128, NCH, 8])
        cdec_st = sb("cdecst", [128, NCH, 8])
        b_ch = [Buf("s_ch%d" % i) for i in range(NCH)]
        tmp = sb("tmp", [128, 512])
        b_tmp = Buf("s_tmp")
        zt = sb("zt", [128, 512])
        b_zt = Buf("s_zt")
        yb = sb("yb", [128, 512], BF16)
        b_yb = Buf("s_yb")
        mxs = sb("mxs", [128, 4, 128], BF16)
        b_mxs = Buf("s_mxs")
        stT = sb("stT", [128, 4, 128])
        b_stT = Buf("s_stT")
        stin = sb("stin", [128, 4, 128])
        b_stin = Buf("s_stin")
        nrm = sb("nrm", [128, 2])
        b_nrm = Buf("s_nrm")
        junk = sb("junk", [128, 512])
        b_junk = Buf("s_junk")
        xsi = [0]

        def load_shift(t0, seq_lo, seq_hi, gi, ncols=768):
            i = xsi[0] % 2
            xsi[0] += 1
            for j in range(5):
                lo = t0 + j - 2
                hi = lo + 128
                p0 = max(0, seq_lo - lo)
                p1 = 128 - max(0, hi - seq_hi)
                if p0 > 0 or p1 < 128:
                    sc.op("pool", lambda e, i=i, j=j: e.memset(xs[i][j][:], 0.0), writes=[b_xs[i][j]])
                sc.dma("sp", xs[i][j][p0:p1, 0:ncols], S.XBC[lo + p0:lo + p1, gi * 768:gi * 768 + ncols], b_xs[i][j],
                       reads=[g.dbuf["XBC"]], writes=[b_xs[i][j]])
            return i

        def conv_silu(i, ncols=768):
            for j in range(5):
                ek = "pool" if j % 2 == 0 else "dve"
                sc.op(ek, lambda e, j=j: e.tensor_tensor(out=xs[i][j][:, 0:ncols], in0=xs[i][j][:, 0:ncols],
                                                          in1=W5[:, j, 0:ncols], op=ALU.mult),
                      reads=[b_xs[i][j], b_gc], writes=[b_xs[i][j]])
            sc.op("dve", lambda e: e.tensor_tensor(out=acc[:, 0:ncols], in0=xs[i][0][:, 0:ncols], in1=xs[i][1][:, 0:ncols], op=ALU.add),
                  reads=[b_xs[i][0], b_xs[i][1]], writes=[b_acc])
            for j in (2, 3, 4):
                sc.op("dve", lambda e, j=j: e.tensor_tensor(out=acc[:, 0:ncols], in0=acc[:, 0:ncols], in1=xs[i][j][:, 0:ncols], op=ALU.add),
                      reads=[b_xs[i][j], b_acc], writes=[b_acc])
            sc.op("dve", lambda e: e.tensor_tensor(out=acc[:, 0:ncols], in0=acc[:, 0:ncols], in1=cbias[:, 0:ncols], op=ALU.add),
                  reads=[b_acc, b_gc], writes=[b_acc])
            sc.op("act", lambda e: e.activation(out=xc[:, 0:ncols], in_=acc[:, 0:ncols], func=AF.Silu), reads=[b_acc], writes=[b_xc])
            sc.op("dve", lambda e: e.tensor_copy(out=xcb[:, 0:ncols], in_=xc[:, 0:ncols]), reads=[b_xc], writes=[b_xc])

        def dt_stuff(t0, gi, dirs):
            di = xsi[0] % 2
            sc.dma("sp", dtr[di][:], S.DT[t0:t0 + 128, :], b_dtr[di], reads=[g.dbuf["DT"]], writes=[b_dtr[di]])
            R = [b_dtr[di], b_const, b_sm]
            Wr = [b_sm]
            s_ = sm
            sc.op("dve", lambda e: e.tensor_tensor(out=s_["dt"][:], in0=dtr[di][:], in1=dtb[:], op=ALU.add), reads=R, writes=Wr)
            sc.op("act", lambda e: e.activation(out=s_["ab"][:], in_=s_["dt"][:], func=AF.Abs), reads=R, writes=Wr)
            sc.op("act", lambda e: e.activation(out=s_["ex"][:], in_=s_["ab"][:], func=AF.Exp, scale=-1.0), reads=R, writes=Wr)
            sc.op("act", lambda e: e.activation(out=s_["ex"][:], in_=s_["ex"][:], func=AF.Ln, bias=1.0), reads=R, writes=Wr)
            sc.op("dve", lambda e: e.scalar_tensor_tensor(out=s_["dt"][:], in0=s_["dt"][:], scalar=0.0, in1=s_["ex"][:],
                                                          op0=ALU.max, op1=ALU.add), reads=R, writes=Wr)
            sc.op("dve", lambda e: e.tensor_tensor(out=s_["a"][:], in0=s_["dt"][:], in1=Ab[:], op=ALU.mult), reads=R, writes=Wr)
            P6 = g.ps[6]
            c_f = slice(gi * 8, gi * 8 + 8)
            c_b = slice(32 + gi * 8, 32 + gi * 8 + 8)

            def mm(e):
                e.matmul(P6[:, 0:64], lhsT=ones[:], rhs=s_["a"][:], start=True, stop=True)
                e.matmul(P6[:, 64:72], lhsT=Uf[:], rhs=s_["a"][:, c_f], start=True, stop=True)
                return e.matmul(P6[:, 72:80], lhsT=Ub[:], rhs=s_["a"][:, c_b], start=True, stop=True)
            sc.op("pe", mm, reads=[b_sm, b_const], writes=[g.psb[6]])
            R2 = [g.psb[6], b_sm]
            sc.op("dve", lambda e: e.tensor_copy(out=s_["tot"][:, 0:8], in_=P6[:, c_f]), reads=R2, writes=Wr)
            sc.op("dve", lambda e: e.tensor_copy(out=s_["tot"][:, 8:16], in_=P6[:, c_b]), reads=R2, writes=Wr)
            sc.op("dve", lambda e: e.tensor_copy(out=s_["cs"][:, 0:16], in_=P6[:, 64:80]), reads=R2, writes=Wr)
            sc.op("dve", lambda e: e.tensor_scalar(out=s_["ncs"][:, 0:16], in0=s_["cs"][:, 0:16], scalar1=-1.0, scalar2=None, op0=ALU.mult),
                  reads=R2, writes=Wr)
            sc.op("act", lambda e: e.activation(out=s_["ecs"][:, 0:16], in_=s_["cs"][:, 0:16], func=AF.Exp), reads=R2, writes=Wr)
            sc.op("dve", lambda e: e.tensor_tensor(out=s_["dcs"][:, 0:16], in0=s_["tot"][:, 0:16], in1=s_["cs"][:, 0:16], op=ALU.subtract),
                  reads=R2, writes=Wr)
            sc.op("act", lambda e: e.activation(out=s_["dcs"][:, 0:16], in_=s_["dcs"][:, 0:16], func=AF.Exp), reads=R2, writes=Wr)
            sc.op("act", lambda e: e.activation(out=s_["cdec"][:, 0:16], in_=s_["tot"][:, 0:16], func=AF.Exp), reads=R2, writes=Wr)
            sc.op("dve", lambda e: e.tensor_copy(out=s_["w2"][:, 0:8], in_=s_["dt"][:, c_f]), reads=R2, writes=Wr)
            sc.op("dve", lambda e: e.tensor_copy(out=s_["w2"][:, 8:16], in_=s_["dt"][:, c_b]), reads=R2, writes=Wr)
            sc.op("dve", lambda e: e.tensor_tensor(out=s_["w2"][:, 16:32], in0=s_["w2"][:, 0:16], in1=s_["dcs"][:, 0:16], op=ALU.mult),
                  reads=R2, writes=Wr)
            sc.op("dve", lambda e: e.tensor_copy(out=s_["w2"][:, 32:40], in_=s_["a"][:, c_f]), reads=R2, writes=Wr)
            sc.op("dve", lambda e: e.tensor_copy(out=s_["w2"][:, 40:48], in_=s_["a"][:, c_b]), reads=R2, writes=Wr)

        def xmul(dst, b_dst, col0, d):
            sc.op("dve", lambda e: e.tensor_tensor(
                out=dst[:].rearrange("p (h q) -> p h q", h=8), in0=xc[:, 0:512].rearrange("p (h q) -> p h q", h=8),
                in1=sm["w2"][:, col0 + d * 8:col0 + d * 8 + 8].unsqueeze(2).to_broadcast([128, 8, 64]), op=ALU.mult),
                reads=[b_xc, b_sm], writes=[b_dst])

        def chunk_states(d):
            xmul(xdd[d], b_xdd[d], 16, d)
            sc.op("pe", lambda e: e.matmul(g.ps[5][:, :], lhsT=xcb[:, 512:640], rhs=xdd[d][:], start=True, stop=True),
                  reads=[b_xc, b_xdd[d]], writes=[g.psb[5]])

        def state_update(d, S_ap, S_bufs, cdec_ap, cdec_bufs):
            sc.op("dve", lambda e: e.tensor_tensor(
                out=state[d][:].rearrange("p (h q) -> p h q", h=8), in0=state[d][:].rearrange("p (h q) -> p h q", h=8),
                in1=cdec_ap.unsqueeze(2).to_broadcast([128, 8, 64]), op=ALU.mult),
                reads=[b_state[d]] + cdec_bufs, writes=[b_state[d]])
            sc.op("dve", lambda e: e.tensor_tensor(out=state[d][:], in0=state[d][:], in1=S_ap, op=ALU.add),
                  reads=[b_state[d]] + S_bufs, writes=[b_state[d]])
            sc.op("pool", lambda e: e.tensor_copy(out=prevb[d][:], in_=state[d][:]), reads=[b_state[d]], writes=[b_state[d]])

        def init_state(d, src):
            if src is None:
                sc.op("pool", lambda e: e.memset(state[d][:], 0.0), writes=[b_state[d]])
                sc.op("pool", lambda e: e.memset(prevb[d][:], 0.0), writes=[b_state[d]])
                return
            sc.dma("sp", stin[:], src.rearrange("(c p) n -> p c n", p=128), b_stin, writes=[b_stin])

            def tr(e):
                ins = None
                for c in range(4):
                    ins = e.transpose(g.ps[5][:, c * 128:(c + 1) * 128], stin[:, c, :], g.ident_f[:])
                return ins
            sc.op("pe", tr, reads=[b_stin, g.b_ident], writes=[g.psb[5]])
            sc.op("dve", lambda e: e.tensor_copy(out=state[d][:], in_=g.ps[5][:, :]), reads=[g.psb[5]], writes=[b_state[d]])
            sc.op("pool", lambda e: e.tensor_copy(out=prevb[d][:], in_=state[d][:]), reads=[b_state[d]], writes=[b_state[d]])

        def out_state(d, dst):
            def tr(e):
                ins = None
                for c in range(4):
                    ins = e.transpose(g.ps[5][:, c * 128:(c + 1) * 128], state[d][:, c * 128:(c + 1) * 128], g.ident_f[:])
                return ins
            sc.op("pe", tr, reads=[b_state[d], g.b_ident], writes=[g.psb[5]])
            sc.op("dve", lambda e: e.tensor_copy(out=stT[:].rearrange("p c n -> p (c n)"), in_=g.ps[5][:, :]),
                  reads=[g.psb[5]], writes=[b_stT])
            sc.dma("sp", dst.rearrange("(c p) n -> p c n", p=128), stT[:], b_stT, reads=[b_stT], writes=[g.dbuf["out"]])

        def diag_part(d, c):
            U = Uf if d == 0 else Ub
            M = Mf if d == 0 else Mb
            a_col = 32 + d * 8
            sc.op("dve" if d == 0 else "pool", lambda e: e.tensor_tensor(
                out=AU[d][:], in0=sm["w2"][:, a_col:a_col + 8].unsqueeze(2).to_broadcast([128, 8, 128]),
                in1=U[:].unsqueeze(1).to_broadcast([128, 8, 128]), op=ALU.mult),
                reads=[b_sm, b_const], writes=[b_AU[d]])
            pb0 = 0 + 2 * d

            def mm(e):
                ins = None
                for hf in range(2):
                    e.matmul(g.ps[pb0 + hf][:, :], lhsT=ones[:], rhs=AU[d][:, hf * 4:(hf + 1) * 4, :].rearrange("p h l -> p (h l)"),
                             start=True, stop=False)
                    ins = e.matmul(g.ps[pb0 + hf][:, :], lhsT=g.ident_f[:], rhs=M[:, hf * 4:(hf + 1) * 4, :].rearrange("p h l -> p (h l)"),
                                   start=False, stop=True)
                return ins
            sc.op("pe", mm, reads=[b_AU[d], b_const, g.b_ident], writes=[g.psb[pb0], g.psb[pb0 + 1]])
            for h in range(8):
                sc.op("act", lambda e, h=h: e.activation(
                    out=Lm[d][:, h, :], in_=g.ps[pb0 + h // 4][:, (h % 4) * 128:(h % 4 + 1) * 128], func=AF.Exp,
                    bias=sm["ncs"][:, d * 8 + h:d * 8 + h + 1]),
                    reads=[g.psb[pb0], g.psb[pb0 + 1], b_sm], writes=[b_Lm[d]])
            sc.op("dve" if d == 1 else "pool", lambda e: e.tensor_tensor(
                out=wts[d][:], in0=Lm[d][:], in1=cbT[:].unsqueeze(1).to_broadcast([128, 8, 128]), op=ALU.mult),
                reads=[b_Lm[d], b_cbT], writes=[b_wts[d]])
            xmul(xdt[d], b_xdt[d], 0, d)

            def mm2(e):
                ins = None
                for h in range(8):
                    ins = e.matmul(g.ps[4][:, h * 64:(h + 1) * 64], lhsT=wts[d][:, h, :], rhs=xdt[d][:, h * 64:(h + 1) * 64],
                                   start=True, stop=True)
                return ins
            sc.op("pe", mm2, reads=[b_wts[d], b_xdt[d]], writes=[g.psb[4]])

        def yoff_add(d, c, ecs_ap, ecs_bufs):
            sc.op("pe", lambda e: e.matmul(g.ps[5][:, :], lhsT=CT[:, c, :], rhs=prevb[d][:], start=True, stop=True),
                  reads=[b_CT[c], b_state[d]], writes=[g.psb[5]])
            sc.op("dve", lambda e: e.tensor_tensor(
                out=tmp[:].rearrange("p (h q) -> p h q", h=8), in0=g.ps[5][:, :].rearrange("p (h q) -> p h q", h=8),
                in1=ecs_ap.unsqueeze(2).to_broadcast([128, 8, 64]), op=ALU.mult),
                reads=[g.psb[5]] + ecs_bufs, writes=[b_tmp])
            sc.op("pool", lambda e: e.tensor_tensor(out=ysum[:, c, :], in0=ysum[:, c, :], in1=tmp[:], op=ALU.add),
                  reads=[b_tmp, b_ch[c]], writes=[b_ch[c]])

        for gi in groups:
            with nc.allow_non_contiguous_dma(reason="broadcast const loads"):
                for (o0, c0, n) in ((0, gi * 512, 512), (512, 2048 + gi * 128, 128), (640, 2560 + gi * 128, 128)):
                    sc.dma("sp", W5[:, :, o0:o0 + n], I.conv_w[:, c0:c0 + n].partition_broadcast(128), b_gc, writes=[b_gc])
                    sc.dma("sp", cbias[:, o0:o0 + n], I.conv_b[0, c0:c0 + n].partition_broadcast(128), b_gc, writes=[b_gc])
                sc.dma("sp", gssm[:], I.g_ssm[0, gi * 512:(gi + 1) * 512].partition_broadcast(128), b_gc, writes=[b_gc])
            for seq in seqs:
                if seq < 4:
                    base, nown, nrem = seq * 256, 2, 0
                    seq_lo, seq_hi = base, base + 256
                    init_state(0, None)
                    init_state(1, None)
                else:
                    base, nown, nrem = NP_TOK, 16, 16
                    seq_lo, seq_hi = base, base + NS_ALL
                    init_state(0, I.sf[gi * 512:(gi + 1) * 512, :])
                    init_state(1, I.sb[gi * 512:(gi + 1) * 512, :])
                    for c in range(nown + nrem - 1, nown - 1, -1):
                        t0 = base + c * 128
                        g.conv_step(2)
                        i = load_shift(t0, seq_lo, seq_hi, gi, 640)
                        conv_silu(i, 640)
                        dt_stuff(t0, gi, (1,))
                        chunk_states(1)
                        state_update(1, g.ps[5][:, :], [g.psb[5]], sm["cdec"][:, 8:16], [b_sm])
                for c in range(nown):
                    t0 = base + c * 128
                    g.conv_step(2)
                    i = load_shift(t0, seq_lo, seq_hi, gi)
                    conv_silu(i)
                    dt_stuff(t0, gi, (0, 1))
                    TP = g.ps[7][:, :].bitcast(BF16)

                    def tr(e, TP=TP):
                        e.transpose(TP[:, 0:128], xcb[:, 512:640], g.ident_b[:])
                        return e.transpose(TP[:, 128:256], xcb[:, 640:768], g.ident_b[:])
                    sc.op("pe", tr, reads=[b_xc, g.b_ident], writes=[g.psb[7]])
                    sc.op("act", lambda e, TP=TP: e.copy(out=BT[:], in_=TP[:, 0:128]), reads=[g.psb[7]], writes=[b_BT])
                    sc.op("act", lambda e, TP=TP, c=c: e.copy(out=CT[:, c, :], in_=TP[:, 128:256]), reads=[g.psb[7]], writes=[b_CT[c]])
                    sc.op("pe", lambda e, c=c: e.matmul(g.ps[6][:, 128:256], lhsT=BT[:], rhs=CT[:, c, :], start=True, stop=True),
                          reads=[b_BT, b_CT[c]], writes=[g.psb[6]])
                    sc.op("act", lambda e: e.copy(out=cbT[:], in_=g.ps[6][:, 128:256]), reads=[g.psb[6]], writes=[b_cbT])
                    sc.op("dve", lambda e, c=c: e.tensor_tensor(
                        out=ysum[:, c, :].rearrange("p (h q) -> p h q", h=8), in0=xc[:, 0:512].rearrange("p (h q) -> p h q", h=8),
                        in1=dsk[:, gi * 8:(gi + 1) * 8].unsqueeze(2).to_broadcast([128, 8, 64]), op=ALU.mult),
                        reads=[b_xc, b_const], writes=[b_ch[c]])
                    for d in (0, 1):
                        diag_part(d, c)
                        sc.op("dve", lambda e, c=c: e.tensor_tensor(out=ysum[:, c, :], in0=ysum[:, c, :], in1=g.ps[4][:, :], op=ALU.add),
                              reads=[g.psb[4], b_ch[c]], writes=[b_ch[c]])
                    yoff_add(0, c, sm["ecs"][:, 0:8], [b_sm])
                    chunk_states(0)
                    state_update(0, g.ps[5][:, :], [g.psb[5]], sm["cdec"][:, 0:8], [b_sm])
                    chunk_states(1)
                    sc.op("act", lambda e, c=c: e.copy(out=Sb_st[:, c, :], in_=g.ps[5][:, :]), reads=[g.psb[5]], writes=[b_ch[c]])
                    sc.op("pool", lambda e, c=c: e.tensor_copy(out=ecs_st[:, c, :], in_=sm["ecs"][:, 8:16]), reads=[b_sm], writes=[b_ch[c]])
                    sc.op("pool", lambda e, c=c: e.tensor_copy(out=cdec_st[:, c, :], in_=sm["cdec"][:, 8:16]), reads=[b_sm], writes=[b_ch[c]])
                if seq < 4:
                    out_state(0, O.nsf[(seq * 32 + gi * 8) * 64:(seq * 32 + gi * 8 + 8) * 64, :])
                for c in range(nown - 1, -1, -1):
                    t0 = base + c * 128
                    yoff_add(1, c, ecs_st[:, c, :], [b_ch[c]])
                    state_update(1, Sb_st[:, c, :], [b_ch[c]], cdec_st[:, c, :], [b_ch[c]])
                    sc.dma("sp", zt[:], S.Z[t0:t0 + 128, gi * 512:(gi + 1) * 512], b_zt, reads=[g.dbuf["Z"]], writes=[b_zt])
                    sc.op("act", lambda e: e.activation(out=zt[:], in_=zt[:], func=AF.Silu), reads=[b_zt], writes=[b_zt])
                    sc.op("dve", lambda e, c=c: e.tensor_tensor(out=tmp[:], in0=ysum[:, c, :], in1=zt[:], op=ALU.mult),
                          reads=[b_ch[c], b_zt], writes=[b_tmp])
                    sc.op("act", lambda e: e.activation(out=junk[:], in_=tmp[:], func=AF.Square, accum_out=nrm[:, 0:1]),
                          reads=[b_tmp], writes=[b_junk, b_nrm])
                    sc.op("act", lambda e: e.activation(out=nrm[:, 1:2], in_=nrm[:, 0:1], func=AF.Sqrt, scale=1.0 / 512, bias=EPS),
                          reads=[b_nrm], writes=[b_nrm])
                    sc.op("dve", lambda e: e.reciprocal(out=nrm[:, 1:2], in_=nrm[:, 1:2]), reads=[b_nrm], writes=[b_nrm])
                    sc.op("dve", lambda e: e.scalar_tensor_tensor(out=yb[:], in0=tmp[:], scalar=nrm[:, 1:2], in1=gssm[:],
                                                                  op0=ALU.mult, op1=ALU.mult),
                          reads=[b_tmp, b_nrm, b_gc], writes=[b_yb])
                    TP = g.ps[7][:, :].bitcast(BF16)

                    def tr2(e, TP=TP):
                        ins = None
                        for k in range(4):
                            ins = e.transpose(TP[:, k * 128:(k + 1) * 128], yb[:, k * 128:(k + 1) * 128], g.ident_b[:])
                        return ins
                    sc.op("pe", tr2, reads=[b_yb, g.b_ident], writes=[g.psb[7]])
                    sc.op("act", lambda e, TP=TP: e.copy(out=mxs[:].rearrange("p k t -> p (k t)"), in_=TP[:, 0:512]),
                          reads=[g.psb[7]], writes=[b_mxs])
                    kc0 = 16 + gi * 4
                    sc.dma("sp", S.MIXT[kc0:kc0 + 4, :, t0:t0 + 128].rearrange("k d t -> d k t"), mxs[:], b_mxs,
                           reads=[b_mxs], writes=[g.dbuf["MIXT"]])
                if seq < 4:
                    out_state(1, O.nsb[(seq * 32 + gi * 8) * 64:(seq * 32 + gi * 8 + 8) * 64, :])


def stage_attn(g):
    nc, sc, I, S = g.nc, g.sc, g.I, g.S
    seqs = g.cfg.get("attn_seqs", [0, 1, 2, 3, 4])
    heads = g.cfg.get("attn_heads", list(range(8)))
    scale = 128 ** -0.5
    with ExitStack() as st:
        def sb(name, shape, dt=F32):
            return st.enter_context(nc.sbuf_tensor("t_" + name, shape, dt))
        NKMAX = (NS_ALL + PAST) // 128
        K2 = [sb("K2_%d" % i, [128, 2, NKMAX * 128], BF16) for i in range(2)]
        Vt = [sb("Vt_%d" % i, [128, NKMAX, 257], BF16) for i in range(2)]
        Q2 = [sb("Q2_%d" % i, [128, 2, 512], BF16) for i in range(2)]
        PT = [sb("PT_%d" % i, [128, 512], BF16) for i in range(3)]
        osb = sb("osb", [128, 4, 256])
        ob = sb("ob", [128, 256], BF16)
        mx = [sb("mx%d" % i, [128, 2, 512], BF16) for i in range(2)]
        lamt = sb("lamt", [128, 512])
        sm = sb("sm", [128, 8])
        rr = sb("rr", [128, 4])
        gsub = sb("gsub", [128, 256])
        junk = sb("junk", [128, 256])
        b_K2, b_Vt, b_Q2 = [Buf("t_K0"), Buf("t_K1")], [Buf("t_V0"), Buf("t_V1")], [Buf("t_Q0"), Buf("t_Q1")]
        b_PT = [Buf("t_PT%d" % i) for i in range(3)]
        b_osb, b_ob, b_mx = Buf("t_osb"), Buf("t_ob"), [Buf("t_mx0"), Buf("t_mx1")]
        b_c, b_rr, b_junk = Buf("t_c"), Buf("t_rr"), Buf("t_junk")
        sc.dma("sp", lamt[:], I.lam[0, :].partition_broadcast(128), b_c, writes=[b_c])
        sc.dma("sp", gsub[:], I.g_subln[0, :].partition_broadcast(128), b_c, writes=[b_c])
        for i in range(2):
            sc.op("dve", lambda e, i=i: e.tensor_tensor(out=lamt[:, i * 256:i * 256 + 128], in0=lamt[:, i * 256:i * 256 + 128],
                                                        in1=lamt[:, i * 256 + 128:i * 256 + 256], op=ALU.mult), reads=[b_c], writes=[b_c])
            sc.op("act", lambda e, i=i: e.activation(out=junk[:, 0:128], in_=lamt[:, i * 256:i * 256 + 128], func=AF.Identity,
                                                     accum_out=sm[:, i:i + 1]), reads=[b_c], writes=[b_c, b_junk])
        sc.op("act", lambda e: e.activation(out=sm[:, 2:4], in_=sm[:, 0:2], func=AF.Exp), reads=[b_c], writes=[b_c])
        sc.op("dve", lambda e: e.tensor_tensor(out=sm[:, 4:5], in0=sm[:, 3:4], in1=sm[:, 2:3], op=ALU.subtract), reads=[b_c], writes=[b_c])
        sc.op("dve", lambda e: e.tensor_scalar(out=sm[:, 4:5], in0=sm[:, 4:5], scalar1=-LAM_INIT, scalar2=None, op0=ALU.add),
              reads=[b_c], writes=[b_c])
        sc.op("dve", lambda e: e.tensor_scalar(out=gsub[:], in0=gsub[:], scalar1=1.0 - LAM_INIT, scalar2=None, op0=ALU.mult),
              reads=[b_c], writes=[b_c])
        for i in range(2):
            sc.op("pool", lambda e, i=i: e.memset(Vt[i][:, :, 256:257], 1.0), writes=[b_Vt[i]])
        cnt = dict(kv=0, q=0, pt=0, st=0, mx=0)
        for seq in seqs:
            if seq < 4:
                k0, nk, q0, nq = seq * 256, 256, seq * 256, 256
            else:
                k0, nk, q0, nq = NP_TOK, NS_ALL + PAST, NP_TOK, NS_OWN
            nkc = nk // 128
            for h in heads:
                ki = cnt["kv"] % 2
                cnt["kv"] += 1
                for j in range(2):
                    sc.dma("sp", K2[ki][:, j, 0:nk], S.KT[h * 2 + j, :, k0:k0 + nk], b_K2[ki], reads=[g.dbuf["KT"]], writes=[b_K2[ki]])
                sc.dma("sp", Vt[ki][:, 0:nkc, 0:256], S.V[k0:k0 + nk, h * 256:(h + 1) * 256].rearrange("(c p) v -> p c v", p=128),
                       b_Vt[ki], reads=[g.dbuf["V"]], writes=[b_Vt[ki]])
                for qt0 in range(0, nq, 512):
                    nqt = min(512, nq - qt0)
                    nqb = nqt // 128
                    qi = cnt["q"] % 2
                    cnt["q"] += 1
                    for j in range(2):
                        sc.dma("sp", Q2[qi][:, j, 0:nqt], S.QT[h * 2 + j, :, q0 + qt0:q0 + qt0 + nqt], b_Q2[qi],
                               reads=[g.dbuf["QT"]], writes=[b_Q2[qi]])
                    items = [(j, kc) for j in range(2) for kc in range(nkc)]
                    slots = {}

                    def issue_st(idx):
                        j, kc = items[idx]
                        pb = 4 + cnt["st"] % 3
                        cnt["st"] += 1
                        pi = cnt["pt"] % 3
                        cnt["pt"] += 1
                        slots[idx] = pi
                        sc.op("pe", lambda e, kc=kc, j=j, pb=pb: e.matmul(
                            g.ps[pb][:, 0:nqt], lhsT=K2[ki][:, j, kc * 128:(kc + 1) * 128], rhs=Q2[qi][:, j, 0:nqt],
                            start=True, stop=True), reads=[b_K2[ki], b_Q2[qi]], writes=[g.psb[pb]])
                        sc.op("act", lambda e, pb=pb, pi=pi: e.activation(out=PT[pi][:, 0:nqt], in_=g.ps[pb][:, 0:nqt],
                                                                          func=AF.Exp, scale=scale),
                              reads=[g.psb[pb]], writes=[b_PT[pi]])
                    PRE = 2
                    for idx in range(min(PRE, len(items))):
                        issue_st(idx)
                    for idx, (j, kc) in enumerate(items):
                        if idx + PRE < len(items):
                            issue_st(idx + PRE)
                        pi = slots.pop(idx)

                        def av(e, kc=kc, pi=pi):
                            ins = None
                            for qb in range(nqb):
                                ins = e.matmul(g.ps[qb][:, 0:257], lhsT=PT[pi][:, qb * 128:(qb + 1) * 128], rhs=Vt[ki][:, kc, :],
                                               start=(kc == 0), stop=(kc == nkc - 1))
                            return ins
                        sc.op("pe", av, reads=[b_PT[pi], b_Vt[ki]], writes=[g.psb[qb] for qb in range(nqb)])
                        if kc != nkc - 1:
                            continue
                        for qb in range(nqb):
                            A = g.ps[qb]
                            sc.op("dve", lambda e, A=A, j=j: e.reciprocal(out=rr[:, j:j + 1], in_=A[:, 256:257]),
                                  reads=[g.psb[qb]], writes=[b_rr])
                            if j == 0:
                                sc.op("dve", lambda e, A=A, qb=qb: e.tensor_scalar(out=osb[:, qb, :], in0=A[:, 0:256], scalar1=rr[:, 0:1],
                                                                                   scalar2=None, op0=ALU.mult),
                                      reads=[g.psb[qb], b_rr], writes=[b_osb])
                            else:
                                sc.op("dve", lambda e: e.tensor_tensor(out=rr[:, 2:3], in0=rr[:, 1:2], in1=sm[:, 4:5], op=ALU.mult),
                                      reads=[b_rr, b_c], writes=[b_rr])
                                sc.op("dve", lambda e, A=A, qb=qb: e.scalar_tensor_tensor(
                                    out=osb[:, qb, :], in0=A[:, 0:256], scalar=rr[:, 2:3], in1=osb[:, qb, :], op0=ALU.mult, op1=ALU.add),
                                    reads=[g.psb[qb], b_rr, b_osb], writes=[b_osb])
                    mi = cnt["mx"] % 2
                    cnt["mx"] += 1
                    for qb in range(nqb):
                        sc.op("act", lambda e, qb=qb: e.activation(out=junk[:], in_=osb[:, qb, :], func=AF.Square, accum_out=rr[:, 3:4]),
                              reads=[b_osb], writes=[b_rr, b_junk])
                        sc.op("act", lambda e: e.activation(out=rr[:, 3:4], in_=rr[:, 3:4], func=AF.Sqrt, scale=1.0 / 256, bias=EPS),
                              reads=[b_rr], writes=[b_rr])
                        sc.op("dve", lambda e: e.reciprocal(out=rr[:, 3:4], in_=rr[:, 3:4]), reads=[b_rr], writes=[b_rr])
                        sc.op("dve", lambda e, qb=qb: e.scalar_tensor_tensor(out=ob[:], in0=osb[:, qb, :], scalar=rr[:, 3:4], in1=gsub[:],
                                                                             op0=ALU.mult, op1=ALU.mult),
                              reads=[b_osb, b_rr, b_c], writes=[b_ob])
                        TP = g.ps[7][:, :].bitcast(BF16)

                        def tr(e, TP=TP):
                            e.transpose(TP[:, 0:128], ob[:, 0:128], g.ident_b[:])
                            return e.transpose(TP[:, 128:256], ob[:, 128:256], g.ident_b[:])
                        sc.op("pe", tr, reads=[b_ob, g.b_ident], writes=[g.psb[7]])
                        sc.op("act", lambda e, TP=TP, qb=qb, mi=mi: e.copy(out=mx[mi][:, :, qb * 128:(qb + 1) * 128],
                                                                          in_=TP[:, 0:256].rearrange("p (k t) -> p k t", k=2)),
                              reads=[g.psb[7]], writes=[b_mx[mi]])
                    sc.dma("sp", S.MIXT[h * 2:h * 2 + 2, :, q0 + qt0:q0 + qt0 + nqt].rearrange("k d t -> d k t"), mx[mi][:, :, 0:nqt],
                           b_mx[mi], reads=[b_mx[mi]], writes=[g.dbuf["MIXT"]])


def stage_mlp(g):
    nc, sc, I, S, O = g.nc, g.sc, g.I, g.S, g.O
    groups = g.cfg.get("mlp_groups", list(range(NOWN // 512)))
    nfb = g.cfg.get("mlp_fblocks", 16)
    if "MIXO" not in g.dbuf:
        g.dbuf["MIXO"] = Buf("d_MIXO", acc=True)
    MIXO = nc.dram_tensor("MIXO", [NOWN, D], F32, kind="ExternalOutput" if "MIXO" in g.debug else "Internal").ap()

    def load_rowmod(st, name, mode, idx, which):
        t = st.enter_context(nc.sbuf_tensor(name, [128, D], F32))
        b = Buf(name)
        return t, b

    def fill_rowmod(t, b, tmp, b_tmp, mode, idx, which):
        sc.dma("sp", t[:], S.modraw[mode, idx * D:(idx + 1) * D].partition_broadcast(128), b, reads=[g.dbuf["modraw"]], writes=[b])
        sc.dma("sp", tmp[:], I.grows[which, :].partition_broadcast(128), b_tmp, writes=[b_tmp])
        sc.op("dve", lambda e: e.tensor_tensor(out=t[:], in0=t[:], in1=tmp[:], op=ALU.mult), reads=[b, b_tmp], writes=[b])

    with ExitStack() as st0:
        h2T = st0.enter_context(nc.sbuf_tensor("m_h2T", [128, NKC, 512], BF16))
        b_h2T = Buf("m_h2T")
        ss = st0.enter_context(nc.sbuf_tensor("m_ss", [128, 1], F32))
        rstd = st0.enter_context(nc.sbuf_tensor("m_rstd", [128, 1], F32))
        b_scr = Buf("m_scr")
        modc = {}
        modc[0] = load_modcols(g, st0, 0, 2, 4, 3, "m_modp")
        modc[1] = load_modcols(g, st0, 1, 2, 4, 3, "m_mods")
        for grp in groups:
            t0g = grp * 512
            mode = 0 if t0g < NP_TOK else 1
            with ExitStack() as st:
                mixT = st.enter_context(nc.sbuf_tensor("m_mixT_%d" % grp, [128, NKC, 512], BF16))
                W = [st.enter_context(nc.sbuf_tensor("m_Wo%d_%d" % (i, grp), [128, NKC, 512], BF16)) for i in range(2)]
                stg = [st.enter_context(nc.sbuf_tensor("m_stg%d_%d" % (i, grp), [128, 512], F32)) for i in range(4)]
                GA = st.enter_context(nc.sbuf_tensor("m_GA_%d" % grp, [128, D], F32))
                mixt = st.enter_context(nc.sbuf_tensor("m_mix_%d" % grp, [128, D], F32))
                xt = st.enter_context(nc.sbuf_tensor("m_x_%d" % grp, [128, D], F32))
                junk = st.enter_context(nc.sbuf_tensor("m_junk_%d" % grp, [128, D], BF16))
                b_mixT, b_W, b_stg = Buf("m_mixT"), [Buf("m_Wo0"), Buf("m_Wo1")], [Buf("m_stg%d" % i) for i in range(4)]
                b_GA, b_mix, b_x, b_junk = Buf("m_GA"), Buf("m_mix"), Buf("m_x"), Buf("m_junk")
                fill_rowmod(GA, b_GA, mixt, b_mix, mode, 2, 1)
                sc.dma("sp", mixT[:], S.MIXT[:, :, t0g:t0g + 512].rearrange("k d t -> d k t"), b_mixT,
                       reads=[g.dbuf["MIXT"]], writes=[b_mixT])
                k = 0
                for c in range(8):
                    wi = c % 2
                    sc.dma("pool", W[wi][:], S.WOB[c].rearrange("p (kc n) -> p kc n", kc=NKC), b_W[wi], reads=[g.dbuf["WB"]], writes=[b_W[wi]])
                    for tt in range(4):
                        pb = k % 4
                        si = k % 4
                        k += 1

                        def mm(e, tt=tt, wi=wi, pb=pb):
                            ins = None
                            for kc in range(NKC):
                                ins = e.matmul(g.ps[pb][:, :], lhsT=mixT[:, kc, tt * 128:(tt + 1) * 128], rhs=W[wi][:, kc, :],
                                               start=(kc == 0), stop=(kc == NKC - 1))
                            return ins
                        sc.op("pe", mm, reads=[b_mixT, b_W[wi]], writes=[g.psb[pb]])
                        if k % 2 == 0:
                            sc.op("act", lambda e, si=si, pb=pb: e.copy(out=stg[si][:], in_=g.ps[pb][:, :]), reads=[g.psb[pb]], writes=[b_stg[si]])
                        else:
                            sc.op("dve", lambda e, si=si, pb=pb: e.tensor_copy(out=stg[si][:], in_=g.ps[pb][:, :]), reads=[g.psb[pb]], writes=[b_stg[si]])
                        sc.dma("sp", MIXO[t0g + tt * 128:t0g + (tt + 1) * 128, c * 512:(c + 1) * 512], stg[si][:], b_stg[si],
                               reads=[b_stg[si]], writes=[g.dbuf["MIXO"]])
                for tt in range(4):
                    t0 = t0g + tt * 128
                    sc.dma("sp", mixt[:], MIXO[t0:t0 + 128, :], b_mix, reads=[g.dbuf["MIXO"]], writes=[b_mix])
                    sc.dma("sp", xt[:], tok_src(g, t0), b_x, writes=[b_x])
                    sc.op("act", lambda e: e.activation(out=junk[:], in_=mixt[:], func=AF.Square, accum_out=ss[:]),
                          reads=[b_mix], writes=[b_scr, b_junk])
                    sc.op("act", lambda e: e.activation(out=rstd[:], in_=ss[:], func=AF.Sqrt, scale=1.0 / D, bias=EPS),
                          reads=[b_scr], writes=[b_scr])
                    sc.op("dve", lambda e: e.reciprocal(out=rstd[:], in_=rstd[:]), reads=[b_scr], writes=[b_scr])
                    sc.op("dve", lambda e: e.scalar_tensor_tensor(out=mixt[:], in0=mixt[:], scalar=rstd[:, 0:1], in1=GA[:],
                                                                  op0=ALU.mult, op1=ALU.mult), reads=[b_mix, b_scr, b_GA], writes=[b_mix])
                    sc.op("pool", lambda e: e.tensor_tensor(out=xt[:], in0=xt[:], in1=mixt[:], op=ALU.add), reads=[b_mix, b_x], writes=[b_x])
                    sc.dma("sp", S.X1[t0:t0 + 128, :], xt[:], b_x, reads=[b_x], writes=[g.dbuf["X1"]])
                    norm_transpose(g, xt, b_x, modc[mode][0], modc[mode][1], h2T, b_h2T, tt, (ss, rstd), b_scr, junk, b_junk)
            sc.barrier()
            with ExitStack() as stB:
                macc = stB.enter_context(nc.sbuf_tensor("m_acc_%d" % grp, [128, 4, D], F32))
                b_macc = [[Buf("m_acc%d_%d" % (a, c)) for c in range(8)] for a in range(4)]
                with ExitStack() as st:
                    Wu = [st.enter_context(nc.sbuf_tensor("m_Wu%d_%d" % (i, grp), [128, NKC, 512], BF16)) for i in range(2)]
                    Wd = [st.enter_context(nc.sbuf_tensor("m_Wd%d_%d" % (i, grp), [128, 8, 512], BF16)) for i in range(2)]
                    uT = [st.enter_context(nc.sbuf_tensor("m_uT%d_%d" % (i, grp), [128, 8, 512], BF16)) for i in range(2)]
                    r32 = [st.enter_context(nc.sbuf_tensor("m_r%d_%d" % (i, grp), [128, 512], F32)) for i in range(2)]
                    b_Wu, b_Wd, b_uT = [Buf("m_Wu%d" % i) for i in range(2)], [Buf("m_Wd0"), Buf("m_Wd1")], [Buf("m_uT0"), Buf("m_uT1")]
                    b_r = [Buf("m_r0"), Buf("m_r1")]
                    cnt = dict(wu=0, wd=0, ps=0, r=0)
                    def load_wu(fb_, hf_):
                        sc.dma("pool", Wu[hf_][:], S.WUB[fb_ * 2 + hf_].rearrange("p (kc n) -> p kc n", kc=NKC), b_Wu[hf_],
                               reads=[g.dbuf["WB"]], writes=[b_Wu[hf_]])

                    def load_wd(fb_, c_):
                        sc.dma("pool", Wd[c_ % 2][:], S.WDB[fb_, c_].rearrange("p (ch n) -> p ch n", ch=8), b_Wd[c_ % 2],
                               reads=[g.dbuf["WB"]], writes=[b_Wd[c_ % 2]])
                    load_wu(0, 0)
                    load_wu(0, 1)
                    load_wd(0, 0)
                    load_wd(0, 1)
                    for fb in range(nfb):
                        ui = fb % 2
                        for hf in range(2):
                            wi = hf
                            for c4 in range(4):
                                ch = hf * 4 + c4
                                pb = 4 + cnt["ps"] % 2

                                def mm(e, wi=wi, pb=pb, c4=c4):
                                    ins = None
                                    for kc in range(NKC):
                                        ins = e.matmul(g.ps[pb][:, :], lhsT=Wu[wi][:, kc, c4 * 128:(c4 + 1) * 128], rhs=h2T[:, kc, :],
                                                       start=(kc == 0), stop=(kc == NKC - 1))
                                    return ins
                                sc.op("pe", mm, reads=[b_Wu[wi], b_h2T], writes=[g.psb[pb]])
                                ri = cnt["r"] % 2
                                cnt["r"] += 1
                                cnt["ps"] += 1
                                sc.op("act", lambda e, ri=ri, pb=pb: e.activation(out=r32[ri][:], in_=g.ps[pb][:, :], func=AF.Relu),
                                      reads=[g.psb[pb]], writes=[b_r[ri]])
                                sc.op("pool", lambda e, ri=ri, ui=ui, ch=ch: e.tensor_tensor(out=uT[ui][:, ch, :], in0=r32[ri][:], in1=r32[ri][:], op=ALU.mult),
                                      reads=[b_r[ri]], writes=[b_uT[ui]])
                        if fb + 1 < nfb:
                            load_wu(fb + 1, 0)
                            load_wu(fb + 1, 1)
                        for c in range(8):
                            di = c % 2
                            for tt in range(4):
                                pb = cnt["ps"] % 4
                                cnt["ps"] += 1

                                def mm2(e, tt=tt, di=di, pb=pb, ui=ui):
                                    ins = None
                                    for ch in range(8):
                                        ins = e.matmul(g.ps[pb][:, :], lhsT=uT[ui][:, ch, tt * 128:(tt + 1) * 128], rhs=Wd[di][:, ch, :],
                                                       start=(ch == 0), stop=(ch == 7))
                                    return ins
                                sc.op("pe", mm2, reads=[b_uT[ui], b_Wd[di]], writes=[g.psb[pb]])
                                dst = macc[:, tt, c * 512:(c + 1) * 512]
                                if fb == 0:
                                    sc.op("act", lambda e, dst=dst, pb=pb: e.copy(out=dst, in_=g.ps[pb][:, :]), reads=[g.psb[pb]], writes=[b_macc[tt][c]])
                                else:
                                    sc.op("dve", lambda e, dst=dst, pb=pb: e.tensor_tensor(out=dst, in0=dst, in1=g.ps[pb][:, :], op=ALU.add),
                                          reads=[g.psb[pb], b_macc[tt][c]], writes=[b_macc[tt][c]])
                            if c + 2 < 8:
                                load_wd(fb, c + 2)
                            elif fb + 1 < nfb:
                                load_wd(fb + 1, c + 2 - 8)
                sc.barrier()
                with ExitStack() as st:
                    GM = st.enter_context(nc.sbuf_tensor("m_GM_%d" % grp, [128, D], F32))
                    xt = st.enter_context(nc.sbuf_tensor("m_x1_%d" % grp, [128, D], F32))
                    junk = st.enter_context(nc.sbuf_tensor("m_junk2_%d" % grp, [128, D], BF16))
                    b_GM, b_x, b_junk = Buf("m_GM"), Buf("m_x1"), Buf("m_junk2")
                    fill_rowmod(GM, b_GM, xt, b_x, mode, 5, 3)
                    for tt in range(4):
                        t0 = t0g + tt * 128
                        mt = macc[:, tt, :]
                        sc.dma("sp", xt[:], S.X1[t0:t0 + 128, :], b_x, reads=[g.dbuf["X1"]], writes=[b_x])
                        sc.op("act", lambda e, mt=mt: e.activation(out=junk[:], in_=mt, func=AF.Square, accum_out=ss[:]),
                              reads=b_macc[tt], writes=[b_scr, b_junk])
                        sc.op("act", lambda e: e.activation(out=rstd[:], in_=ss[:], func=AF.Sqrt, scale=1.0 / D, bias=EPS),
                              reads=[b_scr], writes=[b_scr])
                        sc.op("dve", lambda e: e.reciprocal(out=rstd[:], in_=rstd[:]), reads=[b_scr], writes=[b_scr])
                        sc.op("dve", lambda e, mt=mt: e.scalar_tensor_tensor(out=mt, in0=mt, scalar=rstd[:, 0:1], in1=GM[:],
                                                                             op0=ALU.mult, op1=ALU.mult),
                              reads=b_macc[tt] + [b_scr, b_GM], writes=b_macc[tt])
                        sc.op("pool", lambda e, mt=mt: e.tensor_tensor(out=xt[:], in0=xt[:], in1=mt, op=ALU.add), reads=b_macc[tt] + [b_x], writes=[b_x])
                        dst = O.yp[t0:t0 + 128, :] if t0 < NP_TOK else O.ys[t0 - NP_TOK:t0 - NP_TOK + 128, :]
                        sc.dma("sp", dst, xt[:], b_x, reads=[b_x], writes=[g.dbuf["out"]])
                sc.barrier()
```
